# Optimizing a Trainium2 kernel written in Bass

```python
import math
import jax, jax.numpy as jnp
from jax import lax
import numpy as np

D_MODEL = 1024
BATCH = 32
SEQ = 256
DEPTH = 2
DEC_BATCH = 4
DEC_SEQ = 2048
PAST_LEN = 512

GRID_W = 64
MIX_W = D_MODEL
DIFF_W = MIX_W // 4
DIFF_HEADS = 4
DIFF_V = DIFF_W // DIFF_HEADS
DIFF_QK = DIFF_V // 2
GQA_W = MIX_W // 4
GQA_HD = 64
GQA_Q_HEADS = GQA_W // GQA_HD
GQA_KV_HEADS = 2
SSD_INNER = MIX_W - DIFF_W - GQA_W
SSD_HD = 64
SSD_HEADS = SSD_INNER // SSD_HD
SSD_GROUPS = 2
SSD_STATE = 64
SSD_CONV = 5
SSD_CHUNK = 128
XBC_W = SSD_INNER + 2 * SSD_GROUPS * SSD_STATE
Q_BLOCK = 128
D_FF = -(-8 * D_MODEL // (3 * 256)) * 256
ROPE_THETA = 10000.0
EPS = 1e-5
ALPHA = (2 * DEPTH) ** 0.25
BETA = (8 * DEPTH) ** -0.25
PROJ_SIZES = (DIFF_HEADS * 2 * DIFF_QK, DIFF_HEADS * 2 * DIFF_QK, DIFF_W,
              GQA_W, GQA_KV_HEADS * GQA_HD, GQA_KV_HEADS * GQA_HD,
              SSD_INNER, XBC_W, 2 * SSD_HEADS)
IN_W = sum(PROJ_SIZES)

kernel_name = 'hybrid_diffusion_prefix_trunk_step'


def layer_norm(x, g, b):
    xf = x.astype(jnp.float32)
    mu = jnp.mean(xf, -1, keepdims=True)
    var = jnp.mean(jnp.square(xf - mu), -1, keepdims=True)
    return ((xf - mu) * lax.rsqrt(var + EPS) * g + b).astype(x.dtype)


def rms_norm(x, g):
    xf = x.astype(jnp.float32)
    return (xf * lax.rsqrt(jnp.mean(xf * xf, -1, keepdims=True) + EPS) * g).astype(x.dtype)


def axial_rope(rows, dim):
    row = jnp.repeat(jnp.arange(rows), GRID_W).astype(jnp.float32)
    col = jnp.tile(jnp.arange(GRID_W), rows).astype(jnp.float32)
    n_freq = dim // 4
    inv = ROPE_THETA ** (-jnp.arange(n_freq, dtype=jnp.float32) / n_freq)
    ang = jnp.concatenate([row[:, None] * inv, col[:, None] * inv], -1)
    return (jnp.cos(ang), jnp.sin(ang))


def apply_rope(x, cos, sin):
    shape = (1, x.shape[1]) + (1,) * (x.ndim - 3) + (x.shape[-1] // 2,)
    c, s = cos.reshape(shape), sin.reshape(shape)
    xf = x.astype(jnp.float32)
    x1, x2 = xf[..., 0::2], xf[..., 1::2]
    return jnp.stack([x1 * c - x2 * s, x1 * s + x2 * c], -1).reshape(x.shape).astype(x.dtype)


def sweep_query_blocks(fn, q):
    b, t = q.shape[:2]
    nb = t // Q_BLOCK
    qb = jnp.moveaxis(q.reshape((b, nb, Q_BLOCK) + q.shape[2:]), 1, 0)
    ob = lax.map(fn, qb)
    return jnp.moveaxis(ob, 0, 1).reshape((b, t) + ob.shape[3:])


def diff_attention(q, k, v, lam, subln_g, lam_init):
    scale = DIFF_QK ** -0.5
    def block(qb):
        s = jnp.einsum('bqhmd,bkhmd->bhmqk', qb, k).astype(jnp.float32) * scale
        p = jax.nn.softmax(s, axis=-1)
        w = p[:, :, 0] - lam * p[:, :, 1]
        return jnp.einsum('bhqk,bkhe->bqhe', w.astype(v.dtype), v)
    o = sweep_query_blocks(block, q)
    o = rms_norm(o, subln_g) * (1.0 - lam_init)
    return o.reshape(o.shape[:2] + (DIFF_W,))


def gqa_attention(q, k, v):
    b, t = q.shape[:2]
    scale = GQA_HD ** -0.5
    qg = q.reshape(b, t, GQA_KV_HEADS, GQA_Q_HEADS // GQA_KV_HEADS, GQA_HD)
    def block(qb):
        s = jnp.einsum('bqkgd,bskd->bkgqs', qb, k).astype(jnp.float32) * scale
        p = jax.nn.softmax(s, axis=-1).astype(v.dtype)
        return jnp.einsum('bkgqs,bskd->bqkgd', p, v)
    o = sweep_query_blocks(block, qg)
    return o.reshape(b, t, GQA_W)


def depthwise_conv(x, w, bias):
    y = lax.conv_general_dilated(x, w[:, None, :].astype(x.dtype), window_strides=(1,),
                                 padding=[(SSD_CONV // 2, SSD_CONV // 2)],
                                 dimension_numbers=('NWC', 'WIO', 'NWC'),
                                 feature_group_count=x.shape[-1])
    return y + bias


def ssd_scan(x, dt, a, bm, cm, init):
    f32 = jnp.float32
    b, t, h, p = x.shape
    n = bm.shape[-1]
    nc, L = t // SSD_CHUNK, SSD_CHUNK
    xd = (x.astype(f32) * dt[..., None]).reshape(b, nc, L, h, p)
    bc = bm.astype(f32).reshape(b, nc, L, h, n)
    cc = cm.astype(f32).reshape(b, nc, L, h, n)
    a_cs = jnp.cumsum((dt * a).reshape(b, nc, L, h).transpose(0, 3, 1, 2), axis=-1)
    lower = jnp.tril(jnp.ones((L, L), bool))
    decay = jnp.exp(jnp.where(lower, a_cs[..., :, None] - a_cs[..., None, :], -jnp.inf))
    scores = jnp.einsum('bclhn,bcshn->bhcls', cc, bc) * decay
    y_diag = jnp.einsum('bhcls,bcshp->bclhp', scores, xd)
    to_end = jnp.exp(a_cs[..., -1:] - a_cs)
    chunk_states = jnp.einsum('bclhn,bhcl,bclhp->bchpn', bc, to_end, xd)
    chunk_decay = jnp.exp(a_cs[..., -1])
    def step(s, inp):
        st, dec = inp
        return s * dec[:, :, None, None] + st, s
    final, s_in = lax.scan(step, init.astype(f32),
                           (jnp.moveaxis(chunk_states, 1, 0), jnp.moveaxis(chunk_decay, 2, 0)))
    s_in = jnp.moveaxis(s_in, 0, 1)
    y_off = jnp.einsum('bclhn,bchpn->bclhp', cc, s_in) * jnp.exp(a_cs).transpose(0, 2, 3, 1)[..., None]
    return (y_diag + y_off).reshape(b, t, h, p), final


def ssd_mixer(z, xbc, dt_raw, lp, init_f, init_b):
    b, t, _ = z.shape
    xbc = jax.nn.silu(depthwise_conv(xbc, lp['conv_w'], lp['conv_b']))
    xh, bm, cm = jnp.split(xbc, [SSD_INNER, SSD_INNER + SSD_GROUPS * SSD_STATE], axis=-1)
    rep = SSD_HEADS // SSD_GROUPS
    xh = xh.reshape(b, t, SSD_HEADS, SSD_HD)
    bm = jnp.repeat(bm.reshape(b, t, SSD_GROUPS, SSD_STATE), rep, axis=2)
    cm = jnp.repeat(cm.reshape(b, t, SSD_GROUPS, SSD_STATE), rep, axis=2)
    dt = jax.nn.softplus((dt_raw + lp['dt_bias']).astype(jnp.float32))
    a = -jnp.exp(lp['A_log'].astype(jnp.float32))
    flip = lambda u: jnp.flip(u, 1)
    y_f, s_f = ssd_scan(xh, dt[:, :, 0], a[0], bm, cm, init_f)
    y_b, s_b = ssd_scan(flip(xh), flip(dt[:, :, 1]), a[1], flip(bm), flip(cm), init_b)
    y = y_f + flip(y_b) + xh.astype(jnp.float32) * lp['D'].astype(jnp.float32)[:, None]
    y = y.reshape(b, t, SSD_INNER) * jax.nn.silu(z.astype(jnp.float32))
    return rms_norm(y, lp['ssd_norm_g']).astype(z.dtype), s_f, s_b


def mixing_sublayer(h, lp, lam_init, ctx=None, rope=None):
    b, t, _ = h.shape
    proj = jnp.einsum('btd,de->bte', h, lp['w_in'])
    cuts = np.cumsum(PROJ_SIZES)[:-1].tolist()
    dq, dk, dv, gq, gk, gv, z, xbc, dt_raw = jnp.split(proj, cuts, axis=-1)
    dq = dq.reshape(b, t, DIFF_HEADS, 2, DIFF_QK)
    dk = dk.reshape(b, t, DIFF_HEADS, 2, DIFF_QK)
    dv = dv.reshape(b, t, DIFF_HEADS, DIFF_V)
    gq = rms_norm(gq.reshape(b, t, GQA_Q_HEADS, GQA_HD), lp['qk_g'][0])
    gk = rms_norm(gk.reshape(b, t, GQA_KV_HEADS, GQA_HD), lp['qk_g'][1])
    gv = gv.reshape(b, t, GQA_KV_HEADS, GQA_HD)
    dt_raw = dt_raw.reshape(b, t, 2, SSD_HEADS)
    lq1, lk1, lq2, lk2 = lp['lam'][0], lp['lam'][1], lp['lam'][2], lp['lam'][3]
    lam = (jnp.exp(jnp.sum(lq1 * lk1).astype(jnp.float32))
           - jnp.exp(jnp.sum(lq2 * lk2).astype(jnp.float32)) + lam_init)
    if ctx is None:
        kd, vd, kg, vg = dk, dv, gk, gv
        init_f = jnp.zeros((b, SSD_HEADS, SSD_HD, SSD_STATE), jnp.float32)
        init_b = init_f
    else:
        c_dk, c_dv, c_gk, c_gv, init_f, init_b = ctx
        cos_d, sin_d, cos_g, sin_g = rope
        dq = apply_rope(dq, cos_d, sin_d)
        kd = jnp.concatenate([apply_rope(dk, cos_d, sin_d),
                              c_dk.reshape(b, -1, DIFF_HEADS, 2, DIFF_QK).astype(dk.dtype)], axis=1)
        vd = jnp.concatenate([dv, c_dv.astype(dv.dtype)], axis=1)
        gq = apply_rope(gq, cos_g, sin_g)
        kg = jnp.concatenate([apply_rope(gk, cos_g, sin_g), c_gk.astype(gk.dtype)], axis=1)
        vg = jnp.concatenate([gv, c_gv.astype(gv.dtype)], axis=1)
    o_diff = diff_attention(dq, kd, vd, lam, lp['subln_g'], lam_init)
    o_gqa = gqa_attention(gq, kg, vg)
    o_ssd, s_f, s_b = ssd_mixer(z, xbc, dt_raw, lp, init_f, init_b)
    out = jnp.einsum('bte,ed->btd', jnp.concatenate([o_diff, o_gqa, o_ssd], -1), lp['w_out'])
    if ctx is None:
        cache = (dk.reshape(b, t, DIFF_HEADS, 2 * DIFF_QK), dv, gk, gv,
                 s_f.astype(h.dtype), s_b.astype(h.dtype))
    else:
        cache = None
    return out, cache


def swiglu(h, w_in, w_out):
    g, u = jnp.split(jnp.einsum('btd,df->btf', h, w_in), 2, axis=-1)
    return jnp.einsum('btf,fd->btd', jax.nn.silu(g) * u, w_out)


def trunk_layer(x, mod, lp, lam_init, ctx=None, rope=None):
    shift1, scale1, gate1, shift2, scale2, gate2 = jnp.split(mod[:, None, :], 6, axis=-1)
    o, cache = mixing_sublayer(x * (1 + scale1) + shift1, lp, lam_init, ctx, rope)
    x = layer_norm(ALPHA * x + gate1 * o, lp['ln_g'][0], lp['ln_b'][0])
    f = swiglu(x * (1 + scale2) + shift2, lp['w_ffn_in'], lp['w_ffn_out'])
    x = layer_norm(ALPHA * x + gate2 * f, lp['ln_g'][1], lp['ln_b'][1])
    return x, cache


def setup_inputs(seed: int = 0) -> dict:
    key = jax.random.key(seed)
    ks = jax.random.split(key, 27)
    f32 = jnp.float32
    def nrm(k, shape, s=1.0):
        return s * jax.random.normal(k, shape, f32)
    dt0 = jnp.exp(jax.random.uniform(ks[20], (DEPTH, 2, SSD_HEADS), f32, math.log(1e-3), math.log(1e-1)))
    return {
        'x_prompt': nrm(ks[0], (BATCH, SEQ, D_MODEL)),
        'x_sample': nrm(ks[1], (DEC_BATCH, DEC_SEQ, D_MODEL)),
        'cache_diff_k': nrm(ks[2], (DEC_BATCH, DEPTH, PAST_LEN, DIFF_HEADS, 2 * DIFF_QK)),
        'cache_diff_v': nrm(ks[3], (DEC_BATCH, DEPTH, PAST_LEN, DIFF_HEADS, DIFF_V)),
        'cache_gqa_k': nrm(ks[4], (DEC_BATCH, DEPTH, PAST_LEN, GQA_KV_HEADS, GQA_HD)),
        'cache_gqa_v': nrm(ks[5], (DEC_BATCH, DEPTH, PAST_LEN, GQA_KV_HEADS, GQA_HD)),
        'state_ssd_fwd': nrm(ks[6], (DEC_BATCH, DEPTH, SSD_HEADS, SSD_HD, SSD_STATE), 0.1),
        'state_ssd_bwd': nrm(ks[7], (DEC_BATCH, DEPTH, SSD_HEADS, SSD_HD, SSD_STATE), 0.1),
        'c': nrm(ks[8], (DEC_BATCH, D_MODEL)),
        'c_ctx': nrm(ks[9], (D_MODEL,)),
        'w_ada': nrm(ks[10], (DEPTH, D_MODEL, 6 * D_MODEL), 0.5 * D_MODEL ** -0.5),
        'b_ada': nrm(ks[11], (DEPTH, 6 * D_MODEL), 0.01),
        'w_in': nrm(ks[12], (DEPTH, D_MODEL, IN_W), D_MODEL ** -0.5),
        'w_out': nrm(ks[13], (DEPTH, MIX_W, D_MODEL), BETA * MIX_W ** -0.5),
        'diff_lambda': nrm(ks[14], (DEPTH, 4, DIFF_QK), 0.1),
        'diff_subln_g': 1.0 + nrm(ks[15], (DEPTH, DIFF_V), 0.02),
        'qk_norm_g': 1.0 + nrm(ks[16], (DEPTH, 2, GQA_HD), 0.02),
        'ssd_conv_w': nrm(ks[17], (DEPTH, SSD_CONV, XBC_W), SSD_CONV ** -0.5),
        'ssd_conv_b': nrm(ks[18], (DEPTH, XBC_W), 0.01),
        'ssd_A_log': jnp.log(jax.random.uniform(ks[19], (DEPTH, 2, SSD_HEADS), f32, 1.0, 16.0)),
        'ssd_dt_bias': dt0 + jnp.log(-jnp.expm1(-dt0)),
        'ssd_D': 1.0 + nrm(ks[21], (DEPTH, SSD_HEADS), 0.02),
        'ssd_norm_g': 1.0 + nrm(ks[22], (DEPTH, SSD_INNER), 0.02),
        'ln_g': 1.0 + nrm(ks[23], (DEPTH, 2, D_MODEL), 0.02),
        'ln_b': nrm(ks[24], (DEPTH, 2, D_MODEL), 0.01),
        'w_ffn_in': nrm(ks[25], (DEPTH, D_MODEL, 2 * D_FF), D_MODEL ** -0.5),
        'w_ffn_out': nrm(ks[26], (DEPTH, D_FF, D_MODEL), BETA * D_FF ** -0.5),
    }


def reference(x_prompt, x_sample, cache_diff_k, cache_diff_v, cache_gqa_k, cache_gqa_v,
              state_ssd_fwd, state_ssd_bwd, c, c_ctx, w_ada, b_ada, w_in, w_out, diff_lambda,
              diff_subln_g, qk_norm_g, ssd_conv_w, ssd_conv_b, ssd_A_log, ssd_dt_bias, ssd_D,
              ssd_norm_g, ln_g, ln_b, w_ffn_in, w_ffn_out):
    rows = x_sample.shape[1] // GRID_W
    rope = axial_rope(rows, DIFF_QK) + axial_rope(rows, GQA_HD)
    xp, xs = x_prompt, x_sample
    new_dk, new_dv, new_gk, new_gv, new_sf, new_sb = [], [], [], [], [], []
    for l in range(DEPTH):
        lam_init = 0.8 - 0.6 * math.exp(-0.3 * l)
        lp = dict(w_in=w_in[l], w_out=w_out[l], lam=diff_lambda[l], subln_g=diff_subln_g[l],
                  qk_g=qk_norm_g[l], conv_w=ssd_conv_w[l], conv_b=ssd_conv_b[l], A_log=ssd_A_log[l],
                  dt_bias=ssd_dt_bias[l], D=ssd_D[l], ssd_norm_g=ssd_norm_g[l], ln_g=ln_g[l],
                  ln_b=ln_b[l], w_ffn_in=w_ffn_in[l], w_ffn_out=w_ffn_out[l])
        mod_ctx = (jnp.dot(jax.nn.silu(c_ctx), w_ada[l]) + b_ada[l])[None]
        mod_lat = jnp.dot(jax.nn.silu(c), w_ada[l]) + b_ada[l]
        xp, cache = trunk_layer(xp, mod_ctx, lp, lam_init)
        new_dk.append(cache[0]); new_dv.append(cache[1]); new_gk.append(cache[2])
        new_gv.append(cache[3]); new_sf.append(cache[4]); new_sb.append(cache[5])
        ctx = (cache_diff_k[:, l], cache_diff_v[:, l], cache_gqa_k[:, l], cache_gqa_v[:, l],
               state_ssd_fwd[:, l], state_ssd_bwd[:, l])
        xs, _ = trunk_layer(xs, mod_lat, lp, lam_init, ctx=ctx, rope=rope)
    return (xp, xs, jnp.stack(new_dk, 1), jnp.stack(new_dv, 1), jnp.stack(new_gk, 1),
            jnp.stack(new_gv, 1), jnp.stack(new_sf, 1), jnp.stack(new_sb, 1))
```

```python
import numpy as np
import concourse.bass as bass
import concourse.mybir as mybir

F32 = mybir.dt.float32
F32R = mybir.dt.float32r
AF = mybir.ActivationFunctionType
ALU = mybir.AluOpType
AX = mybir.AxisListType

N_DMA_SEMS = 32
DEBUG_NAMES = None
BIN = 256
BIN_DRAM = 1 << 16
UNTRACKED = set()
SB_NAMES = set(['sb'] + ['ps%d' % i for i in range(8)])


def _prod(xs):
    r = 1
    for x in xs:
        r *= int(x)
    return r


def region(ap):
    t = ap.tensor
    name = t.name
    apl = [(int(s), int(c)) for s, c in ap.ap]
    off = int(ap.offset)
    sp = str(ap.space)
    if 'DRAM' in sp or 'HBM' in sp:
        ext = 1 + sum(abs(s) * (c - 1) for s, c in apl)
        return (name, 0, 1, off, off + ext)
    rowlen = _prod(list(t.shape)[1:])
    p0 = off // rowlen
    f0 = off % rowlen
    np_ = apl[0][1]
    ext = 1 + sum(abs(s) * (c - 1) for s, c in apl[1:])
    if 'PSUM' in sp:
        return (name, 0, 128, 0, rowlen)
    return (name, p0, p0 + np_, f0, f0 + ext)


class Op:
    __slots__ = ('eng', 'fn', 'waits', 'is_dma', 'chan', 'val', 'marked', 'clock', 'inc', 'desc')

    def __init__(self, eng, fn, is_dma):
        self.eng = eng
        self.fn = fn
        self.is_dma = is_dma
        self.waits = []
        self.marked = False
        self.chan = None
        self.val = 0
        self.clock = None


class Rec:
    __slots__ = ('p0', 'p1', 'f0', 'f1', 'w', 'op', 'dead')

    def __init__(self, r, w, op):
        self.p0, self.p1, self.f0, self.f1 = r[1], r[2], r[3], r[4]
        self.w = w
        self.op = op
        self.dead = False


class Sched:
    ENGS = ('pe', 'act', 'dve', 'pool', 'sp')

    def __init__(self, nc):
        self.nc = nc
        self.streams = {e: [] for e in self.ENGS}
        self.pos = {e: 0 for e in self.ENGS}
        self.clock = {e: {} for e in self.ENGS}
        self.bins = {}
        self.dma_rr = 0
        self.dma_rr2 = [0, 0]
        self.dma_last = [None] * N_DMA_SEMS
        self.dma_cnt = [0] * N_DMA_SEMS
        self.nops = 0

    def _overlaps(self, reg, want_reads):
        name, p0, p1, f0, f1 = reg
        out = []
        seen = set()
        bn = BIN if name in SB_NAMES else BIN_DRAM
        for b in range(f0 // bn, (f1 - 1) // bn + 1):
            lst = self.bins.get((name, b))
            if not lst:
                continue
            alive = []
            for rec in lst:
                if rec.dead:
                    continue
                alive.append(rec)
                if id(rec) in seen:
                    continue
                if rec.f0 < f1 and f0 < rec.f1 and rec.p0 < p1 and p0 < rec.p1:
                    if rec.w or want_reads:
                        seen.add(id(rec))
                        out.append(rec)
            if len(alive) != len(lst):
                self.bins[(name, b)] = alive
        return out

    def _add_rec(self, reg, w, op):
        name, p0, p1, f0, f1 = reg
        rec = Rec(reg, w, op)
        bn = BIN if name in SB_NAMES else BIN_DRAM
        for b in range(f0 // bn, (f1 - 1) // bn + 1):
            self.bins.setdefault((name, b), []).append(rec)

    def _need(self, op, dep):
        if dep is op:
            return
        e = op.eng
        ck = self.clock[e]
        if ck.get(dep.chan, 0) >= dep.val:
            return
        op.waits.append(dep)
        dep.marked = True
        for k, v in dep.clock.items():
            if ck.get(k, 0) < v:
                ck[k] = v
        if ck.get(dep.chan, 0) < dep.val:
            ck[dep.chan] = dep.val

    def add(self, eng, fn, reads=(), writes=(), is_dma=False, after=None):
        op = Op(eng, fn, is_dma)
        self.nops += 1
        if DEBUG_NAMES is not None:
            op.desc = (eng, [region(a) for a in reads], [region(a) for a in writes])
        rregs = [r_ for r_ in (region(a) for a in reads) if r_[0] not in UNTRACKED]
        wregs = [region(a) for a in writes]
        deps = []
        for r in rregs:
            for rec in self._overlaps(r, False):
                deps.append(rec.op)
        for w in wregs:
            for rec in self._overlaps(w, True):
                deps.append(rec.op)
        if is_dma:
            half = N_DMA_SEMS // 2
            qi = 0 if eng == 'sp' else 1
            k = qi * half + self.dma_rr2[qi]
            self.dma_rr2[qi] = (self.dma_rr2[qi] + 1) % half
            if self.dma_last[k] is not None:
                deps.append(self.dma_last[k])
            self.dma_cnt[k] += 1
            op.chan = 'dma%d' % k
            op.val = self.dma_cnt[k]
            self.dma_last[k] = op
            op.inc = k
        else:
            self.pos[eng] += 1
            op.chan = eng
            op.val = self.pos[eng]
            op.inc = None
        own = [d for d in deps if (not d.is_dma) and d.eng == eng]
        oth = [d for d in deps if d.is_dma or d.eng != eng]
        oth.sort(key=lambda d: -d.val)
        own.sort(key=lambda d: -d.val)
        for d in oth + own:
            if eng == 'pe' and d.eng == 'pe' and not d.is_dma and not is_dma:
                continue
            self._need(op, d)
        if after is not None:
            self._need(op, after)
        op.clock = dict(self.clock[eng])
        for w in wregs:
            name, p0, p1, f0, f1 = w
            for rec in self._overlaps(w, True):
                if rec.p0 >= p0 and rec.p1 <= p1 and rec.f0 >= f0 and rec.f1 <= f1:
                    rec.dead = True
            self._add_rec(w, True, op)
        for r in rregs:
            name, p0, p1, f0, f1 = r
            for rec in self._overlaps(r, True):
                if (not rec.w) and rec.op.eng == eng and rec.op.is_dma == is_dma and \
                        rec.p0 == p0 and rec.p1 == p1 and rec.f0 == f0 and rec.f1 == f1 and not is_dma:
                    rec.dead = True
            self._add_rec(r, False, op)
        self.streams[eng].append(op)
        return op

    def drain(self):
        op = Op('sp', None, False)
        op.chan = 'sp'
        op.val = 0
        for k in range(N_DMA_SEMS):
            if self.dma_last[k] is not None:
                op.waits.append(self.dma_last[k])
        op.clock = {}
        self.streams['sp'].append(op)

    def barrier(self):
        for e in self.ENGS:
            self.streams[e].append(None)

    def dma(self, eng, out, in_, extra_reads=(), extra_writes=(), **kw):
        def fn(e, out=out, in_=in_, kw=kw):
            return e.dma_start(out=out, in_=in_, **kw)
        return self.add(eng, fn, reads=[in_] + list(extra_reads), writes=[out] + list(extra_writes), is_dma=True)

    def emit(self):
        nc = self.nc
        fin = Op('sp', None, False)
        for k in range(N_DMA_SEMS):
            if self.dma_last[k] is not None:
                fin.waits.append(self.dma_last[k])
        for e in ('pe', 'act', 'dve', 'pool'):
            if self.streams[e]:
                last = None
                for o in reversed(self.streams[e]):
                    if o is not None and not o.is_dma:
                        last = o
                        break
                if last is not None:
                    last.marked = True
                    fin.waits.append(last)
        self.streams['sp'].append(fin)
        cnt_of = {}
        for e in self.ENGS:
            c = 0
            for o in self.streams[e]:
                if o is None or o.is_dma or o.fn is None:
                    continue
                if o.marked:
                    c += 1
                    cnt_of[id(o)] = c
        segs = {e: [[]] for e in self.ENGS}
        for e in self.ENGS:
            for o in self.streams[e]:
                if o is None:
                    segs[e].append([])
                else:
                    segs[e][-1].append(o)
        nseg = max(len(segs[e]) for e in self.ENGS)
        import contextlib
        with contextlib.ExitStack() as st:
            esem = {e: st.enter_context(nc.semaphore('s_' + e)) for e in self.ENGS}
            dsem = [st.enter_context(nc.semaphore('s_dma%d' % k)) for k in range(N_DMA_SEMS)]

            def make(ename, ops):
                def body(e):
                    for o in ops:
                        ws = {}
                        for d in o.waits:
                            if d.is_dma:
                                key = ('d', d.inc)
                                v = d.val * 16
                            else:
                                key = ('e', d.eng)
                                v = cnt_of[id(d)]
                            if ws.get(key, 0) < v:
                                ws[key] = v
                        for key, v in ws.items():
                            sem = dsem[key[1]] if key[0] == 'd' else esem[key[1]]
                            e.wait_ge(sem, v)
                        if o.fn is None:
                            continue
                        ins = o.fn(e)
                        if DEBUG_NAMES is not None:
                            try:
                                DEBUG_NAMES[str(ins.ins.name)] = getattr(o, 'desc', None)
                            except Exception:
                                pass
                        if o.is_dma:
                            ins.then_inc(dsem[o.inc], 16)
                        elif o.marked:
                            ins.then_inc(esem[ename], 1)
                return body
            for si in range(nseg):
                with nc.Block() as blk:
                    engobj = {'pe': blk.tensor, 'act': blk.scalar, 'dve': blk.vector, 'pool': blk.gpsimd, 'sp': blk.sync}
                    for ename in self.ENGS:
                        ops = segs[ename][si] if si < len(segs[ename]) else []
                        engobj[ename](make(ename, ops))


import math
import contextlib
from concourse.bass_utils import run_bass_kernel_spmd

D = 1024
NT = 2048
NTB = 16
DEPTH = 2
PAST = 512
NKB = 20
DFF = 2816
NFF = 22
INW = 2576
EPS = 1e-5
ALPHA = (2 * DEPTH) ** 0.25
NEG = -30000.0
SB_COLS = 52000


def build_program(nc, dbg=False, stop_after=None, nlayers=DEPTH):
    S = Sched(nc)
    es = contextlib.ExitStack()

    def din(name, shape):
        UNTRACKED.add(name)
        return nc.dram_tensor(name, list(shape), F32, kind="ExternalInput").ap()

    def dout(name, shape):
        return nc.dram_tensor(name, list(shape), F32, kind="ExternalOutput").ap()

    def dscr(name, shape):
        return nc.dram_tensor(name, list(shape), F32, kind=("ExternalOutput" if dbg else "Internal")).ap()

    x_d = din("x", [NT, D])
    cvt_d = din("cvecT", [128, 8])
    cdk_d = din("cache_dk", [DEPTH, PAST, 256])
    cdv_d = din("cache_dv", [DEPTH, PAST, 256])
    cgk_d = din("cache_gk", [DEPTH, PAST, 128])
    cgv_d = din("cache_gv", [DEPTH, PAST, 128])
    stf_d = din("st_f", [DEPTH, 8, 64, 64])
    stb_d = din("st_b", [DEPTH, 8, 64, 64])
    wada_d = din("w_ada", [DEPTH, D, 6 * D])
    bada_d = din("b_ada", [DEPTH, 6 * D])
    win_d = din("w_in", [DEPTH, D, INW])
    wout_d = din("w_out", [DEPTH, D, D])
    lam_d = din("diff_lambda", [DEPTH, 128])
    subg_d = din("diff_subln_g", [DEPTH, 64])
    qkg_d = din("qk_norm_g", [DEPTH, 128])
    convw_d = din("convw", [DEPTH, 128, 30])
    convb_d = din("convb", [DEPTH, 128, 6])
    alog_d = din("ssd_A_log", [DEPTH, 16])
    dtb_d = din("ssd_dt_bias", [DEPTH, 16])
    dd_d = din("ssd_D", [DEPTH, 8])
    ssdg_d = din("ssd_norm_g", [DEPTH, 512])
    lng_d = din("ln_g", [DEPTH, 2, D])
    lnb_d = din("ln_b", [DEPTH, 2, D])
    wfi_d = din("w_ffn_in", [DEPTH, D, 2 * DFF])
    wfo_d = din("w_ffn_out", [DEPTH, DFF, D])
    c128_d = din("c128", [128, 6 * 128])
    rope_d = din("rope", [128, 16 * 96])
    maskb_d = din("maskb", [128, 160])
    keep_d = din("keep", [128, 8])

    y_d = dout("y", [NT, D])
    ndk_d = dout("ndk", [DEPTH, NT, 256])
    ndv_d = dout("ndv", [DEPTH, NT, 256])
    ngk_d = dout("ngk", [DEPTH, NT, 128])
    ngv_d = dout("ngv", [DEPTH, NT, 128])
    nsf_d = dout("nsf", [DEPTH, 8, 8, 64, 64])
    nsb_d = dout("nsb", [DEPTH, 8, 8, 64, 64])

    qts_d = dscr("qts", [512, NT])
    mixt_d = dscr("mixt", [1024, NT])
    xbcs_d = dscr("xbcs", [768, NT])
    zs_d = dscr("zs", [NT, 512])
    if dbg:
        dbg_modf = dout("dbg_modf", [128, 96])
        dbg_dt = dout("dbg_dt", [128, 256])
        dbg_x1 = dout("dbg_x1", [NT, D])

    PERS = 18720
    sb = es.enter_context(nc.sbuf_tensor("sb", [128, PERS], F32))
    ps = [es.enter_context(nc.psum_tensor("ps%d" % i, [128, 512], F32)) for i in range(8)]

    class Alloc:
        def __init__(self, t=None, limit=0):
            self.bind(t, limit)

        def bind(self, t, limit):
            self.t = t
            self.cur = 0
            self.limit = limit

        def get(self, n):
            o = self.cur
            self.cur += n
            assert self.cur <= self.limit, ("SBUF overflow", self.cur, self.limit)
            return self.t[:, o:o + n]

        def reset(self):
            self.cur = 0

    pa = Alloc(sb, PERS)
    X = pa.get(NTB * D).rearrange("p (a b) -> p a b", a=NTB)
    C128 = pa.get(768)
    IDENT = C128[:, 0:128]
    TL = C128[:, 128:256]
    TU = C128[:, 256:384]
    NMF = C128[:, 384:512]
    NMB = C128[:, 512:640]
    ONES = C128[:, 640:768]
    MASKB = pa.get(160)
    KEEP = pa.get(8)
    MODF = pa.get(96).rearrange("p (l c) -> p l c", l=2)
    S8 = pa.get(8)
    DT = pa.get(256).rearrange("p (a b) -> p a b", a=NTB)
    CONVW = pa.get(30).rearrange("p (j k) -> p j k", j=6)
    CONVB = pa.get(6)
    pa.get(4)
    ABC = pa.get(16)
    DTB = pa.get(16)
    DBC = pa.get(8)
    LAMS = pa.get(16)
    SSDG = pa.get(512)
    QKG = pa.get(128)
    SUBG = pa.get(64)
    LAMB = pa.get(128)
    tmpl = pa.get(64)
    ph = Alloc()
    pr = Alloc()
    phase_ctx = [None]
    FREE = SB_COLS - PERS

    def close_phase():
        if phase_ctx[0] is not None:
            S.drain()
            S.barrier()
            phase_ctx[0].close()
            phase_ctx[0] = None

    def open_phase(tag, n_plain, n_r):
        close_phase()
        assert n_plain + n_r <= FREE, (tag, n_plain, n_r, FREE)
        ctx = contextlib.ExitStack()
        tp_ = ctx.enter_context(nc.sbuf_tensor("pp_" + tag, [128, n_plain], F32))
        SB_NAMES.add("pp_" + tag)
        ph.bind(tp_, n_plain)
        if n_r:
            tr_ = ctx.enter_context(nc.sbuf_tensor("pr_" + tag, [128, n_r], F32))
            SB_NAMES.add("pr_" + tag)
            pr.bind(tr_, n_r)
        phase_ctx[0] = ctx

    def aps(*xs):
        return [a for a in xs if hasattr(a, 'tensor')]

    import os as _os0
    USE_R = _os0.environ.get('KDBG_R', '1') == '1'

    def R(ap):
        return ap.bitcast(F32R) if USE_R else ap

    def mm(out, lhsT, rhs, start=True, stop=True, tp=None, after=None, r=False):
        if r:
            lhsT = R(lhsT)
            rhs = R(rhs)

        def fn(e):
            if tp is not None:
                return e.matmul(out, lhsT=lhsT, rhs=rhs, start=start, stop=stop, tile_position=tp, skip_group_check=True)
            return e.matmul(out, lhsT=lhsT, rhs=rhs, start=start, stop=stop, skip_group_check=True)
        return S.add('pe', fn, reads=[lhsT, rhs], writes=[out], after=after)

    def tr(out, in_, ident, after=None):
        return S.add('pe', lambda e: e.transpose(out=out, in_=in_, identity=ident), reads=[in_, ident], writes=[out], after=after)

    def act(out, in_, func, bias=None, scale=None, eng='act'):
        kw = {}
        if bias is not None:
            kw['bias'] = bias
        if scale is not None:
            kw['scale'] = scale
        return S.add('act', lambda e: e.activation(out=out, in_=in_, func=func, **kw),
                     reads=aps(in_, bias, scale), writes=[out])

    def ts(eng, out, in0, s1, s2, op0, op1=None):
        def fn(e):
            if op1 is None:
                return e.tensor_scalar(out=out, in0=in0, scalar1=s1, scalar2=None, op0=op0)
            return e.tensor_scalar(out=out, in0=in0, scalar1=s1, scalar2=s2, op0=op0, op1=op1)
        return S.add(eng, fn, reads=aps(in0, s1, s2), writes=[out])

    def tt(eng, out, in0, in1, op):
        return S.add(eng, lambda e: e.tensor_tensor(out=out, in0=in0, in1=in1, op=op), reads=[in0, in1], writes=[out])

    def stt(out, in0, scalar, in1, op0, op1):
        return S.add('dve', lambda e: e.scalar_tensor_tensor(out=out, in0=in0, scalar=scalar, in1=in1, op0=op0, op1=op1),
                     reads=aps(in0, scalar, in1), writes=[out])

    def cp(eng, out, in_):
        if eng == 'act':
            return S.add('act', lambda e: e.copy(out=out, in_=in_), reads=[in_], writes=[out])
        return S.add(eng, lambda e: e.tensor_copy(out=out, in_=in_), reads=[in_], writes=[out])

    def red(out, in_, op=ALU.add):
        return S.add('dve', lambda e: e.tensor_reduce(out=out, in_=in_, axis=AX.X, op=op), reads=[in_], writes=[out])

    def recip(out, in_):
        return S.add('dve', lambda e: e.reciprocal(out=out, in_=in_), reads=[in_], writes=[out])

    def mset(eng, ap, val):
        return S.add(eng, lambda e: e.memset(ap, val), reads=[], writes=[ap])

    def dma(out, in_, eng=None, cast=False):
        is_store = 'DRAM' in str(out.space)
        if cast and USE_R:
            return S.dma('pool', R(out), in_)
        return S.dma('pool' if is_store else 'sp', out, in_)

    def bc(ap, shape):
        return ap.to_broadcast(list(shape))

    class Pref:
        def __init__(self, keys, ahead, issue):
            self.keys = keys
            self.ahead = ahead
            self.issue = issue
            self.n = 0
            self.buf = {}

        def get(self, idx):
            while self.n < len(self.keys) and self.n <= idx + self.ahead:
                self.buf[self.n] = self.issue(self.keys[self.n])
                self.n += 1
            return self.buf.pop(idx)

    rr = {}

    def rot(key, n):
        v = rr.get(key, 0)
        rr[key] = (v + 1) % n
        return v

    def rstd_from(out, ssum, n, tmp):
        act(tmp, ssum, AF.Sqrt, bias=EPS, scale=1.0 / n)
        recip(out, tmp)

    dma(C128, c128_d)
    dma(MASKB, maskb_d)
    dma(KEEP, keep_d)
    for tb in range(NTB):
        dma(X[:, tb, :], x_d[tb * 128:(tb + 1) * 128, :])
    mset('pool', MODF.rearrange("p l c -> p (l c)"), 0.0)
    mset('pool', DT.rearrange("p a b -> p (a b)"), 0.0)
    dma(S8, cvt_d)
    act(S8, S8, AF.Silu)

    done = [False]

    def finish():
        if phase_ctx[0] is not None:
            phase_ctx[0].close()
            phase_ctx[0] = None
        es.enter_context(nc.sbuf_tensor("filler", [128, FREE], F32))
        S.emit()
        done[0] = True

    for l in range(nlayers):
        lam_init = 0.8 - 0.6 * math.exp(-0.3 * l)
        dma(CONVW.rearrange("p j k -> p (j k)"), convw_d[l])
        dma(CONVB, convb_d[l])
        dma(ABC, alog_d[l:l + 1, :].partition_broadcast(128))
        dma(DTB, dtb_d[l:l + 1, :].partition_broadcast(128))
        dma(DBC, dd_d[l:l + 1, :].partition_broadcast(128))
        dma(SSDG, ssdg_d[l:l + 1, :].partition_broadcast(128))
        dma(QKG, qkg_d[l:l + 1, :].partition_broadcast(128))
        dma(SUBG, subg_d[l:l + 1, :].partition_broadcast(128))
        dma(LAMB, lam_d[l:l + 1, :].partition_broadcast(128))
        act(ABC, ABC, AF.Exp)
        ts('dve', ABC, ABC, -1.0, None, ALU.mult)
        ts('dve', SUBG, SUBG, 1.0 - lam_init, None, ALU.mult)
        tt('dve', tmpl[:, 0:32], LAMB[:, 0:32], LAMB[:, 32:64], ALU.mult)
        tt('dve', tmpl[:, 32:64], LAMB[:, 64:96], LAMB[:, 96:128], ALU.mult)
        red(LAMS[:, 0:2], tmpl.rearrange("p (a b) -> p a b", a=2))
        act(LAMS[:, 2:4], LAMS[:, 0:2], AF.Exp)
        tt('dve', LAMS[:, 4:5], LAMS[:, 3:4], LAMS[:, 2:3], ALU.subtract)
        ts('dve', LAMS[:, 4:5], LAMS[:, 4:5], -lam_init, None, ALU.add)
        NEGLAM = LAMS[:, 4:5]

        open_phase('M%d' % l, 10304, 0)
        WA = [ph.get(4096).rearrange("p (a b) -> p a b", a=8) for _ in range(2)]
        ACC = [ph.get(512) for _ in range(2)]
        BST = [ph.get(512) for _ in range(2)]
        for s in range(12):
            b = s % 2
            wsrc = wada_d[l][:, s * 512:(s + 1) * 512].rearrange("(kc p) n -> p kc n", p=128)
            dma(WA[b], wsrc)
            dma(BST[b][0:1, :], bada_d[l:l + 1, s * 512:(s + 1) * 512])
            ts('dve', ACC[b], WA[b][:, 0, :], S8[:, 0:1], None, ALU.mult)
            for kc in range(1, 8):
                stt(ACC[b], WA[b][:, kc, :], S8[:, kc:kc + 1], ACC[b], ALU.mult, ALU.add)
            tt('pool', ACC[b][0:1, :], ACC[b][0:1, :], BST[b][0:1, :], ALU.add)
            bk = ps[rot('m', 2)]
            for j in range(4):
                mm(bk[:, 2 * j:2 * j + 2], ACC[b][:, j * 128:(j + 1) * 128], ONES[:, 0:2])
            m = s // 2
            hf = s % 2
            cp('dve', MODF[:, l, m * 8 + hf * 4:m * 8 + hf * 4 + 4],
               bk[:, 0:8].rearrange("p (j t) -> p j t", t=2)[:, :, 0])
        ts('dve', MODF[:, l, 8:16], MODF[:, l, 8:16], 1.0, None, ALU.add)
        ts('dve', MODF[:, l, 32:40], MODF[:, l, 32:40], 1.0, None, ALU.add)
        if dbg and l == nlayers - 1:
            dma(dbg_modf, MODF.rearrange("p l c -> p (l c)"))
        if stop_after == 'M' and l == nlayers - 1:
            finish()
            break

        open_phase('PA%d' % l, 6944, 26200)
        KTD = pr.get(2 * 2560).rearrange("p (a b) -> p a b", a=2)
        KTG = pr.get(2560)
        VW = 68 if USE_R else 65
        VD = pr.get(NKB * 4 * VW).rearrange("p (k h e) -> p k h e", k=NKB, h=4)
        VG = pr.get(NKB * 2 * VW).rearrange("p (k h e) -> p k h e", k=NKB, h=2)
        r_mark = pr.cur
        if stop_after == 'M2' and l == nlayers - 1:
            mset('dve', KTG, 1.0)
            mset('pool', VG[:, :, :, 64:65], 1.0)
            _tq = ph.get(512)
            cp('act', _tq, KTG[:, 0:512])
            dma(zs_d[0:128, :], _tq)
            import os as _os3
            if _os3.environ.get('KDBG_M3') == '1':
                _rp = ph.get(384).rearrange("p (a b) -> p a b", a=4)
                dma(_rp, rope_d[:, 0:384].rearrange("p (a b) -> p a b", a=4))
                _t2 = ph.get(384)
                cp('dve', _t2, _rp.rearrange("p a b -> p (a b)"))
                dma(zs_d[128:256, 0:384], _t2)
            if _os3.environ.get('KDBG_M4') == '1':
                _h = pr.get(512)
                _w = pr.get(512)
                cp('dve', R(_h), X[:, 0, 0:512])
                cp('dve', R(_w), X[:, 1, 0:512])
                tr(ps[6][:, 0:128], _h[:, 0:128], IDENT)
                mm(ps[5][:, 0:512], _h[:, 0:128], _w, r=True)
                _t3 = ph.get(512)
                cp('dve', _t3, ps[5][:, 0:512])
                dma(zs_d[256:384, :], _t3)
                cp('dve', _t3[:, 0:128], ps[6][:, 0:128])
                dma(zs_d[384:512, 0:128], _t3[:, 0:128])
            finish()
            break
        HTg = pr.get(4096).rearrange("p (a b) -> p a b", a=8)
        WS = [pr.get(2048).rearrange("p (a b) -> p a b", a=8) for _ in range(3)]
        ROPE = ph.get(4 * 96).rearrange("p (a b) -> p a b", a=4)
        TOK = [ph.get(256) for _ in range(2)]
        NRM = [ph.get(256) for _ in range(2)]
        ROT = [ph.get(256) for _ in range(2)]
        RT = [ph.get(128) for _ in range(4)]
        QST = [ph.get(1024).rearrange("p (a b) -> p a b", a=2) for _ in range(2)]
        ZST = [ph.get(256) for _ in range(2)]
        XST = [ph.get(512) for _ in range(2)]
        CK = QST[0].rearrange("p a (b c) -> p (a b) c", b=2)
        CKG = QST[1][:, 0, :].rearrange("p (a b) -> p a b", a=4)
        SM = ph.get(64)
        if not USE_R:
            for vv in (VD, VG):
                mset('pool', vv[:, :, :, 64:65], 1.0)
        else:
            for kb in range(NKB):
                for (vv, nh_) in ((VD, 4), (VG, 2)):
                    ts('dve', R(vv[:, kb, :, 64]), ONES[:, 0:nh_], 1.0, None, ALU.mult)
                    ts('dve', R(vv[:, kb, :, 65:VW]), ONES[:, 0:nh_ * (VW - 65)].rearrange("p (h e) -> p h e", h=nh_), 0.0, None, ALU.mult)
        for kb in range(4):
            dma(VD[:, 16 + kb, :, 0:64], cdv_d[l, kb * 128:(kb + 1) * 128, :].rearrange("p (h e) -> p h e", h=4), cast=True)
            dma(VG[:, 16 + kb, :, 0:64], cgv_d[l, kb * 128:(kb + 1) * 128, :].rearrange("p (h e) -> p h e", h=2), cast=True)
        dma(CK, cdk_d[l].rearrange("(kb p) c -> p kb c", p=128))
        dma(CKG, cgk_d[l].rearrange("(kb p) c -> p kb c", p=128))
        for kb in range(4):
            bk = ps[6 + rot('p1t', 2)]
            for t2 in range(2):
                tr(bk[:, t2 * 128:(t2 + 1) * 128], CK[:, kb, t2 * 128:(t2 + 1) * 128], IDENT)
            tr(bk[:, 256:384], CKG[:, kb, :], IDENT)
            cp('dve', R(KTD[:, :, 2048 + kb * 128:2048 + (kb + 1) * 128]), bk[:, 0:256].rearrange("p (a b) -> p a b", a=2))
            cp('dve', R(KTG[:, 2048 + kb * 128:2048 + (kb + 1) * 128]), bk[:, 256:384])

        if stop_after == 'P0' and l == nlayers - 1:
            finish()
            break
        SQT = ph.get(256)

        def rope_apply(dv, sv, cos, sin, nh, npair):
            cb = bc(cos.unsqueeze(1), [128, nh, npair])
            sbb = bc(sin.unsqueeze(1), [128, nh, npair])
            n = nh * npair
            ta = RT[0][:, 0:n].rearrange("p (h i) -> p h i", h=nh)
            tb_ = RT[1][:, 0:n].rearrange("p (h i) -> p h i", h=nh)
            tc = RT[2][:, 0:n].rearrange("p (h i) -> p h i", h=nh)
            td = RT[3][:, 0:n].rearrange("p (h i) -> p h i", h=nh)
            tt('dve', ta, sv[:, :, :, 0], cb, ALU.mult)
            tt('pool', tb_, sv[:, :, :, 1], sbb, ALU.mult)
            tt('dve', dv[:, :, :, 0], ta, tb_, ALU.subtract)
            tt('pool', tc, sv[:, :, :, 0], sbb, ALU.mult)
            tt('dve', td, sv[:, :, :, 1], cb, ALU.mult)
            tt('pool', dv[:, :, :, 1], tc, td, ALU.add)

        def v4(ap, nh):
            return ap.rearrange("p (h i t) -> p h i t", h=nh, t=2)

        def rmsn(dst, src, nh, gain):
            sv = src.rearrange("p (h e) -> p h e", h=nh)
            dv = dst.rearrange("p (h e) -> p h e", h=nh)
            sqv = SQT[:, 0:nh * 64].rearrange("p (h e) -> p h e", h=nh)
            tt('dve', sqv, sv, sv, ALU.mult)
            red(SM[:, 0:nh], sqv)
            rstd_from(SM[:, 8:8 + nh], SM[:, 0:nh], 64, SM[:, 16:16 + nh])
            tt('dve', dv, sv, bc(SM[:, 8:8 + nh].unsqueeze(2), [128, nh, 64]), ALU.mult)
            tt('dve', dv, dv, bc(gain.unsqueeze(1), [128, nh, 64]), ALU.mult)

        slabs = [(c0, min(256, INW - c0)) for c0 in range(0, INW, 256)]

        def issue_ws(key):
            c0_, ncol_ = slabs[key[1]]
            wb_ = WS[rot('ws', 3)]
            dma(wb_[:, :, 0:ncol_], win_d[l][:, c0_:c0_ + ncol_].rearrange("(kc p) n -> p kc n", p=128), cast=True)
            return wb_
        ws_pref = Pref([(g_, si_) for g_ in range(4) for si_ in range(len(slabs))], 2, issue_ws)
        for g in range(4):
            dma(ROPE, rope_d[:, g * 384:(g + 1) * 384].rearrange("p (a b) -> p a b", a=4))
            for dc in range(8):
                bk = ps[rot('p1h', 2)]
                for tb in range(4):
                    tr(bk[:, tb * 128:(tb + 1) * 128], X[:, 4 * g + tb, dc * 128:(dc + 1) * 128], IDENT)
                act(R(HTg[:, dc, :]), bk[:, :], AF.Identity, bias=MODF[:, l, dc:dc + 1], scale=MODF[:, l, 8 + dc:9 + dc])
            for si, (c0, ncol) in enumerate(slabs):
                wb = ws_pref.get(g * len(slabs) + si)
                if 7 <= si <= 9:
                    for t2 in range(2):
                        bk = ps[2 + rot('p1p', 4)]
                        for dc in range(8):
                            mm(bk[:, :], wb[:, dc, t2 * 128:(t2 + 1) * 128], HTg[:, dc, :], start=(dc == 0), stop=(dc == 7), r=True)
                        xs = XST[rot('xst', 2)]
                        cp('act', xs, bk[:, :])
                        j = (si - 7) * 2 + t2
                        dma(xbcs_d[j * 128:(j + 1) * 128, g * 512:(g + 1) * 512], xs)
                    continue
                qs = None
                for tb in range(4):
                    tbg = 4 * g + tb
                    bk = ps[2 + rot('p1p', 4)]
                    for dc in range(8):
                        mm(bk[:, 0:ncol], HTg[:, dc, tb * 128:(tb + 1) * 128], wb[:, dc, 0:ncol], start=(dc == 0), stop=(dc == 7), r=True)
                    cosd = ROPE[:, tb, 0:16]
                    sind = ROPE[:, tb, 16:32]
                    cosg = ROPE[:, tb, 32:64]
                    sing = ROPE[:, tb, 64:96]
                    if si == 0 or si == 1:
                        tk = TOK[rot('tok', 2)]
                        ro = ROT[rot('rot', 2)]
                        cp('act', tk, bk[:, 0:256])
                        if si == 1:
                            dma(ndk_d[l, tbg * 128:(tbg + 1) * 128, :], tk, eng='pool')
                        rope_apply(v4(ro, 8), v4(tk, 8), cosd, sind, 8, 16)
                        b2 = ps[6 + rot('p1t', 2)]
                        for t2 in range(2):
                            tr(b2[:, t2 * 128:(t2 + 1) * 128], ro[:, t2 * 128:(t2 + 1) * 128], IDENT)
                        if si == 0:
                            if tb == 0:
                                qs = QST[rot('qst', 2)]
                            cp('dve', R(qs[:, :, tb * 128:(tb + 1) * 128]), b2[:, 0:256].rearrange("p (a b) -> p a b", a=2))
                            if tb == 3:
                                dma(qts_d[0:256, g * 512:(g + 1) * 512].rearrange("(a p) t -> p a t", p=128), qs)
                        else:
                            cp('dve', R(KTD[:, :, tbg * 128:(tbg + 1) * 128]), b2[:, 0:256].rearrange("p (a b) -> p a b", a=2))
                    elif si == 2:
                        tk = TOK[rot('tok', 2)]
                        cp('act', tk, bk[:, 0:256])
                        cp('dve', R(VD[:, tbg, :, 0:64]), tk.rearrange("p (h e) -> p h e", h=4))
                        dma(ndv_d[l, tbg * 128:(tbg + 1) * 128, :], tk)
                    elif si == 3:
                        tk = TOK[rot('tok', 2)]
                        nr = NRM[rot('nrm', 2)]
                        ro = ROT[rot('rot', 2)]
                        cp('act', tk, bk[:, 0:256])
                        rmsn(nr, tk, 4, QKG[:, 0:64])
                        for a_ in range(2):
                            dvw = ro.rearrange("p (b a x) -> p a b x", a=2, b=2)[:, a_].rearrange("p b (i t) -> p b i t", t=2)
                            svw = nr[:, a_ * 128:(a_ + 1) * 128].rearrange("p (b i t) -> p b i t", b=2, t=2)
                            rope_apply(dvw, svw, cosg, sing, 2, 32)
                        b2 = ps[6 + rot('p1t', 2)]
                        for t2 in range(2):
                            tr(b2[:, t2 * 128:(t2 + 1) * 128], ro[:, t2 * 128:(t2 + 1) * 128], IDENT)
                        if tb == 0:
                            qs = QST[rot('qst', 2)]
                        cp('dve', R(qs[:, :, tb * 128:(tb + 1) * 128]), b2[:, 0:256].rearrange("p (a b) -> p a b", a=2))
                        if tb == 3:
                            dma(qts_d[256:512, g * 512:(g + 1) * 512].rearrange("(a p) t -> p a t", p=128), qs)
                    elif si == 4:
                        tk = TOK[rot('tok', 2)]
                        nr = NRM[rot('nrm', 2)]
                        ro = ROT[rot('rot', 2)]
                        cp('act', tk, bk[:, 0:256])
                        cp('dve', R(VG[:, tbg, :, 0:64]), tk[:, 128:256].rearrange("p (h e) -> p h e", h=2))
                        dma(ngv_d[l, tbg * 128:(tbg + 1) * 128, :], tk[:, 128:256])
                        rmsn(nr[:, 0:128], tk[:, 0:128], 2, QKG[:, 64:128])
                        dma(ngk_d[l, tbg * 128:(tbg + 1) * 128, :], nr[:, 0:128], eng='pool')
                        rope_apply(v4(ro[:, 0:128], 2), v4(nr[:, 0:128], 2), cosg, sing, 2, 32)
                        b2 = ps[6 + rot('p1t', 2)]
                        tr(b2[:, 0:128], ro[:, 0:128], IDENT)
                        cp('dve', R(KTG[:, tbg * 128:(tbg + 1) * 128]), b2[:, 0:128])
                    elif si == 5 or si == 6:
                        zt = ZST[rot('zst', 2)]
                        act(zt, bk[:, 0:256], AF.Silu)
                        dma(zs_d[tbg * 128:(tbg + 1) * 128, (si - 5) * 256:(si - 4) * 256], zt)
                    elif si == 10:
                        tt('dve', DT[:, tbg, :], bk[:, 0:16], DTB, ALU.add)
                        act(DT[:, tbg, :], DT[:, tbg, :], AF.Exp)
                        act(DT[:, tbg, :], DT[:, tbg, :], AF.Ln, bias=1.0)
        if dbg and l == nlayers - 1:
            dma(dbg_dt, DT.rearrange("p a b -> p (a b)"))
        if stop_after == 'P' and l == nlayers - 1:
            finish()
            break

        S.barrier()
        pr.cur = r_mark
        ph.reset()
        QG = [pr.get(2048).rearrange("p (a b) -> p a b", a=4) for _ in range(2)]
        PT = [pr.get(512) for _ in range(4)]
        OTOK = ph.get(2048).rearrange("p (a b) -> p a b", a=4)
        T0 = ph.get(256).rearrange("p (a b) -> p a b", a=4)
        T1 = ph.get(256).rearrange("p (a b) -> p a b", a=4)
        T2 = ph.get(256).rearrange("p (a b) -> p a b", a=4)
        RR = ph.get(32)
        MAT = [ph.get(2048).rearrange("p (a b) -> p a b", a=4) for _ in range(2)]
        sc_d = 32 ** -0.5
        sc_g = 64 ** -0.5
        def issue_qg(qg_):
            dma(QG[qg_ % 2], qts_d[:, qg_ * 512:(qg_ + 1) * 512].rearrange("(a p) t -> p a t", p=128), cast=True)
            return QG[qg_ % 2]
        qg_pref = Pref(list(range(4)), 1, issue_qg)
        for qg in range(4):
            qb_ = qg_pref.get(qg)

            def head_pass(kT, kbase, kn, qT, vfn, scale, obank):
                for kb in range(NKB):
                    sbk = ps[rot('as', 3)]
                    tp = (kbase, 0) if kn == 32 else None
                    mm(sbk[:, :], kT(kb), qT, tp=tp, r=True)
                    pt = PT[rot('pt', 4)]
                    for hf in range(2):
                        act(R(pt[:, hf * 256:(hf + 1) * 256]), sbk[:, hf * 256:(hf + 1) * 256], AF.Exp,
                            bias=MASKB[:, kb * 8 + qg * 2 + hf:kb * 8 + qg * 2 + hf + 1], scale=scale)
                    for qb in range(4):
                        mm(obank[:, qb * VW:(qb + 1) * VW], pt[:, qb * 128:(qb + 1) * 128], vfn(kb),
                           start=(kb == 0 and qb == 0), stop=(kb == NKB - 1 and qb == 3), r=True)

            for h in range(4):
                ob = [ps[3 + 2 * (h % 2)], ps[4 + 2 * (h % 2)]]
                for m_ in range(2):
                    hm = 2 * h + m_
                    tile = hm // 4
                    pb = 32 * (hm % 4)
                    head_pass(lambda kb, tile=tile, pb=pb: KTD[pb:pb + 32, tile, kb * 128:(kb + 1) * 128], pb, 32,
                              qb_[pb:pb + 32, tile, :], lambda kb, h=h: VD[:, kb, h, :], sc_d, ob[m_])
                o0 = ob[0][:, 0:4 * VW].rearrange("p (q e) -> p q e", q=4)
                o1 = ob[1][:, 0:4 * VW].rearrange("p (q e) -> p q e", q=4)
                recip(RR[:, 0:4], o0[:, :, 64])
                recip(RR[:, 4:8], o1[:, :, 64])
                ts('dve', RR[:, 4:8], RR[:, 4:8], NEGLAM, None, ALU.mult)
                tt('dve', T0, o0[:, :, 0:64], bc(RR[:, 0:4].unsqueeze(2), [128, 4, 64]), ALU.mult)
                tt('dve', T1, o1[:, :, 0:64], bc(RR[:, 4:8].unsqueeze(2), [128, 4, 64]), ALU.mult)
                tt('pool', T0, T0, T1, ALU.add)
                tt('pool', T2, T0, T0, ALU.mult)
                red(RR[:, 8:12], T2)
                rstd_from(RR[:, 12:16], RR[:, 8:12], 64, RR[:, 16:20])
                tt('dve', T0, T0, bc(RR[:, 12:16].unsqueeze(2), [128, 4, 64]), ALU.mult)
                tt('dve', OTOK[:, :, h * 64:(h + 1) * 64], T0, bc(SUBG.unsqueeze(1), [128, 4, 64]), ALU.mult)
            for qh in range(4):
                a_ = qh // 2
                b_ = qh % 2
                ob = ps[3 + (qh % 2) * 2]
                pb = 64 * a_
                head_pass(lambda kb, pb=pb: KTG[pb:pb + 64, kb * 128:(kb + 1) * 128], pb, 64,
                          qb_[pb:pb + 64, 2 + b_, :], lambda kb, a_=a_: VG[:, kb, a_, :], sc_g, ob)
                o0 = ob[:, 0:4 * VW].rearrange("p (q e) -> p q e", q=4)
                recip(RR[:, 20:24], o0[:, :, 64])
                tt('dve', OTOK[:, :, 256 + qh * 64:256 + (qh + 1) * 64], o0[:, :, 0:64],
                   bc(RR[:, 20:24].unsqueeze(2), [128, 4, 64]), ALU.mult)
            mt = MAT[qg % 2]
            for qb in range(4):
                bk = ps[7]
                for c in range(4):
                    tr(bk[:, c * 128:(c + 1) * 128], OTOK[:, qb, c * 128:(c + 1) * 128], IDENT)
                cp('act', mt[:, :, qb * 128:(qb + 1) * 128], bk[:, :].rearrange("p (a b) -> p a b", a=4))
            dma(mixt_d[0:512, qg * 512:(qg + 1) * 512].rearrange("(a p) t -> p a t", p=128), mt)
        if stop_after == 'A' and l == nlayers - 1:
            finish()
            break

        open_phase('S%d' % l, 30800, 0)
        XH = ph.get(8192).rearrange("p (c e) -> p c e", c=16)
        BT = ph.get(2048)
        CT = ph.get(2048)
        s_mark = ph.cur
        PB = [ph.get(8 * 260).rearrange("p (s i) -> p s i", s=8) for _ in range(2)]
        CA = [ph.get(2048) for _ in range(2)]
        for b in range(2):
            mset('pool', PB[b][:, 0, 0:2], 0.0)
            mset('pool', PB[b][:, 7, 258:260], 0.0)
        for j in range(6):
            pbuf = PB[j % 2]
            ca = CA[j % 2]
            dma(pbuf[:, :, 2:258], xbcs_d[j * 128:(j + 1) * 128, :].rearrange("p (s i) -> p s i", s=8))
            ts('dve', pbuf[:, 1:8, 0:2], pbuf[:, 0:7, 256:258], KEEP[:, 0:1], None, ALU.mult)
            ts('dve', pbuf[:, 0:7, 258:260], pbuf[:, 1:8, 2:4], KEEP[:, 0:1], None, ALU.mult)
            cav = ca.rearrange("p (s i) -> p s i", s=8)
            ts('dve', cav, pbuf[:, :, 0:256], CONVW[:, j, 0:1], CONVB[:, j:j + 1], ALU.mult, ALU.add)
            for k in range(1, 5):
                stt(cav, pbuf[:, :, k:k + 256], CONVW[:, j, k:k + 1], cav, ALU.mult, ALU.add)
            if j < 4:
                act(ca, ca, AF.Silu)
                for c4 in range(4):
                    bk = ps[6 + rot('s0', 2)]
                    for cc in range(4):
                        c = c4 * 4 + cc
                        tr(bk[:, cc * 128:(cc + 1) * 128], ca[:, c * 128:(c + 1) * 128], IDENT)
                    cp('dve', XH[:, c4 * 4:(c4 + 1) * 4, j * 128:(j + 1) * 128], bk[:, :].rearrange("p (a b) -> p a b", a=4))
            elif j == 4:
                act(BT, ca, AF.Silu)
            else:
                act(CT, ca, AF.Silu)
        if stop_after == 'S0' and l == nlayers - 1:
            finish()
            break
        S.barrier()
        ph.cur = s_mark
        YB = ph.get(8192).rearrange("p (c e) -> p c e", c=16)
        ph.get(64 + 256)
        STATE = [ph.get(256), ph.get(256)]
        L2 = ph.get(512).rearrange("p (a b) -> p a b", a=4)
        DTA = ph.get(8)
        ACS = ph.get(8)
        NACS = ph.get(8)
        EACS = ph.get(8)
        TEA = ph.get(8)
        TOEND = ph.get(8)
        CDB = ph.get(8)
        SS = ph.get(8)
        DTA_A = [ph.get(128).rearrange("p (c h) -> p c h", c=16) for _ in range(2)]
        ACS_A = [ph.get(128).rearrange("p (c h) -> p c h", c=16) for _ in range(2)]
        NACS_A = [ph.get(128).rearrange("p (c h) -> p c h", c=16) for _ in range(2)]
        EACS_A = [ph.get(128).rearrange("p (c h) -> p c h", c=16) for _ in range(2)]
        AEND_A = [ph.get(128).rearrange("p (c h) -> p c h", c=16) for _ in range(2)]
        CDB_A = [ph.get(128).rearrange("p (c h) -> p c h", c=16) for _ in range(2)]
        TOE_A = [ph.get(128).rearrange("p (c h) -> p c h", c=16) for _ in range(2)]
        EMf = ph.get(1024)
        EM = EMf.rearrange("p (h s) -> p h s", h=8)
        MT = ph.get(1024).rearrange("p (h s) -> p h s", h=8)
        GT = ph.get(256).rearrange("p (g s) -> p g s", g=2)
        BTOK = ph.get(128)
        XD = ph.get(512)
        XDTE = ph.get(512)
        ZC = [ph.get(512) for _ in range(2)]
        F1 = ph.get(512)
        F2 = ph.get(512)
        MST = [ph.get(512).rearrange("p (a b) -> p a b", a=4) for _ in range(2)]
        SOUT = [ph.get(256).rearrange("p (a b) -> p a b", a=4) for _ in range(2)]

        def h8(ap):
            return ap.rearrange("p (h e) -> p h e", h=8)

        for d_, st_d in enumerate([stf_d, stb_d]):
            mset('pool', L2, 0.0)
            src = st_d[l].rearrange("(i hh) p n -> (hh p) i n", hh=2)
            dma(L2[:, 0:2, 0:64], src[:, 0:2, :])
            dma(L2[:, 2:4, 64:128], src[:, 2:4, :])
            bk = ps[7]
            for i in range(4):
                tr(bk[:, i * 128:(i + 1) * 128], L2[:, i, :], IDENT)
            cp('dve', STATE[d_][0:64, :], bk[0:64, 0:256])
            cp('dve', STATE[d_][64:128, :], bk[64:128, 256:512])

        if stop_after == 'S1' and l == nlayers - 1:
            finish()
            break

        S.barrier()
        import os as _os
        _cut = int(_os.environ.get('KDBG_CUT', '99'))

        def fl(a):
            return a.rearrange("p c h -> p (c h)")
        for d_ in range(2):
            Tm_ = TL if d_ == 0 else TU
            tt('dve', DTA_A[d_], DT[:, :, d_ * 8:(d_ + 1) * 8], bc(ABC[:, d_ * 8:(d_ + 1) * 8].unsqueeze(1), [128, 16, 8]), ALU.mult)
            mm(ps[0][:, 0:128], Tm_, fl(DTA_A[d_]))
            cp('dve', fl(ACS_A[d_]), ps[0][:, 0:128])
            ts('dve', fl(NACS_A[d_]), fl(ACS_A[d_]), -1.0, None, ALU.mult)
            act(fl(EACS_A[d_]), fl(ACS_A[d_]), AF.Exp)
            mm(ps[1][:, 0:128], ONES, fl(DTA_A[d_]))
            cp('dve', fl(AEND_A[d_]), ps[1][:, 0:128])
            act(fl(CDB_A[d_]), fl(AEND_A[d_]), AF.Exp)
            tt('dve', fl(TOE_A[d_]), fl(AEND_A[d_]), fl(ACS_A[d_]), ALU.subtract)
            act(fl(TOE_A[d_]), fl(TOE_A[d_]), AF.Exp)

        def ssd_chunk(d_, c, final):
            cs = slice(c * 128, (c + 1) * 128)
            Tm = TL if d_ == 0 else TU
            NM = NMF if d_ == 0 else NMB
            eidx = 127 if d_ == 0 else 0
            dtc = DT[:, c, d_ * 8:(d_ + 1) * 8]
            DTA = DTA_A[d_][:, c, :]
            ACS = ACS_A[d_][:, c, :]
            NACS = NACS_A[d_][:, c, :]
            EACS = EACS_A[d_][:, c, :]
            CDB = CDB_A[d_][:, c, :]
            TOEND = TOE_A[d_][:, c, :]
            for h in range(8):
                bk = ps[1 + h // 4]
                mm(bk[:, (h % 4) * 128:(h % 4 + 1) * 128], bc(DTA[:, h:h + 1], [128, 128]), Tm)
            for hh in range(2):
                v = ps[1 + hh][:, :].rearrange("p (h s) -> p h s", h=4)
                for h4 in range(4):
                    h = hh * 4 + h4
                    stt(EM[:, h, :], v[:, h4, :], NACS[:, h:h + 1], NM, ALU.add, ALU.add)
            act(EMf, EMf, AF.Exp)
            if _cut <= 2:
                return
            gbank = [ps[3][:, 0:128], ps[0][:, 128:256]]
            prev = None
            for g in range(2):
                prev = mm(gbank[g], BT[64 * g:64 * g + 64, cs], CT[64 * g:64 * g + 64, cs], after=prev)
            for g in range(2):
                cp('act', GT[:, g, :], gbank[g])
            for g in range(2):
                tt('dve', MT[:, 4 * g:4 * g + 4, :], EM[:, 4 * g:4 * g + 4, :],
                   bc(GT[:, g, :].unsqueeze(1), [128, 4, 128]), ALU.mult)
            if _cut <= 3:
                return
            xh = h8(XH[:, c, :])
            tt('dve', h8(XD), xh, bc(dtc.unsqueeze(2), [128, 8, 64]), ALU.mult)
            tt('dve', h8(XDTE), h8(XD), bc(TOEND.unsqueeze(2), [128, 8, 64]), ALU.mult)
            if _cut <= 4:
                return
            for h in range(8):
                mm(ps[4][:, h * 64:(h + 1) * 64], MT[:, h, :], XD[:, h * 64:(h + 1) * 64])
            ybank = [ps[5][:, 0:256], ps[3][:, 256:512]]
            prev = None
            for g in range(2):
                prev = mm(ybank[g], CT[64 * g:64 * g + 64, cs], STATE[d_][64 * g:64 * g + 64, :], after=prev)
            tr(ps[7][:, 0:128], BT[:, cs], IDENT)
            cp('act', BTOK, ps[7][:, 0:128])
            mm(ps[6][:, :], BTOK, XDTE)
            if _cut <= 5:
                return
            for g in range(2):
                tt('dve', F2[:, 256 * g:256 * g + 256].rearrange("p (h e) -> p h e", h=4),
                   ybank[g].rearrange("p (h e) -> p h e", h=4),
                   bc(EACS[:, 4 * g:4 * g + 4].unsqueeze(2), [128, 4, 64]), ALU.mult)
            if not final:
                tt('dve', YB[:, c, :], ps[4][:, :], F2, ALU.add)
            else:
                zc = ZC[rot('zc', 2)]
                dma(zc, zs_d[c * 128:(c + 1) * 128, :])
                tt('dve', F1, ps[4][:, :], F2, ALU.add)
                tt('dve', F1, F1, YB[:, c, :], ALU.add)
                tt('dve', h8(F2), xh, bc(DBC.unsqueeze(2), [128, 8, 64]), ALU.mult)
                tt('dve', F1, F1, F2, ALU.add)
                tt('dve', F1, F1, zc, ALU.mult)
                tt('dve', F2, F1, F1, ALU.mult)
                red(SS[:, 0:1], F2)
                rstd_from(SS[:, 1:2], SS[:, 0:1], 512, SS[:, 2:3])
                ts('dve', F1, F1, SS[:, 1:2], None, ALU.mult)
                tt('dve', F1, F1, SSDG, ALU.mult)
                for j in range(4):
                    tr(ps[7][:, j * 128:(j + 1) * 128], F1[:, j * 128:(j + 1) * 128], IDENT)
                mst = MST[rot('mst', 2)]
                cp('act', mst, ps[7][:, :].rearrange("p (a b) -> p a b", a=4))
                dma(mixt_d[512:1024, cs].rearrange("(j p) t -> p j t", p=128), mst)
            if _cut <= 6:
                return
            for g in range(2):
                sl = STATE[d_][64 * g:64 * g + 64, :]
                sv = sl.rearrange("p (h e) -> p h e", h=4)
                tt('dve', sv, sv, bc(CDB[64 * g:64 * g + 64, 4 * g:4 * g + 4].unsqueeze(2), [64, 4, 64]), ALU.mult)
                tt('dve', sl, sl, ps[6][64 * g:64 * g + 64, 256 * g:256 * g + 256], ALU.add)
            boundary = (c % 2 == 1) if d_ == 0 else (c % 2 == 0)
            if boundary:
                seq = c // 2
                sbank = [ps[7], ps[6]]
                prev = None
                for i in range(4):
                    g = i // 2
                    prev = tr(sbank[g][:, i * 64:(i + 1) * 64], STATE[d_][64 * g:64 * g + 64, (i % 2) * 128:(i % 2 + 1) * 128],
                              IDENT[64 * g:64 * g + 64, 64 * g:64 * g + 64], after=(prev if i == 2 else None))
                so = SOUT[rot('so', 2)]
                for g in range(2):
                    cp('act', so[:, 2 * g:2 * g + 2, :], sbank[g][:, 128 * g:128 * g + 128].rearrange("p (a b) -> p a b", a=2))
                od = nsf_d if d_ == 0 else nsb_d
                dma(od[l, seq].rearrange("(i hh) p n -> (hh p) i n", hh=2), so)
                last = (c == 15) if d_ == 0 else (c == 0)
                if not last:
                    ts('dve', STATE[d_], STATE[d_], KEEP[:, 0:1], None, ALU.mult)

        for c in range(int(_os.environ.get('KDBG_NCH', '16'))):
            ssd_chunk(0, c, False)
        if stop_after == 'S2' and l == nlayers - 1:
            finish()
            break
        for c in reversed(range(16)):
            ssd_chunk(1, c, True)
        if stop_after == 'S' and l == nlayers - 1:
            finish()
            break

        def build_gate(GBC, DG, col0):
            for c in range(8):
                ts('dve', DG[:, c, :], IDENT, MODF[:, l, col0 + c:col0 + c + 1], None, ALU.mult)
            for hf in range(2):
                bk = ps[rot('o', 8)]
                mm(bk[:, :], ONES, DG[:, 4 * hf:4 * hf + 4, :].rearrange("p a b -> p (a b)"))
                cp('act', GBC[:, hf * 512:(hf + 1) * 512], bk[:, :])

        def resid_evac(bk, tbg, hf, GBC, TMP):
            tmp = TMP[rot('tmp', 2)]
            tt('dve', tmp, bk[:, :], GBC[:, hf * 512:(hf + 1) * 512], ALU.mult)
            xs = X[:, tbg, hf * 512:(hf + 1) * 512]
            stt(xs, xs, ALPHA, tmp, ALU.mult, ALU.add)

        def ln_block(tbg, LNG, LNB, ST6, MV):
            xs = X[:, tbg, :]
            for hf in range(2):
                src_ = X[:, tbg, hf * 512:(hf + 1) * 512]
                dst_ = ST6[:, hf * 6:(hf + 1) * 6]
                S.add('dve', lambda e, dst_=dst_, src_=src_: e.bn_stats(out=dst_, in_=src_), reads=[src_], writes=[dst_])
            S.add('dve', lambda e: e.bn_aggr(out=MV[:, 0:2], in_=ST6[:, 0:12]), reads=[ST6[:, 0:12]], writes=[MV[:, 0:2]])
            act(MV[:, 2:3], MV[:, 1:2], AF.Sqrt, bias=EPS, scale=1.0)
            recip(MV[:, 3:4], MV[:, 2:3])
            ts('dve', xs, xs, MV[:, 0:1], MV[:, 3:4], ALU.subtract, ALU.mult)
            tt('pool', xs, xs, LNG, ALU.mult)
            tt('pool', xs, xs, LNB, ALU.add)

        open_phase('O%d' % l, 5400, 16384)
        WO = pr.get(8192).rearrange("p (c n) -> p c n", c=8)
        MX = [pr.get(4096).rearrange("p (c t) -> p c t", c=8) for _ in range(2)]
        GBC = ph.get(1024)
        LNG = ph.get(1024)
        LNB = ph.get(1024)
        DG = ph.get(1024).rearrange("p (c q) -> p c q", c=8)
        TMP = [ph.get(512) for _ in range(2)]
        ST6 = ph.get(16)
        MV = ph.get(8)
        dma(WO, wout_d[l].rearrange("(c p) n -> p c n", p=128), cast=True)
        dma(LNG, lng_d[l, 0:1, :].partition_broadcast(128))
        dma(LNB, lnb_d[l, 0:1, :].partition_broadcast(128))
        build_gate(GBC, DG, 16)
        def issue_mx(g_):
            dma(MX[g_ % 2], mixt_d[:, g_ * 512:(g_ + 1) * 512].rearrange("(c p) t -> p c t", p=128), cast=True)
            return MX[g_ % 2]
        mx_pref = Pref(list(range(4)), 1, issue_mx)
        for g in range(4):
            mx = mx_pref.get(g)
            for tb in range(4):
                tbg = 4 * g + tb
                for hf in range(2):
                    bk = ps[rot('o', 8)]
                    for ec in range(8):
                        mm(bk[:, :], mx[:, ec, tb * 128:(tb + 1) * 128], WO[:, ec, hf * 512:(hf + 1) * 512],
                           start=(ec == 0), stop=(ec == 7), r=True)
                    resid_evac(bk, tbg, hf, GBC, TMP)
                ln_block(tbg, LNG, LNB, ST6, MV)
        if dbg and l == nlayers - 1:
            for tb in range(NTB):
                dma(dbg_x1[tb * 128:(tb + 1) * 128, :], X[:, tb, :])
        if stop_after == 'O' and l == nlayers - 1:
            finish()
            break

        open_phase('F%d' % l, 11600, 20480)
        HTf = pr.get(4096).rearrange("p (a b) -> p a b", a=8)
        ACTT = pr.get(NFF * 512).rearrange("p (i t) -> p i t", i=NFF)
        WGU = [pr.get(2048).rearrange("p (u c n) -> p u c n", u=2, c=8) for _ in range(2)]
        WD = [pr.get(512) for _ in range(2)]
        WGS = [ph.get(2048).rearrange("p (u c n) -> p u c n", u=2, c=8) for _ in range(2)]
        WDS = [ph.get(512) for _ in range(2)]
        GBC = ph.get(1024)
        LNG = ph.get(1024)
        LNB = ph.get(1024)
        DG = ph.get(1024).rearrange("p (c q) -> p c q", c=8)
        TMP = [ph.get(512) for _ in range(2)]
        SIL = [ph.get(512) for _ in range(2)]
        ST6 = ph.get(16)
        MV = ph.get(8)
        dma(LNG, lng_d[l, 1:2, :].partition_broadcast(128))
        dma(LNB, lnb_d[l, 1:2, :].partition_broadcast(128))
        build_gate(GBC, DG, 40)

        NUP = 4 * NFF
        NDN = 4 * 2 * NFF

        def up_dma(n):
            if n >= NUP:
                return
            i_ = n % NFF
            stg = WGS[n % 2]
            dma(stg[:, 0], wfi_d[l][:, i_ * 128:(i_ + 1) * 128].rearrange("(c p) n -> p c n", p=128))
            dma(stg[:, 1], wfi_d[l][:, DFF + i_ * 128:DFF + (i_ + 1) * 128].rearrange("(c p) n -> p c n", p=128))

        def up_round(n):
            if n >= NUP:
                return
            cp('act', R(WGU[n % 2][:, 0]), WGS[n % 2][:, 0])
            cp('dve', R(WGU[n % 2][:, 1]), WGS[n % 2][:, 1])

        def dn_dma(n):
            if n >= NDN:
                return
            i_ = n % NFF
            hf_ = (n // NFF) % 2
            dma(WDS[n % 2], wfo_d[l][i_ * 128:(i_ + 1) * 128, hf_ * 512:(hf_ + 1) * 512])

        def dn_round(n):
            if n >= NDN:
                return
            cp('dve', R(WD[n % 2]), WDS[n % 2])
        up_dma(0)
        up_round(0)
        up_dma(1)
        dn_dma(0)
        dn_round(0)
        dn_dma(1)
        for g in range(4):
            for dc in range(8):
                bk = ps[rot('fh', 2)]
                for tb in range(4):
                    tr(bk[:, tb * 128:(tb + 1) * 128], X[:, 4 * g + tb, dc * 128:(dc + 1) * 128], IDENT)
                act(R(HTf[:, dc, :]), bk[:, :], AF.Identity, bias=MODF[:, l, 24 + dc:25 + dc], scale=MODF[:, l, 32 + dc:33 + dc])
            for i in range(NFF):
                nu = g * NFF + i
                w = WGU[nu % 2]
                bg = ps[rot('fg', 2)]
                bu = ps[2 + rot('fu', 2)]
                for dc in range(8):
                    mm(bg[:, :], w[:, 0, dc, :], HTf[:, dc, :], start=(dc == 0), stop=(dc == 7), r=True)
                for dc in range(8):
                    mm(bu[:, :], w[:, 1, dc, :], HTf[:, dc, :], start=(dc == 0), stop=(dc == 7), r=True)
                sil = SIL[rot('sil', 2)]
                act(sil, bg[:, :], AF.Silu)
                tt('dve', R(ACTT[:, i, :]), sil, bu[:, :], ALU.mult)
                up_round(nu + 1)
                up_dma(nu + 2)
            for hf in range(2):
                for i in range(NFF):
                    nd = (g * 2 + hf) * NFF + i
                    wd = WD[nd % 2]
                    for tb in range(4):
                        mm(ps[4 + tb][:, :], ACTT[:, i, tb * 128:(tb + 1) * 128], wd, start=(i == 0), stop=(i == NFF - 1), r=True)
                    dn_round(nd + 1)
                    dn_dma(nd + 2)
                for tb in range(4):
                    resid_evac(ps[4 + tb], 4 * g + tb, hf, GBC, TMP)
            for tb in range(4):
                ln_block(4 * g + tb, LNG, LNB, ST6, MV)
    if not done[0]:
        for tb in range(NTB):
            dma(y_d[tb * 128:(tb + 1) * 128, :], X[:, tb, :])
        finish()
    if phase_ctx[0] is not None:
        phase_ctx[0].close()
    return es


def _consts():
    i = np.arange(128)
    ident = np.eye(128, dtype=np.float32)
    tl = (i[:, None] <= i[None, :]).astype(np.float32)
    tu = (i[:, None] >= i[None, :]).astype(np.float32)
    nmf = np.where(i[None, :] >= i[:, None], 0.0, NEG).astype(np.float32)
    nmb = np.where(i[None, :] <= i[:, None], 0.0, NEG).astype(np.float32)
    ones = np.ones((128, 128), np.float32)
    return np.concatenate([ident, tl, tu, nmf, nmb, ones], axis=1)


def _rope_tables(is_sample):
    t = np.arange(NT)
    out = np.zeros((NT, 96), np.float32)
    if is_sample:
        row = (t // 64).astype(np.float32)
        col = (t % 64).astype(np.float32)

        def tab(dim):
            nf = dim // 4
            inv = (np.float32(10000.0) ** (-np.arange(nf, dtype=np.float32) / np.float32(nf))).astype(np.float32)
            ang = np.concatenate([row[:, None] * inv, col[:, None] * inv], -1).astype(np.float32)
            return np.cos(ang).astype(np.float32), np.sin(ang).astype(np.float32)
        cd, sd = tab(32)
        cg, sg = tab(64)
        out[:, 0:16] = cd
        out[:, 16:32] = sd
        out[:, 32:64] = cg
        out[:, 64:96] = sg
    else:
        out[:, 0:16] = 1.0
        out[:, 32:64] = 1.0
    return np.ascontiguousarray(out.reshape(16, 128, 96).transpose(1, 0, 2).reshape(128, 16 * 96))


def _maskb(is_sample):
    m = np.zeros((128, NKB, 8), np.float32)
    if not is_sample:
        for kb in range(NKB):
            for qh in range(8):
                if kb >= 16 or (kb // 2) != qh:
                    m[:, kb, qh] = NEG
    return m.reshape(128, 160)


def make_in_maps(inputs):
    f = lambda a: np.ascontiguousarray(np.asarray(a, dtype=np.float32))
    w_shared = {
        "w_ada": f(inputs["w_ada"]), "b_ada": f(inputs["b_ada"]), "w_in": f(inputs["w_in"]), "w_out": f(inputs["w_out"]),
        "diff_lambda": f(inputs["diff_lambda"]).reshape(DEPTH, 128), "diff_subln_g": f(inputs["diff_subln_g"]),
        "qk_norm_g": f(inputs["qk_norm_g"]).reshape(DEPTH, 128),
        "convw": np.ascontiguousarray(f(inputs["ssd_conv_w"]).reshape(DEPTH, 5, 6, 128).transpose(0, 3, 2, 1).reshape(DEPTH, 128, 30)),
        "convb": np.ascontiguousarray(f(inputs["ssd_conv_b"]).reshape(DEPTH, 6, 128).transpose(0, 2, 1)),
        "ssd_A_log": f(inputs["ssd_A_log"]).reshape(DEPTH, 16), "ssd_dt_bias": f(inputs["ssd_dt_bias"]).reshape(DEPTH, 16),
        "ssd_D": f(inputs["ssd_D"]), "ssd_norm_g": f(inputs["ssd_norm_g"]),
        "ln_g": f(inputs["ln_g"]), "ln_b": f(inputs["ln_b"]),
        "w_ffn_in": f(inputs["w_ffn_in"]), "w_ffn_out": f(inputs["w_ffn_out"]),
        "c128": _consts(),
    }
    xp = f(inputs["x_prompt"])
    xs = f(inputs["x_sample"])
    cc = f(inputs["c"])
    cctx = f(inputs["c_ctx"])
    cdk = f(inputs["cache_diff_k"]).reshape(4, DEPTH, PAST, 256)
    cdv = f(inputs["cache_diff_v"]).reshape(4, DEPTH, PAST, 256)
    cgk = f(inputs["cache_gqa_k"]).reshape(4, DEPTH, PAST, 128)
    cgv = f(inputs["cache_gqa_v"]).reshape(4, DEPTH, PAST, 128)
    sf = f(inputs["state_ssd_fwd"])
    sbw = f(inputs["state_ssd_bwd"])
    maps = []
    rope_p, rope_s = _rope_tables(False), _rope_tables(True)
    mb_p, mb_s = _maskb(False), _maskb(True)
    for core in range(8):
        m = dict(w_shared)
        if core < 4:
            m["x"] = np.ascontiguousarray(xp[8 * core:8 * core + 8].reshape(NT, D))
            cv = cctx
            m["cache_dk"] = np.zeros((DEPTH, PAST, 256), np.float32)
            m["cache_dv"] = np.zeros((DEPTH, PAST, 256), np.float32)
            m["cache_gk"] = np.zeros((DEPTH, PAST, 128), np.float32)
            m["cache_gv"] = np.zeros((DEPTH, PAST, 128), np.float32)
            m["st_f"] = np.zeros((DEPTH, 8, 64, 64), np.float32)
            m["st_b"] = np.zeros((DEPTH, 8, 64, 64), np.float32)
            m["rope"] = rope_p
            m["maskb"] = mb_p
            m["keep"] = np.zeros((128, 8), np.float32)
        else:
            j = core - 4
            m["x"] = np.ascontiguousarray(xs[j])
            cv = cc[j]
            m["cache_dk"] = np.ascontiguousarray(cdk[j])
            m["cache_dv"] = np.ascontiguousarray(cdv[j])
            m["cache_gk"] = np.ascontiguousarray(cgk[j])
            m["cache_gv"] = np.ascontiguousarray(cgv[j])
            m["st_f"] = np.ascontiguousarray(sf[j])
            m["st_b"] = np.ascontiguousarray(sbw[j])
            m["rope"] = rope_s
            m["maskb"] = mb_s
            m["keep"] = np.ones((128, 8), np.float32)
        m["cvecT"] = np.ascontiguousarray(cv.reshape(8, 128).T)
        maps.append(m)
    return maps


def kernel(**inputs):
    nc = bass.Bass("TRN2", target_bir_lowering=False)
    es = build_program(nc)
    maps = make_in_maps(inputs)
    res = run_bass_kernel_spmd(nc, maps, core_ids=list(range(8)))
    es.close()
    r = res.results
    B, SEQ = 32, 256
    y_prompt = np.zeros((B, SEQ, D), np.float32)
    y_sample = np.zeros((4, NT, D), np.float32)
    ndk = np.zeros((B, DEPTH, SEQ, 4, 64), np.float32)
    ndv = np.zeros((B, DEPTH, SEQ, 4, 64), np.float32)
    ngk = np.zeros((B, DEPTH, SEQ, 2, 64), np.float32)
    ngv = np.zeros((B, DEPTH, SEQ, 2, 64), np.float32)
    nsf = np.zeros((B, DEPTH, 8, 64, 64), np.float32)
    nsb = np.zeros((B, DEPTH, 8, 64, 64), np.float32)
    for core in range(4):
        o = r[core]
        sl = slice(8 * core, 8 * core + 8)
        y_prompt[sl] = o["y"].reshape(8, SEQ, D)
        ndk[sl] = o["ndk"].reshape(DEPTH, 8, SEQ, 4, 64).transpose(1, 0, 2, 3, 4)
        ndv[sl] = o["ndv"].reshape(DEPTH, 8, SEQ, 4, 64).transpose(1, 0, 2, 3, 4)
        ngk[sl] = o["ngk"].reshape(DEPTH, 8, SEQ, 2, 64).transpose(1, 0, 2, 3, 4)
        ngv[sl] = o["ngv"].reshape(DEPTH, 8, SEQ, 2, 64).transpose(1, 0, 2, 3, 4)
        nsf[sl] = o["nsf"].reshape(DEPTH, 8, 8, 64, 64).transpose(1, 0, 2, 3, 4)
        nsb[sl] = o["nsb"].reshape(DEPTH, 8, 8, 64, 64).transpose(1, 0, 2, 3, 4)
    for j in range(4):
        y_sample[j] = r[4 + j]["y"]
    return (y_prompt, y_sample, ndk, ndv, ngk, ngv, nsf, nsb)
```

```python
import numpy as np
import concourse.bass as bass
import concourse.mybir as mybir

F32 = mybir.dt.float32
F32R = mybir.dt.float32r
AF = mybir.ActivationFunctionType
ALU = mybir.AluOpType
AX = mybir.AxisListType

N_DMA_SEMS = 32
DEBUG_NAMES = None
BIN = 256
BIN_DRAM = 1 << 16
UNTRACKED = set()
SB_NAMES = set(['sb'] + ['ps%d' % i for i in range(8)])


def _prod(xs):
    r = 1
    for x in xs:
        r *= int(x)
    return r


def region(ap):
    t = ap.tensor
    name = t.name
    apl = [(int(s), int(c)) for s, c in ap.ap]
    off = int(ap.offset)
    sp = str(ap.space)
    if 'DRAM' in sp or 'HBM' in sp:
        ext = 1 + sum(abs(s) * (c - 1) for s, c in apl)
        return (name, 0, 1, off, off + ext)
    rowlen = _prod(list(t.shape)[1:])
    p0 = off // rowlen
    f0 = off % rowlen
    np_ = apl[0][1]
    ext = 1 + sum(abs(s) * (c - 1) for s, c in apl[1:])
    if 'PSUM' in sp:
        return (name, 0, 128, 0, rowlen)
    return (name, p0, p0 + np_, f0, f0 + ext)


class Op:
    __slots__ = ('eng', 'fn', 'waits', 'is_dma', 'chan', 'val', 'marked', 'clock', 'inc', 'desc')

    def __init__(self, eng, fn, is_dma):
        self.eng = eng
        self.fn = fn
        self.is_dma = is_dma
        self.waits = []
        self.marked = False
        self.chan = None
        self.val = 0
        self.clock = None


class Rec:
    __slots__ = ('p0', 'p1', 'f0', 'f1', 'w', 'op', 'dead')

    def __init__(self, r, w, op):
        self.p0, self.p1, self.f0, self.f1 = r[1], r[2], r[3], r[4]
        self.w = w
        self.op = op
        self.dead = False


class Sched:
    ENGS = ('pe', 'act', 'dve', 'pool', 'sp')

    def __init__(self, nc):
        self.nc = nc
        self.streams = {e: [] for e in self.ENGS}
        self.pos = {e: 0 for e in self.ENGS}
        self.clock = {e: {} for e in self.ENGS}
        self.bins = {}
        self.dma_rr = 0
        self.dma_rr2 = [0, 0]
        self.dma_last = [None] * N_DMA_SEMS
        self.dma_cnt = [0] * N_DMA_SEMS
        self.nops = 0

    def _overlaps(self, reg, want_reads):
        name, p0, p1, f0, f1 = reg
        out = []
        seen = set()
        bn = BIN if name in SB_NAMES else BIN_DRAM
        for b in range(f0 // bn, (f1 - 1) // bn + 1):
            lst = self.bins.get((name, b))
            if not lst:
                continue
            alive = []
            for rec in lst:
                if rec.dead:
                    continue
                alive.append(rec)
                if id(rec) in seen:
                    continue
                if rec.f0 < f1 and f0 < rec.f1 and rec.p0 < p1 and p0 < rec.p1:
                    if rec.w or want_reads:
                        seen.add(id(rec))
                        out.append(rec)
            if len(alive) != len(lst):
                self.bins[(name, b)] = alive
        return out

    def _add_rec(self, reg, w, op):
        name, p0, p1, f0, f1 = reg
        rec = Rec(reg, w, op)
        bn = BIN if name in SB_NAMES else BIN_DRAM
        for b in range(f0 // bn, (f1 - 1) // bn + 1):
            self.bins.setdefault((name, b), []).append(rec)

    def _need(self, op, dep):
        if dep is op:
            return
        e = op.eng
        ck = self.clock[e]
        if ck.get(dep.chan, 0) >= dep.val:
            return
        op.waits.append(dep)
        dep.marked = True
        for k, v in dep.clock.items():
            if ck.get(k, 0) < v:
                ck[k] = v
        if ck.get(dep.chan, 0) < dep.val:
            ck[dep.chan] = dep.val

    def add(self, eng, fn, reads=(), writes=(), is_dma=False, after=None):
        op = Op(eng, fn, is_dma)
        self.nops += 1
        if DEBUG_NAMES is not None:
            op.desc = (eng, [region(a) for a in reads], [region(a) for a in writes])
        rregs = [r_ for r_ in (region(a) for a in reads) if r_[0] not in UNTRACKED]
        wregs = [region(a) for a in writes]
        deps = []
        for r in rregs:
            for rec in self._overlaps(r, False):
                deps.append(rec.op)
        for w in wregs:
            for rec in self._overlaps(w, True):
                deps.append(rec.op)
        if is_dma:
            half = N_DMA_SEMS // 2
            qi = 0 if eng == 'sp' else 1
            k = qi * half + self.dma_rr2[qi]
            self.dma_rr2[qi] = (self.dma_rr2[qi] + 1) % half
            if self.dma_last[k] is not None:
                deps.append(self.dma_last[k])
            self.dma_cnt[k] += 1
            op.chan = 'dma%d' % k
            op.val = self.dma_cnt[k]
            self.dma_last[k] = op
            op.inc = k
        else:
            self.pos[eng] += 1
            op.chan = eng
            op.val = self.pos[eng]
            op.inc = None
        own = [d for d in deps if (not d.is_dma) and d.eng == eng]
        oth = [d for d in deps if d.is_dma or d.eng != eng]
        oth.sort(key=lambda d: -d.val)
        own.sort(key=lambda d: -d.val)
        for d in oth + own:
            if eng == 'pe' and d.eng == 'pe' and not d.is_dma and not is_dma:
                continue
            self._need(op, d)
        if after is not None:
            self._need(op, after)
        op.clock = dict(self.clock[eng])
        for w in wregs:
            name, p0, p1, f0, f1 = w
            for rec in self._overlaps(w, True):
                if rec.p0 >= p0 and rec.p1 <= p1 and rec.f0 >= f0 and rec.f1 <= f1:
                    rec.dead = True
            self._add_rec(w, True, op)
        for r in rregs:
            name, p0, p1, f0, f1 = r
            for rec in self._overlaps(r, True):
                if (not rec.w) and rec.op.eng == eng and rec.op.is_dma == is_dma and \
                        rec.p0 == p0 and rec.p1 == p1 and rec.f0 == f0 and rec.f1 == f1 and not is_dma:
                    rec.dead = True
            self._add_rec(r, False, op)
        self.streams[eng].append(op)
        return op

    def drain(self):
        op = Op('sp', None, False)
        op.chan = 'sp'
        op.val = 0
        for k in range(N_DMA_SEMS):
            if self.dma_last[k] is not None:
                op.waits.append(self.dma_last[k])
        op.clock = {}
        self.streams['sp'].append(op)

    def barrier(self):
        for e in self.ENGS:
            self.streams[e].append(None)

    def dma(self, eng, out, in_, extra_reads=(), extra_writes=(), **kw):
        def fn(e, out=out, in_=in_, kw=kw):
            return e.dma_start(out=out, in_=in_, **kw)
        return self.add(eng, fn, reads=[in_] + list(extra_reads), writes=[out] + list(extra_writes), is_dma=True)

    def emit(self):
        nc = self.nc
        fin = Op('sp', None, False)
        for k in range(N_DMA_SEMS):
            if self.dma_last[k] is not None:
                fin.waits.append(self.dma_last[k])
        for e in ('pe', 'act', 'dve', 'pool'):
            if self.streams[e]:
                last = None
                for o in reversed(self.streams[e]):
                    if o is not None and not o.is_dma:
                        last = o
                        break
                if last is not None:
                    last.marked = True
                    fin.waits.append(last)
        self.streams['sp'].append(fin)
        cnt_of = {}
        for e in self.ENGS:
            c = 0
            for o in self.streams[e]:
                if o is None or o.is_dma or o.fn is None:
                    continue
                if o.marked:
                    c += 1
                    cnt_of[id(o)] = c
        segs = {e: [[]] for e in self.ENGS}
        for e in self.ENGS:
            for o in self.streams[e]:
                if o is None:
                    segs[e].append([])
                else:
                    segs[e][-1].append(o)
        nseg = max(len(segs[e]) for e in self.ENGS)
        import contextlib
        with contextlib.ExitStack() as st:
            esem = {e: st.enter_context(nc.semaphore('s_' + e)) for e in self.ENGS}
            dsem = [st.enter_context(nc.semaphore('s_dma%d' % k)) for k in range(N_DMA_SEMS)]

            def make(ename, ops):
                def body(e):
                    for o in ops:
                        ws = {}
                        for d in o.waits:
                            if d.is_dma:
                                key = ('d', d.inc)
                                v = d.val * 16
                            else:
                                key = ('e', d.eng)
                                v = cnt_of[id(d)]
                            if ws.get(key, 0) < v:
                                ws[key] = v
                        for key, v in ws.items():
                            sem = dsem[key[1]] if key[0] == 'd' else esem[key[1]]
                            e.wait_ge(sem, v)
                        if o.fn is None:
                            continue
                        ins = o.fn(e)
                        if DEBUG_NAMES is not None:
                            try:
                                DEBUG_NAMES[str(ins.ins.name)] = getattr(o, 'desc', None)
                            except Exception:
                                pass
                        if o.is_dma:
                            ins.then_inc(dsem[o.inc], 16)
                        elif o.marked:
                            ins.then_inc(esem[ename], 1)
                return body
            for si in range(nseg):
                with (nc.Block(no_gpsimd_drain=True) if si < nseg - 1 else nc.Block()) as blk:
                    engobj = {'pe': blk.tensor, 'act': blk.scalar, 'dve': blk.vector, 'pool': blk.gpsimd, 'sp': blk.sync}
                    for ename in self.ENGS:
                        ops = segs[ename][si] if si < len(segs[ename]) else []
                        engobj[ename](make(ename, ops))


import math
import contextlib
from concourse.bass_utils import run_bass_kernel_spmd

D = 1024
NT = 2048
NTB = 16
DEPTH = 2
PAST = 512
NKB = 20
DFF = 2816
NFF = 22
INW = 2576
EPS = 1e-5
ALPHA = (2 * DEPTH) ** 0.25
NEG = -30000.0
SB_COLS = 52000


def build_program(nc, dbg=False, stop_after=None, nlayers=DEPTH):
    S = Sched(nc)
    es = contextlib.ExitStack()

    def din(name, shape):
        UNTRACKED.add(name)
        return nc.dram_tensor(name, list(shape), F32, kind="ExternalInput").ap()

    def dout(name, shape):
        return nc.dram_tensor(name, list(shape), F32, kind="ExternalOutput").ap()

    def dscr(name, shape):
        return nc.dram_tensor(name, list(shape), F32, kind=("ExternalOutput" if dbg else "Internal")).ap()

    x_d = din("x", [NT, D])
    cvt_d = din("cvecT", [128, 8])
    cdk_d = din("cache_dk", [DEPTH, PAST, 256])
    cdv_d = din("cache_dv", [DEPTH, PAST, 256])
    cgk_d = din("cache_gk", [DEPTH, PAST, 128])
    cgv_d = din("cache_gv", [DEPTH, PAST, 128])
    stf_d = din("st_f", [DEPTH, 8, 64, 64])
    stb_d = din("st_b", [DEPTH, 8, 64, 64])
    wada_d = din("w_ada", [DEPTH, D, 6 * D])
    bada_d = din("b_ada", [DEPTH, 6 * D])
    win_d = din("w_in", [DEPTH, D, INW])
    wout_d = din("w_out", [DEPTH, D, D])
    lam_d = din("diff_lambda", [DEPTH, 128])
    subg_d = din("diff_subln_g", [DEPTH, 64])
    qkg_d = din("qk_norm_g", [DEPTH, 128])
    convw_d = din("convw", [DEPTH, 128, 30])
    convb_d = din("convb", [DEPTH, 128, 6])
    alog_d = din("ssd_A_log", [DEPTH, 16])
    dtb_d = din("ssd_dt_bias", [DEPTH, 16])
    dd_d = din("ssd_D", [DEPTH, 8])
    ssdg_d = din("ssd_norm_g", [DEPTH, 512])
    lng_d = din("ln_g", [DEPTH, 2, D])
    lnb_d = din("ln_b", [DEPTH, 2, D])
    wfi_d = din("w_ffn_in", [DEPTH, D, 2 * DFF])
    wfo_d = din("w_ffn_out", [DEPTH, DFF, D])
    c128_d = din("c128", [128, 6 * 128])
    rope_d = din("rope", [128, 16 * 96])
    maskb_d = din("maskb", [128, 160])
    keep_d = din("keep", [128, 8])

    y_d = dout("y", [NT, D])
    ndk_d = dout("ndk", [DEPTH, NT, 256])
    ndv_d = dout("ndv", [DEPTH, NT, 256])
    ngk_d = dout("ngk", [DEPTH, NT, 128])
    ngv_d = dout("ngv", [DEPTH, NT, 128])
    nsf_d = dout("nsf", [DEPTH, 8, 8, 64, 64])
    nsb_d = dout("nsb", [DEPTH, 8, 8, 64, 64])

    qts_d = dscr("qts", [512, NT])
    mixt_d = dscr("mixt", [1024, NT])
    xbcs_d = dscr("xbcs", [768, NT])
    zs_d = dscr("zs", [NT, 512])
    if dbg:
        dbg_modf = dout("dbg_modf", [128, 96])
        dbg_dt = dout("dbg_dt", [128, 256])
        dbg_x1 = dout("dbg_x1", [NT, D])

    PERS = 18720
    sb = es.enter_context(nc.sbuf_tensor("sb", [128, PERS], F32))
    ps = [es.enter_context(nc.psum_tensor("ps%d" % i, [128, 512], F32)) for i in range(8)]

    class Alloc:
        def __init__(self, t=None, limit=0):
            self.bind(t, limit)

        def bind(self, t, limit):
            self.t = t
            self.cur = 0
            self.limit = limit

        def get(self, n):
            o = self.cur
            self.cur += n
            assert self.cur <= self.limit, ("SBUF overflow", self.cur, self.limit)
            return self.t[:, o:o + n]

        def reset(self):
            self.cur = 0

    pa = Alloc(sb, PERS)
    X = pa.get(NTB * D).rearrange("p (a b) -> p a b", a=NTB)
    C128 = pa.get(768)
    IDENT = C128[:, 0:128]
    TL = C128[:, 128:256]
    TU = C128[:, 256:384]
    NMF = C128[:, 384:512]
    NMB = C128[:, 512:640]
    ONES = C128[:, 640:768]
    MASKB = pa.get(160)
    KEEP = pa.get(8)
    MODF = pa.get(96).rearrange("p (l c) -> p l c", l=2)
    S8 = pa.get(8)
    DT = pa.get(256).rearrange("p (a b) -> p a b", a=NTB)
    CONVW = pa.get(30).rearrange("p (j k) -> p j k", j=6)
    CONVB = pa.get(6)
    pa.get(4)
    ABC = pa.get(16)
    DTB = pa.get(16)
    DBC = pa.get(8)
    LAMS = pa.get(16)
    SSDG = pa.get(512)
    QKG = pa.get(128)
    SUBG = pa.get(64)
    LAMB = pa.get(128)
    tmpl = pa.get(64)
    ph = Alloc()
    pr = Alloc()
    phase_ctx = [None]
    FREE = SB_COLS - PERS

    def close_phase():
        if phase_ctx[0] is not None:
            S.drain()
            S.barrier()
            phase_ctx[0].close()
            phase_ctx[0] = None

    def open_phase(tag, n_plain, n_r):
        close_phase()
        assert n_plain + n_r <= FREE, (tag, n_plain, n_r, FREE)
        ctx = contextlib.ExitStack()
        tp_ = ctx.enter_context(nc.sbuf_tensor("pp_" + tag, [128, n_plain], F32))
        SB_NAMES.add("pp_" + tag)
        ph.bind(tp_, n_plain)
        if n_r:
            tr_ = ctx.enter_context(nc.sbuf_tensor("pr_" + tag, [128, n_r], F32))
            SB_NAMES.add("pr_" + tag)
            pr.bind(tr_, n_r)
        phase_ctx[0] = ctx

    def aps(*xs):
        return [a for a in xs if hasattr(a, 'tensor')]

    import os as _os0
    USE_R = _os0.environ.get('KDBG_R', '1') == '1'

    def R(ap):
        return ap.bitcast(F32R) if USE_R else ap

    def mm(out, lhsT, rhs, start=True, stop=True, tp=None, after=None, r=False):
        if r:
            lhsT = R(lhsT)
            rhs = R(rhs)

        def fn(e):
            if tp is not None:
                return e.matmul(out, lhsT=lhsT, rhs=rhs, start=start, stop=stop, tile_position=tp, skip_group_check=True)
            return e.matmul(out, lhsT=lhsT, rhs=rhs, start=start, stop=stop, skip_group_check=True)
        return S.add('pe', fn, reads=[lhsT, rhs], writes=[out], after=after)

    def tr(out, in_, ident, after=None):
        return S.add('pe', lambda e: e.transpose(out=out, in_=in_, identity=ident), reads=[in_, ident], writes=[out], after=after)

    def act(out, in_, func, bias=None, scale=None, eng='act'):
        kw = {}
        if bias is not None:
            kw['bias'] = bias
        if scale is not None:
            kw['scale'] = scale
        return S.add('act', lambda e: e.activation(out=out, in_=in_, func=func, **kw),
                     reads=aps(in_, bias, scale), writes=[out])

    def ts(eng, out, in0, s1, s2, op0, op1=None):
        def fn(e):
            if op1 is None:
                return e.tensor_scalar(out=out, in0=in0, scalar1=s1, scalar2=None, op0=op0)
            return e.tensor_scalar(out=out, in0=in0, scalar1=s1, scalar2=s2, op0=op0, op1=op1)
        return S.add(eng, fn, reads=aps(in0, s1, s2), writes=[out])

    def tt(eng, out, in0, in1, op):
        return S.add(eng, lambda e: e.tensor_tensor(out=out, in0=in0, in1=in1, op=op), reads=[in0, in1], writes=[out])

    def stt(out, in0, scalar, in1, op0, op1):
        return S.add('dve', lambda e: e.scalar_tensor_tensor(out=out, in0=in0, scalar=scalar, in1=in1, op0=op0, op1=op1),
                     reads=aps(in0, scalar, in1), writes=[out])

    def cp(eng, out, in_):
        if eng == 'act':
            return S.add('act', lambda e: e.copy(out=out, in_=in_), reads=[in_], writes=[out])
        return S.add(eng, lambda e: e.tensor_copy(out=out, in_=in_), reads=[in_], writes=[out])

    def red(out, in_, op=ALU.add):
        return S.add('dve', lambda e: e.tensor_reduce(out=out, in_=in_, axis=AX.X, op=op), reads=[in_], writes=[out])

    def recip(out, in_):
        return S.add('dve', lambda e: e.reciprocal(out=out, in_=in_), reads=[in_], writes=[out])

    def mset(eng, ap, val):
        return S.add(eng, lambda e: e.memset(ap, val), reads=[], writes=[ap])

    def dma(out, in_, eng=None, cast=False):
        is_store = 'DRAM' in str(out.space)
        if cast and USE_R:
            return S.dma('pool', R(out), in_)
        return S.dma('pool' if is_store else 'sp', out, in_)

    def bc(ap, shape):
        return ap.to_broadcast(list(shape))

    class Pref:
        def __init__(self, keys, ahead, issue):
            self.keys = keys
            self.ahead = ahead
            self.issue = issue
            self.n = 0
            self.buf = {}

        def get(self, idx):
            while self.n < len(self.keys) and self.n <= idx + self.ahead:
                self.buf[self.n] = self.issue(self.keys[self.n])
                self.n += 1
            return self.buf.pop(idx)

    rr = {}

    def rot(key, n):
        v = rr.get(key, 0)
        rr[key] = (v + 1) % n
        return v

    def rstd_from(out, ssum, n, tmp):
        act(tmp, ssum, AF.Sqrt, bias=EPS, scale=1.0 / n)
        recip(out, tmp)

    dma(C128, c128_d)
    dma(MASKB, maskb_d)
    dma(KEEP, keep_d)
    for tb in range(NTB):
        dma(X[:, tb, :], x_d[tb * 128:(tb + 1) * 128, :])
    mset('pool', MODF.rearrange("p l c -> p (l c)"), 0.0)
    mset('pool', DT.rearrange("p a b -> p (a b)"), 0.0)
    dma(S8, cvt_d)
    act(S8, S8, AF.Silu)

    done = [False]

    def finish():
        if phase_ctx[0] is not None:
            phase_ctx[0].close()
            phase_ctx[0] = None
        es.enter_context(nc.sbuf_tensor("filler", [128, FREE], F32))
        S.emit()
        done[0] = True

    for l in range(nlayers):
        lam_init = 0.8 - 0.6 * math.exp(-0.3 * l)
        dma(CONVW.rearrange("p j k -> p (j k)"), convw_d[l])
        dma(CONVB, convb_d[l])
        dma(ABC, alog_d[l:l + 1, :].partition_broadcast(128))
        dma(DTB, dtb_d[l:l + 1, :].partition_broadcast(128))
        dma(DBC, dd_d[l:l + 1, :].partition_broadcast(128))
        dma(SSDG, ssdg_d[l:l + 1, :].partition_broadcast(128))
        dma(QKG, qkg_d[l:l + 1, :].partition_broadcast(128))
        dma(SUBG, subg_d[l:l + 1, :].partition_broadcast(128))
        dma(LAMB, lam_d[l:l + 1, :].partition_broadcast(128))
        act(ABC, ABC, AF.Exp)
        ts('dve', ABC, ABC, -1.0, None, ALU.mult)
        ts('dve', SUBG, SUBG, 1.0 - lam_init, None, ALU.mult)
        tt('dve', tmpl[:, 0:32], LAMB[:, 0:32], LAMB[:, 32:64], ALU.mult)
        tt('dve', tmpl[:, 32:64], LAMB[:, 64:96], LAMB[:, 96:128], ALU.mult)
        red(LAMS[:, 0:2], tmpl.rearrange("p (a b) -> p a b", a=2))
        act(LAMS[:, 2:4], LAMS[:, 0:2], AF.Exp)
        tt('dve', LAMS[:, 4:5], LAMS[:, 3:4], LAMS[:, 2:3], ALU.subtract)
        ts('dve', LAMS[:, 4:5], LAMS[:, 4:5], -lam_init, None, ALU.add)
        NEGLAM = LAMS[:, 4:5]

        open_phase('M%d' % l, 10304, 0)
        WA = [ph.get(4096).rearrange("p (a b) -> p a b", a=8) for _ in range(2)]
        ACC = [ph.get(512) for _ in range(2)]
        BST = [ph.get(512) for _ in range(2)]
        for s in range(12):
            b = s % 2
            wsrc = wada_d[l][:, s * 512:(s + 1) * 512].rearrange("(kc p) n -> p kc n", p=128)
            dma(WA[b], wsrc)
            dma(BST[b][0:1, :], bada_d[l:l + 1, s * 512:(s + 1) * 512])
            ts('dve', ACC[b], WA[b][:, 0, :], S8[:, 0:1], None, ALU.mult)
            for kc in range(1, 8):
                stt(ACC[b], WA[b][:, kc, :], S8[:, kc:kc + 1], ACC[b], ALU.mult, ALU.add)
            tt('pool', ACC[b][0:1, :], ACC[b][0:1, :], BST[b][0:1, :], ALU.add)
            bk = ps[rot('m', 2)]
            for j in range(4):
                mm(bk[:, 2 * j:2 * j + 2], ACC[b][:, j * 128:(j + 1) * 128], ONES[:, 0:2])
            m = s // 2
            hf = s % 2
            cp('dve', MODF[:, l, m * 8 + hf * 4:m * 8 + hf * 4 + 4],
               bk[:, 0:8].rearrange("p (j t) -> p j t", t=2)[:, :, 0])
        ts('dve', MODF[:, l, 8:16], MODF[:, l, 8:16], 1.0, None, ALU.add)
        ts('dve', MODF[:, l, 32:40], MODF[:, l, 32:40], 1.0, None, ALU.add)
        if dbg and l == nlayers - 1:
            dma(dbg_modf, MODF.rearrange("p l c -> p (l c)"))
        if stop_after == 'M' and l == nlayers - 1:
            finish()
            break

        open_phase('PA%d' % l, 6944, 26200)
        KTD = pr.get(2 * 2560).rearrange("p (a b) -> p a b", a=2)
        KTG = pr.get(2560)
        VW = 68 if USE_R else 65
        VD = pr.get(NKB * 4 * VW).rearrange("p (k h e) -> p k h e", k=NKB, h=4)
        VG = pr.get(NKB * 2 * VW).rearrange("p (k h e) -> p k h e", k=NKB, h=2)
        r_mark = pr.cur
        if stop_after == 'M2' and l == nlayers - 1:
            mset('dve', KTG, 1.0)
            mset('pool', VG[:, :, :, 64:65], 1.0)
            _tq = ph.get(512)
            cp('act', _tq, KTG[:, 0:512])
            dma(zs_d[0:128, :], _tq)
            import os as _os3
            if _os3.environ.get('KDBG_M3') == '1':
                _rp = ph.get(384).rearrange("p (a b) -> p a b", a=4)
                dma(_rp, rope_d[:, 0:384].rearrange("p (a b) -> p a b", a=4))
                _t2 = ph.get(384)
                cp('dve', _t2, _rp.rearrange("p a b -> p (a b)"))
                dma(zs_d[128:256, 0:384], _t2)
            if _os3.environ.get('KDBG_M4') == '1':
                _h = pr.get(512)
                _w = pr.get(512)
                cp('dve', R(_h), X[:, 0, 0:512])
                cp('dve', R(_w), X[:, 1, 0:512])
                tr(ps[6][:, 0:128], _h[:, 0:128], IDENT)
                mm(ps[5][:, 0:512], _h[:, 0:128], _w, r=True)
                _t3 = ph.get(512)
                cp('dve', _t3, ps[5][:, 0:512])
                dma(zs_d[256:384, :], _t3)
                cp('dve', _t3[:, 0:128], ps[6][:, 0:128])
                dma(zs_d[384:512, 0:128], _t3[:, 0:128])
            finish()
            break
        HTg = pr.get(4096).rearrange("p (a b) -> p a b", a=8)
        WS = [pr.get(2048).rearrange("p (a b) -> p a b", a=8) for _ in range(3)]
        ROPE = ph.get(4 * 96).rearrange("p (a b) -> p a b", a=4)
        TOK = [ph.get(256) for _ in range(2)]
        NRM = [ph.get(256) for _ in range(2)]
        ROT = [ph.get(256) for _ in range(2)]
        RT = [ph.get(128) for _ in range(4)]
        QST = [ph.get(1024).rearrange("p (a b) -> p a b", a=2) for _ in range(2)]
        ZST = [ph.get(256) for _ in range(2)]
        XST = [ph.get(512) for _ in range(2)]
        CK = QST[0].rearrange("p a (b c) -> p (a b) c", b=2)
        CKG = QST[1][:, 0, :].rearrange("p (a b) -> p a b", a=4)
        SM = ph.get(64)
        if not USE_R:
            for vv in (VD, VG):
                mset('pool', vv[:, :, :, 64:65], 1.0)
        else:
            for kb in range(NKB):
                for (vv, nh_) in ((VD, 4), (VG, 2)):
                    ts('dve', R(vv[:, kb, :, 64]), ONES[:, 0:nh_], 1.0, None, ALU.mult)
                    ts('dve', R(vv[:, kb, :, 65:VW]), ONES[:, 0:nh_ * (VW - 65)].rearrange("p (h e) -> p h e", h=nh_), 0.0, None, ALU.mult)
        for kb in range(4):
            dma(VD[:, 16 + kb, :, 0:64], cdv_d[l, kb * 128:(kb + 1) * 128, :].rearrange("p (h e) -> p h e", h=4), cast=True)
            dma(VG[:, 16 + kb, :, 0:64], cgv_d[l, kb * 128:(kb + 1) * 128, :].rearrange("p (h e) -> p h e", h=2), cast=True)
        dma(CK, cdk_d[l].rearrange("(kb p) c -> p kb c", p=128))
        dma(CKG, cgk_d[l].rearrange("(kb p) c -> p kb c", p=128))
        for kb in range(4):
            bk = ps[6 + rot('p1t', 2)]
            for t2 in range(2):
                tr(bk[:, t2 * 128:(t2 + 1) * 128], CK[:, kb, t2 * 128:(t2 + 1) * 128], IDENT)
            tr(bk[:, 256:384], CKG[:, kb, :], IDENT)
            cp('dve', R(KTD[:, :, 2048 + kb * 128:2048 + (kb + 1) * 128]), bk[:, 0:256].rearrange("p (a b) -> p a b", a=2))
            cp('dve', R(KTG[:, 2048 + kb * 128:2048 + (kb + 1) * 128]), bk[:, 256:384])

        if stop_after == 'P0' and l == nlayers - 1:
            finish()
            break
        SQT = ph.get(256)

        def rope_apply(dv, sv, cos, sin, nh, npair):
            cb = bc(cos.unsqueeze(1), [128, nh, npair])
            sbb = bc(sin.unsqueeze(1), [128, nh, npair])
            n = nh * npair
            ta = RT[0][:, 0:n].rearrange("p (h i) -> p h i", h=nh)
            tb_ = RT[1][:, 0:n].rearrange("p (h i) -> p h i", h=nh)
            tc = RT[2][:, 0:n].rearrange("p (h i) -> p h i", h=nh)
            td = RT[3][:, 0:n].rearrange("p (h i) -> p h i", h=nh)
            tt('dve', ta, sv[:, :, :, 0], cb, ALU.mult)
            tt('pool', tb_, sv[:, :, :, 1], sbb, ALU.mult)
            tt('dve', dv[:, :, :, 0], ta, tb_, ALU.subtract)
            tt('pool', tc, sv[:, :, :, 0], sbb, ALU.mult)
            tt('dve', td, sv[:, :, :, 1], cb, ALU.mult)
            tt('pool', dv[:, :, :, 1], tc, td, ALU.add)

        def v4(ap, nh):
            return ap.rearrange("p (h i t) -> p h i t", h=nh, t=2)

        def rmsn(dst, src, nh, gain):
            sv = src.rearrange("p (h e) -> p h e", h=nh)
            dv = dst.rearrange("p (h e) -> p h e", h=nh)
            sqv = SQT[:, 0:nh * 64].rearrange("p (h e) -> p h e", h=nh)
            tt('dve', sqv, sv, sv, ALU.mult)
            red(SM[:, 0:nh], sqv)
            rstd_from(SM[:, 8:8 + nh], SM[:, 0:nh], 64, SM[:, 16:16 + nh])
            tt('dve', dv, sv, bc(SM[:, 8:8 + nh].unsqueeze(2), [128, nh, 64]), ALU.mult)
            tt('dve', dv, dv, bc(gain.unsqueeze(1), [128, nh, 64]), ALU.mult)

        slabs = [(c0, min(256, INW - c0)) for c0 in range(0, INW, 256)]

        def issue_ws(key):
            c0_, ncol_ = slabs[key[1]]
            wb_ = WS[rot('ws', 3)]
            dma(wb_[:, :, 0:ncol_], win_d[l][:, c0_:c0_ + ncol_].rearrange("(kc p) n -> p kc n", p=128), cast=True)
            return wb_
        ws_pref = Pref([(g_, si_) for g_ in range(4) for si_ in range(len(slabs))], 2, issue_ws)
        for g in range(4):
            dma(ROPE, rope_d[:, g * 384:(g + 1) * 384].rearrange("p (a b) -> p a b", a=4))
            for dc in range(8):
                bk = ps[rot('p1h', 2)]
                for tb in range(4):
                    tr(bk[:, tb * 128:(tb + 1) * 128], X[:, 4 * g + tb, dc * 128:(dc + 1) * 128], IDENT)
                act(R(HTg[:, dc, :]), bk[:, :], AF.Identity, bias=MODF[:, l, dc:dc + 1], scale=MODF[:, l, 8 + dc:9 + dc])
            for si, (c0, ncol) in enumerate(slabs):
                wb = ws_pref.get(g * len(slabs) + si)
                if 7 <= si <= 9:
                    for t2 in range(2):
                        bk = ps[2 + rot('p1p', 4)]
                        for dc in range(8):
                            mm(bk[:, :], wb[:, dc, t2 * 128:(t2 + 1) * 128], HTg[:, dc, :], start=(dc == 0), stop=(dc == 7), r=True)
                        xs = XST[rot('xst', 2)]
                        cp('act', xs, bk[:, :])
                        j = (si - 7) * 2 + t2
                        dma(xbcs_d[j * 128:(j + 1) * 128, g * 512:(g + 1) * 512], xs)
                    continue
                qs = None
                for tb in range(4):
                    tbg = 4 * g + tb
                    bk = ps[2 + rot('p1p', 4)]
                    for dc in range(8):
                        mm(bk[:, 0:ncol], HTg[:, dc, tb * 128:(tb + 1) * 128], wb[:, dc, 0:ncol], start=(dc == 0), stop=(dc == 7), r=True)
                    cosd = ROPE[:, tb, 0:16]
                    sind = ROPE[:, tb, 16:32]
                    cosg = ROPE[:, tb, 32:64]
                    sing = ROPE[:, tb, 64:96]
                    if si == 0 or si == 1:
                        tk = TOK[rot('tok', 2)]
                        ro = ROT[rot('rot', 2)]
                        cp('act', tk, bk[:, 0:256])
                        if si == 1:
                            dma(ndk_d[l, tbg * 128:(tbg + 1) * 128, :], tk, eng='pool')
                        rope_apply(v4(ro, 8), v4(tk, 8), cosd, sind, 8, 16)
                        b2 = ps[6 + rot('p1t', 2)]
                        for t2 in range(2):
                            tr(b2[:, t2 * 128:(t2 + 1) * 128], ro[:, t2 * 128:(t2 + 1) * 128], IDENT)
                        if si == 0:
                            if tb == 0:
                                qs = QST[rot('qst', 2)]
                            cp('dve', R(qs[:, :, tb * 128:(tb + 1) * 128]), b2[:, 0:256].rearrange("p (a b) -> p a b", a=2))
                            if tb == 3:
                                dma(qts_d[0:256, g * 512:(g + 1) * 512].rearrange("(a p) t -> p a t", p=128), qs)
                        else:
                            cp('dve', R(KTD[:, :, tbg * 128:(tbg + 1) * 128]), b2[:, 0:256].rearrange("p (a b) -> p a b", a=2))
                    elif si == 2:
                        tk = TOK[rot('tok', 2)]
                        cp('act', tk, bk[:, 0:256])
                        cp('dve', R(VD[:, tbg, :, 0:64]), tk.rearrange("p (h e) -> p h e", h=4))
                        dma(ndv_d[l, tbg * 128:(tbg + 1) * 128, :], tk)
                    elif si == 3:
                        tk = TOK[rot('tok', 2)]
                        nr = NRM[rot('nrm', 2)]
                        ro = ROT[rot('rot', 2)]
                        cp('act', tk, bk[:, 0:256])
                        rmsn(nr, tk, 4, QKG[:, 0:64])
                        for a_ in range(2):
                            dvw = ro.rearrange("p (b a x) -> p a b x", a=2, b=2)[:, a_].rearrange("p b (i t) -> p b i t", t=2)
                            svw = nr[:, a_ * 128:(a_ + 1) * 128].rearrange("p (b i t) -> p b i t", b=2, t=2)
                            rope_apply(dvw, svw, cosg, sing, 2, 32)
                        b2 = ps[6 + rot('p1t', 2)]
                        for t2 in range(2):
                            tr(b2[:, t2 * 128:(t2 + 1) * 128], ro[:, t2 * 128:(t2 + 1) * 128], IDENT)
                        if tb == 0:
                            qs = QST[rot('qst', 2)]
                        cp('dve', R(qs[:, :, tb * 128:(tb + 1) * 128]), b2[:, 0:256].rearrange("p (a b) -> p a b", a=2))
                        if tb == 3:
                            dma(qts_d[256:512, g * 512:(g + 1) * 512].rearrange("(a p) t -> p a t", p=128), qs)
                    elif si == 4:
                        tk = TOK[rot('tok', 2)]
                        nr = NRM[rot('nrm', 2)]
                        ro = ROT[rot('rot', 2)]
                        cp('act', tk, bk[:, 0:256])
                        cp('dve', R(VG[:, tbg, :, 0:64]), tk[:, 128:256].rearrange("p (h e) -> p h e", h=2))
                        dma(ngv_d[l, tbg * 128:(tbg + 1) * 128, :], tk[:, 128:256])
                        rmsn(nr[:, 0:128], tk[:, 0:128], 2, QKG[:, 64:128])
                        dma(ngk_d[l, tbg * 128:(tbg + 1) * 128, :], nr[:, 0:128], eng='pool')
                        rope_apply(v4(ro[:, 0:128], 2), v4(nr[:, 0:128], 2), cosg, sing, 2, 32)
                        b2 = ps[6 + rot('p1t', 2)]
                        tr(b2[:, 0:128], ro[:, 0:128], IDENT)
                        cp('dve', R(KTG[:, tbg * 128:(tbg + 1) * 128]), b2[:, 0:128])
                    elif si == 5 or si == 6:
                        zt = ZST[rot('zst', 2)]
                        act(zt, bk[:, 0:256], AF.Silu)
                        dma(zs_d[tbg * 128:(tbg + 1) * 128, (si - 5) * 256:(si - 4) * 256], zt)
                    elif si == 10:
                        tt('dve', DT[:, tbg, :], bk[:, 0:16], DTB, ALU.add)
                        act(DT[:, tbg, :], DT[:, tbg, :], AF.Exp)
                        act(DT[:, tbg, :], DT[:, tbg, :], AF.Ln, bias=1.0)
        if dbg and l == nlayers - 1:
            dma(dbg_dt, DT.rearrange("p a b -> p (a b)"))
        if stop_after == 'P' and l == nlayers - 1:
            finish()
            break

        S.barrier()
        pr.cur = r_mark
        ph.reset()
        QG = [pr.get(2048).rearrange("p (a b) -> p a b", a=4) for _ in range(2)]
        PT = [pr.get(512) for _ in range(4)]
        OTOK = ph.get(2048).rearrange("p (a b) -> p a b", a=4)
        T0 = ph.get(256).rearrange("p (a b) -> p a b", a=4)
        T1 = ph.get(256).rearrange("p (a b) -> p a b", a=4)
        T2 = ph.get(256).rearrange("p (a b) -> p a b", a=4)
        RR = ph.get(32)
        MAT = [ph.get(2048).rearrange("p (a b) -> p a b", a=4) for _ in range(2)]
        sc_d = 32 ** -0.5
        sc_g = 64 ** -0.5
        def issue_qg(qg_):
            dma(QG[qg_ % 2], qts_d[:, qg_ * 512:(qg_ + 1) * 512].rearrange("(a p) t -> p a t", p=128), cast=True)
            return QG[qg_ % 2]
        qg_pref = Pref(list(range(4)), 1, issue_qg)
        for qg in range(4):
            qb_ = qg_pref.get(qg)

            def head_pass(kT, kbase, kn, qT, vfn, scale, obank):
                for kb in range(NKB):
                    sbk = ps[rot('as', 3)]
                    tp = (kbase, 0) if kn == 32 else None
                    mm(sbk[:, :], kT(kb), qT, tp=tp, r=True)
                    pt = PT[rot('pt', 4)]
                    for hf in range(2):
                        act(R(pt[:, hf * 256:(hf + 1) * 256]), sbk[:, hf * 256:(hf + 1) * 256], AF.Exp,
                            bias=MASKB[:, kb * 8 + qg * 2 + hf:kb * 8 + qg * 2 + hf + 1], scale=scale)
                    for qb in range(4):
                        mm(obank[:, qb * VW:(qb + 1) * VW], pt[:, qb * 128:(qb + 1) * 128], vfn(kb),
                           start=(kb == 0 and qb == 0), stop=(kb == NKB - 1 and qb == 3), r=True)

            for h in range(4):
                ob = [ps[3 + 2 * (h % 2)], ps[4 + 2 * (h % 2)]]
                for m_ in range(2):
                    hm = 2 * h + m_
                    tile = hm // 4
                    pb = 32 * (hm % 4)
                    head_pass(lambda kb, tile=tile, pb=pb: KTD[pb:pb + 32, tile, kb * 128:(kb + 1) * 128], pb, 32,
                              qb_[pb:pb + 32, tile, :], lambda kb, h=h: VD[:, kb, h, :], sc_d, ob[m_])
                o0 = ob[0][:, 0:4 * VW].rearrange("p (q e) -> p q e", q=4)
                o1 = ob[1][:, 0:4 * VW].rearrange("p (q e) -> p q e", q=4)
                recip(RR[:, 0:4], o0[:, :, 64])
                recip(RR[:, 4:8], o1[:, :, 64])
                ts('dve', RR[:, 4:8], RR[:, 4:8], NEGLAM, None, ALU.mult)
                tt('dve', T0, o0[:, :, 0:64], bc(RR[:, 0:4].unsqueeze(2), [128, 4, 64]), ALU.mult)
                tt('dve', T1, o1[:, :, 0:64], bc(RR[:, 4:8].unsqueeze(2), [128, 4, 64]), ALU.mult)
                tt('pool', T0, T0, T1, ALU.add)
                tt('pool', T2, T0, T0, ALU.mult)
                red(RR[:, 8:12], T2)
                rstd_from(RR[:, 12:16], RR[:, 8:12], 64, RR[:, 16:20])
                tt('dve', T0, T0, bc(RR[:, 12:16].unsqueeze(2), [128, 4, 64]), ALU.mult)
                tt('dve', OTOK[:, :, h * 64:(h + 1) * 64], T0, bc(SUBG.unsqueeze(1), [128, 4, 64]), ALU.mult)
            for qh in range(4):
                a_ = qh // 2
                b_ = qh % 2
                ob = ps[3 + (qh % 2) * 2]
                pb = 64 * a_
                head_pass(lambda kb, pb=pb: KTG[pb:pb + 64, kb * 128:(kb + 1) * 128], pb, 64,
                          qb_[pb:pb + 64, 2 + b_, :], lambda kb, a_=a_: VG[:, kb, a_, :], sc_g, ob)
                o0 = ob[:, 0:4 * VW].rearrange("p (q e) -> p q e", q=4)
                recip(RR[:, 20:24], o0[:, :, 64])
                tt('dve', OTOK[:, :, 256 + qh * 64:256 + (qh + 1) * 64], o0[:, :, 0:64],
                   bc(RR[:, 20:24].unsqueeze(2), [128, 4, 64]), ALU.mult)
            mt = MAT[qg % 2]
            for qb in range(4):
                bk = ps[7]
                for c in range(4):
                    tr(bk[:, c * 128:(c + 1) * 128], OTOK[:, qb, c * 128:(c + 1) * 128], IDENT)
                cp('act', mt[:, :, qb * 128:(qb + 1) * 128], bk[:, :].rearrange("p (a b) -> p a b", a=4))
            dma(mixt_d[0:512, qg * 512:(qg + 1) * 512].rearrange("(a p) t -> p a t", p=128), mt)
        if stop_after == 'A' and l == nlayers - 1:
            finish()
            break

        open_phase('S%d' % l, 30800, 0)
        XH = ph.get(8192).rearrange("p (c e) -> p c e", c=16)
        BT = ph.get(2048)
        CT = ph.get(2048)
        s_mark = ph.cur
        PB = [ph.get(8 * 260).rearrange("p (s i) -> p s i", s=8) for _ in range(2)]
        CA = [ph.get(2048) for _ in range(2)]
        for b in range(2):
            mset('pool', PB[b][:, 0, 0:2], 0.0)
            mset('pool', PB[b][:, 7, 258:260], 0.0)
        for j in range(6):
            pbuf = PB[j % 2]
            ca = CA[j % 2]
            dma(pbuf[:, :, 2:258], xbcs_d[j * 128:(j + 1) * 128, :].rearrange("p (s i) -> p s i", s=8))
            ts('dve', pbuf[:, 1:8, 0:2], pbuf[:, 0:7, 256:258], KEEP[:, 0:1], None, ALU.mult)
            ts('dve', pbuf[:, 0:7, 258:260], pbuf[:, 1:8, 2:4], KEEP[:, 0:1], None, ALU.mult)
            cav = ca.rearrange("p (s i) -> p s i", s=8)
            ts('dve', cav, pbuf[:, :, 0:256], CONVW[:, j, 0:1], CONVB[:, j:j + 1], ALU.mult, ALU.add)
            for k in range(1, 5):
                stt(cav, pbuf[:, :, k:k + 256], CONVW[:, j, k:k + 1], cav, ALU.mult, ALU.add)
            if j < 4:
                act(ca, ca, AF.Silu)
                for c4 in range(4):
                    bk = ps[6 + rot('s0', 2)]
                    for cc in range(4):
                        c = c4 * 4 + cc
                        tr(bk[:, cc * 128:(cc + 1) * 128], ca[:, c * 128:(c + 1) * 128], IDENT)
                    cp('dve', XH[:, c4 * 4:(c4 + 1) * 4, j * 128:(j + 1) * 128], bk[:, :].rearrange("p (a b) -> p a b", a=4))
            elif j == 4:
                act(BT, ca, AF.Silu)
            else:
                act(CT, ca, AF.Silu)
        if stop_after == 'S0' and l == nlayers - 1:
            finish()
            break
        S.barrier()
        ph.cur = s_mark
        YB = ph.get(8192).rearrange("p (c e) -> p c e", c=16)
        ph.get(64 + 256)
        STATE = [ph.get(256), ph.get(256)]
        L2 = ph.get(512).rearrange("p (a b) -> p a b", a=4)
        DTA = ph.get(8)
        ACS = ph.get(8)
        NACS = ph.get(8)
        EACS = ph.get(8)
        TEA = ph.get(8)
        TOEND = ph.get(8)
        CDB = ph.get(8)
        SS = ph.get(8)
        DTA_A = [ph.get(128).rearrange("p (c h) -> p c h", c=16) for _ in range(2)]
        ACS_A = [ph.get(128).rearrange("p (c h) -> p c h", c=16) for _ in range(2)]
        NACS_A = [ph.get(128).rearrange("p (c h) -> p c h", c=16) for _ in range(2)]
        EACS_A = [ph.get(128).rearrange("p (c h) -> p c h", c=16) for _ in range(2)]
        AEND_A = [ph.get(128).rearrange("p (c h) -> p c h", c=16) for _ in range(2)]
        CDB_A = [ph.get(128).rearrange("p (c h) -> p c h", c=16) for _ in range(2)]
        TOE_A = [ph.get(128).rearrange("p (c h) -> p c h", c=16) for _ in range(2)]
        EMf = ph.get(1024)
        EM = EMf.rearrange("p (h s) -> p h s", h=8)
        MT = ph.get(1024).rearrange("p (h s) -> p h s", h=8)
        GT = ph.get(256).rearrange("p (g s) -> p g s", g=2)
        BTOK = ph.get(128)
        XD = ph.get(512)
        XDTE = ph.get(512)
        ZC = [ph.get(512) for _ in range(2)]
        F1 = ph.get(512)
        F2 = ph.get(512)
        MST = [ph.get(512).rearrange("p (a b) -> p a b", a=4) for _ in range(2)]
        SOUT = [ph.get(256).rearrange("p (a b) -> p a b", a=4) for _ in range(2)]

        def h8(ap):
            return ap.rearrange("p (h e) -> p h e", h=8)

        for d_, st_d in enumerate([stf_d, stb_d]):
            mset('pool', L2, 0.0)
            src = st_d[l].rearrange("(i hh) p n -> (hh p) i n", hh=2)
            dma(L2[:, 0:2, 0:64], src[:, 0:2, :])
            dma(L2[:, 2:4, 64:128], src[:, 2:4, :])
            bk = ps[7]
            for i in range(4):
                tr(bk[:, i * 128:(i + 1) * 128], L2[:, i, :], IDENT)
            cp('dve', STATE[d_][0:64, :], bk[0:64, 0:256])
            cp('dve', STATE[d_][64:128, :], bk[64:128, 256:512])

        if stop_after == 'S1' and l == nlayers - 1:
            finish()
            break

        S.barrier()
        import os as _os
        _cut = int(_os.environ.get('KDBG_CUT', '99'))

        def fl(a):
            return a.rearrange("p c h -> p (c h)")
        for d_ in range(2):
            Tm_ = TL if d_ == 0 else TU
            tt('dve', DTA_A[d_], DT[:, :, d_ * 8:(d_ + 1) * 8], bc(ABC[:, d_ * 8:(d_ + 1) * 8].unsqueeze(1), [128, 16, 8]), ALU.mult)
            mm(ps[0][:, 0:128], Tm_, fl(DTA_A[d_]))
            cp('dve', fl(ACS_A[d_]), ps[0][:, 0:128])
            ts('dve', fl(NACS_A[d_]), fl(ACS_A[d_]), -1.0, None, ALU.mult)
            act(fl(EACS_A[d_]), fl(ACS_A[d_]), AF.Exp)
            mm(ps[1][:, 0:128], ONES, fl(DTA_A[d_]))
            cp('dve', fl(AEND_A[d_]), ps[1][:, 0:128])
            act(fl(CDB_A[d_]), fl(AEND_A[d_]), AF.Exp)
            tt('dve', fl(TOE_A[d_]), fl(AEND_A[d_]), fl(ACS_A[d_]), ALU.subtract)
            act(fl(TOE_A[d_]), fl(TOE_A[d_]), AF.Exp)

        def ssd_chunk(d_, c, final):
            cs = slice(c * 128, (c + 1) * 128)
            Tm = TL if d_ == 0 else TU
            NM = NMF if d_ == 0 else NMB
            eidx = 127 if d_ == 0 else 0
            dtc = DT[:, c, d_ * 8:(d_ + 1) * 8]
            DTA = DTA_A[d_][:, c, :]
            ACS = ACS_A[d_][:, c, :]
            NACS = NACS_A[d_][:, c, :]
            EACS = EACS_A[d_][:, c, :]
            CDB = CDB_A[d_][:, c, :]
            TOEND = TOE_A[d_][:, c, :]
            for h in range(8):
                bk = ps[1 + h // 4]
                mm(bk[:, (h % 4) * 128:(h % 4 + 1) * 128], bc(DTA[:, h:h + 1], [128, 128]), Tm)
            for hh in range(2):
                v = ps[1 + hh][:, :].rearrange("p (h s) -> p h s", h=4)
                for h4 in range(4):
                    h = hh * 4 + h4
                    stt(EM[:, h, :], v[:, h4, :], NACS[:, h:h + 1], NM, ALU.add, ALU.add)
            act(EMf, EMf, AF.Exp)
            if _cut <= 2:
                return
            gbank = [ps[3][:, 0:128], ps[0][:, 128:256]]
            prev = None
            for g in range(2):
                prev = mm(gbank[g], BT[64 * g:64 * g + 64, cs], CT[64 * g:64 * g + 64, cs], after=prev)
            for g in range(2):
                cp('act', GT[:, g, :], gbank[g])
            for g in range(2):
                tt('dve', MT[:, 4 * g:4 * g + 4, :], EM[:, 4 * g:4 * g + 4, :],
                   bc(GT[:, g, :].unsqueeze(1), [128, 4, 128]), ALU.mult)
            if _cut <= 3:
                return
            xh = h8(XH[:, c, :])
            tt('dve', h8(XD), xh, bc(dtc.unsqueeze(2), [128, 8, 64]), ALU.mult)
            tt('dve', h8(XDTE), h8(XD), bc(TOEND.unsqueeze(2), [128, 8, 64]), ALU.mult)
            if _cut <= 4:
                return
            for h in range(8):
                mm(ps[4][:, h * 64:(h + 1) * 64], MT[:, h, :], XD[:, h * 64:(h + 1) * 64])
            ybank = [ps[5][:, 0:256], ps[3][:, 256:512]]
            prev = None
            for g in range(2):
                prev = mm(ybank[g], CT[64 * g:64 * g + 64, cs], STATE[d_][64 * g:64 * g + 64, :], after=prev)
            tr(ps[7][:, 0:128], BT[:, cs], IDENT)
            cp('act', BTOK, ps[7][:, 0:128])
            mm(ps[6][:, :], BTOK, XDTE)
            if _cut <= 5:
                return
            for g in range(2):
                tt('dve', F2[:, 256 * g:256 * g + 256].rearrange("p (h e) -> p h e", h=4),
                   ybank[g].rearrange("p (h e) -> p h e", h=4),
                   bc(EACS[:, 4 * g:4 * g + 4].unsqueeze(2), [128, 4, 64]), ALU.mult)
            if not final:
                tt('dve', YB[:, c, :], ps[4][:, :], F2, ALU.add)
            else:
                zc = ZC[rot('zc', 2)]
                dma(zc, zs_d[c * 128:(c + 1) * 128, :])
                tt('dve', F1, ps[4][:, :], F2, ALU.add)
                tt('dve', F1, F1, YB[:, c, :], ALU.add)
                tt('dve', h8(F2), xh, bc(DBC.unsqueeze(2), [128, 8, 64]), ALU.mult)
                tt('dve', F1, F1, F2, ALU.add)
                tt('dve', F1, F1, zc, ALU.mult)
                tt('dve', F2, F1, F1, ALU.mult)
                red(SS[:, 0:1], F2)
                rstd_from(SS[:, 1:2], SS[:, 0:1], 512, SS[:, 2:3])
                ts('dve', F1, F1, SS[:, 1:2], None, ALU.mult)
                tt('dve', F1, F1, SSDG, ALU.mult)
                for j in range(4):
                    tr(ps[7][:, j * 128:(j + 1) * 128], F1[:, j * 128:(j + 1) * 128], IDENT)
                mst = MST[rot('mst', 2)]
                cp('act', mst, ps[7][:, :].rearrange("p (a b) -> p a b", a=4))
                dma(mixt_d[512:1024, cs].rearrange("(j p) t -> p j t", p=128), mst)
            if _cut <= 6:
                return
            for g in range(2):
                sl = STATE[d_][64 * g:64 * g + 64, :]
                sv = sl.rearrange("p (h e) -> p h e", h=4)
                tt('dve', sv, sv, bc(CDB[64 * g:64 * g + 64, 4 * g:4 * g + 4].unsqueeze(2), [64, 4, 64]), ALU.mult)
                tt('dve', sl, sl, ps[6][64 * g:64 * g + 64, 256 * g:256 * g + 256], ALU.add)
            boundary = (c % 2 == 1) if d_ == 0 else (c % 2 == 0)
            if boundary:
                seq = c // 2
                sbank = [ps[7], ps[6]]
                prev = None
                for i in range(4):
                    g = i // 2
                    prev = tr(sbank[g][:, i * 64:(i + 1) * 64], STATE[d_][64 * g:64 * g + 64, (i % 2) * 128:(i % 2 + 1) * 128],
                              IDENT[64 * g:64 * g + 64, 64 * g:64 * g + 64], after=(prev if i == 2 else None))
                so = SOUT[rot('so', 2)]
                for g in range(2):
                    cp('act', so[:, 2 * g:2 * g + 2, :], sbank[g][:, 128 * g:128 * g + 128].rearrange("p (a b) -> p a b", a=2))
                od = nsf_d if d_ == 0 else nsb_d
                dma(od[l, seq].rearrange("(i hh) p n -> (hh p) i n", hh=2), so)
                last = (c == 15) if d_ == 0 else (c == 0)
                if not last:
                    ts('dve', STATE[d_], STATE[d_], KEEP[:, 0:1], None, ALU.mult)

        for c in range(int(_os.environ.get('KDBG_NCH', '16'))):
            ssd_chunk(0, c, False)
        if stop_after == 'S2' and l == nlayers - 1:
            finish()
            break
        for c in reversed(range(16)):
            ssd_chunk(1, c, True)
        if stop_after == 'S' and l == nlayers - 1:
            finish()
            break

        def build_gate(GBC, DG, col0):
            for c in range(8):
                ts('dve', DG[:, c, :], IDENT, MODF[:, l, col0 + c:col0 + c + 1], None, ALU.mult)
            for hf in range(2):
                bk = ps[rot('o', 8)]
                mm(bk[:, :], ONES, DG[:, 4 * hf:4 * hf + 4, :].rearrange("p a b -> p (a b)"))
                cp('act', GBC[:, hf * 512:(hf + 1) * 512], bk[:, :])

        def resid_evac(bk, tbg, hf, GBC, TMP):
            tmp = TMP[rot('tmp', 2)]
            tt('dve', tmp, bk[:, :], GBC[:, hf * 512:(hf + 1) * 512], ALU.mult)
            xs = X[:, tbg, hf * 512:(hf + 1) * 512]
            stt(xs, xs, ALPHA, tmp, ALU.mult, ALU.add)

        def ln_block(tbg, LNG, LNB, ST6, MV):
            xs = X[:, tbg, :]
            for hf in range(2):
                src_ = X[:, tbg, hf * 512:(hf + 1) * 512]
                dst_ = ST6[:, hf * 6:(hf + 1) * 6]
                S.add('dve', lambda e, dst_=dst_, src_=src_: e.bn_stats(out=dst_, in_=src_), reads=[src_], writes=[dst_])
            S.add('dve', lambda e: e.bn_aggr(out=MV[:, 0:2], in_=ST6[:, 0:12]), reads=[ST6[:, 0:12]], writes=[MV[:, 0:2]])
            act(MV[:, 2:3], MV[:, 1:2], AF.Sqrt, bias=EPS, scale=1.0)
            recip(MV[:, 3:4], MV[:, 2:3])
            ts('dve', xs, xs, MV[:, 0:1], MV[:, 3:4], ALU.subtract, ALU.mult)
            tt('pool', xs, xs, LNG, ALU.mult)
            tt('pool', xs, xs, LNB, ALU.add)

        open_phase('O%d' % l, 5400, 16384)
        WO = pr.get(8192).rearrange("p (c n) -> p c n", c=8)
        MX = [pr.get(4096).rearrange("p (c t) -> p c t", c=8) for _ in range(2)]
        GBC = ph.get(1024)
        LNG = ph.get(1024)
        LNB = ph.get(1024)
        DG = ph.get(1024).rearrange("p (c q) -> p c q", c=8)
        TMP = [ph.get(512) for _ in range(2)]
        ST6 = ph.get(16)
        MV = ph.get(8)
        dma(WO, wout_d[l].rearrange("(c p) n -> p c n", p=128), cast=True)
        dma(LNG, lng_d[l, 0:1, :].partition_broadcast(128))
        dma(LNB, lnb_d[l, 0:1, :].partition_broadcast(128))
        build_gate(GBC, DG, 16)
        def issue_mx(g_):
            dma(MX[g_ % 2], mixt_d[:, g_ * 512:(g_ + 1) * 512].rearrange("(c p) t -> p c t", p=128), cast=True)
            return MX[g_ % 2]
        mx_pref = Pref(list(range(4)), 1, issue_mx)
        for g in range(4):
            mx = mx_pref.get(g)
            for tb in range(4):
                tbg = 4 * g + tb
                for hf in range(2):
                    bk = ps[rot('o', 8)]
                    for ec in range(8):
                        mm(bk[:, :], mx[:, ec, tb * 128:(tb + 1) * 128], WO[:, ec, hf * 512:(hf + 1) * 512],
                           start=(ec == 0), stop=(ec == 7), r=True)
                    resid_evac(bk, tbg, hf, GBC, TMP)
                ln_block(tbg, LNG, LNB, ST6, MV)
        if dbg and l == nlayers - 1:
            for tb in range(NTB):
                dma(dbg_x1[tb * 128:(tb + 1) * 128, :], X[:, tb, :])
        if stop_after == 'O' and l == nlayers - 1:
            finish()
            break

        open_phase('F%d' % l, 6400, 23552)
        HTf = pr.get(4096).rearrange("p (a b) -> p a b", a=8)
        ACTT = pr.get(NFF * 512).rearrange("p (i t) -> p i t", i=NFF)
        WGU = [pr.get(2048).rearrange("p (u c n) -> p u c n", u=2, c=8) for _ in range(3)]
        WD = [pr.get(512) for _ in range(4)]
        GBC = ph.get(1024)
        LNG = ph.get(1024)
        LNB = ph.get(1024)
        DG = ph.get(1024).rearrange("p (c q) -> p c q", c=8)
        TMP = [ph.get(512) for _ in range(2)]
        SIL = [ph.get(512) for _ in range(2)]
        ST6 = ph.get(16)
        MV = ph.get(8)
        dma(LNG, lng_d[l, 1:2, :].partition_broadcast(128))
        dma(LNB, lnb_d[l, 1:2, :].partition_broadcast(128))
        build_gate(GBC, DG, 40)

        def issue_wgu(key):
            i_ = key[1]
            w_ = WGU[rot('wgu', 3)]
            dma(w_[:, 0], wfi_d[l][:, i_ * 128:(i_ + 1) * 128].rearrange("(c p) n -> p c n", p=128), cast=True)
            dma(w_[:, 1], wfi_d[l][:, DFF + i_ * 128:DFF + (i_ + 1) * 128].rearrange("(c p) n -> p c n", p=128), cast=True)
            return w_

        def issue_wd(key):
            _, hf_, i_ = key
            wd_ = WD[rot('wd', 4)]
            dma(wd_, wfo_d[l][i_ * 128:(i_ + 1) * 128, hf_ * 512:(hf_ + 1) * 512], cast=True)
            return wd_
        wgu_pref = Pref([(g_, i_) for g_ in range(4) for i_ in range(NFF)], 2, issue_wgu)
        wd_pref = Pref([(g_, hf_, i_) for g_ in range(4) for hf_ in range(2) for i_ in range(NFF)], 3, issue_wd)
        for g in range(4):
            for dc in range(8):
                bk = ps[rot('fh', 2)]
                for tb in range(4):
                    tr(bk[:, tb * 128:(tb + 1) * 128], X[:, 4 * g + tb, dc * 128:(dc + 1) * 128], IDENT)
                act(R(HTf[:, dc, :]), bk[:, :], AF.Identity, bias=MODF[:, l, 24 + dc:25 + dc], scale=MODF[:, l, 32 + dc:33 + dc])
            for i in range(NFF):
                w = wgu_pref.get(g * NFF + i)
                bg = ps[rot('fg', 2)]
                bu = ps[2 + rot('fu', 2)]
                for dc in range(8):
                    mm(bg[:, :], w[:, 0, dc, :], HTf[:, dc, :], start=(dc == 0), stop=(dc == 7), r=True)
                for dc in range(8):
                    mm(bu[:, :], w[:, 1, dc, :], HTf[:, dc, :], start=(dc == 0), stop=(dc == 7), r=True)
                sil = SIL[rot('sil', 2)]
                act(sil, bg[:, :], AF.Silu)
                tt('dve', R(ACTT[:, i, :]), sil, bu[:, :], ALU.mult)
            for hf in range(2):
                for i in range(NFF):
                    wd = wd_pref.get((g * 2 + hf) * NFF + i)
                    for tb in range(4):
                        mm(ps[4 + tb][:, :], ACTT[:, i, tb * 128:(tb + 1) * 128], wd, start=(i == 0), stop=(i == NFF - 1), r=True)
                for tb in range(4):
                    resid_evac(ps[4 + tb], 4 * g + tb, hf, GBC, TMP)
            for tb in range(4):
                ln_block(4 * g + tb, LNG, LNB, ST6, MV)
    if not done[0]:
        for tb in range(NTB):
            dma(y_d[tb * 128:(tb + 1) * 128, :], X[:, tb, :])
        finish()
    if phase_ctx[0] is not None:
        phase_ctx[0].close()
    return es


def _consts():
    i = np.arange(128)
    ident = np.eye(128, dtype=np.float32)
    tl = (i[:, None] <= i[None, :]).astype(np.float32)
    tu = (i[:, None] >= i[None, :]).astype(np.float32)
    nmf = np.where(i[None, :] >= i[:, None], 0.0, NEG).astype(np.float32)
    nmb = np.where(i[None, :] <= i[:, None], 0.0, NEG).astype(np.float32)
    ones = np.ones((128, 128), np.float32)
    return np.concatenate([ident, tl, tu, nmf, nmb, ones], axis=1)


def _rope_tables(is_sample):
    t = np.arange(NT)
    out = np.zeros((NT, 96), np.float32)
    if is_sample:
        row = (t // 64).astype(np.float32)
        col = (t % 64).astype(np.float32)

        def tab(dim):
            nf = dim // 4
            inv = (np.float32(10000.0) ** (-np.arange(nf, dtype=np.float32) / np.float32(nf))).astype(np.float32)
            ang = np.concatenate([row[:, None] * inv, col[:, None] * inv], -1).astype(np.float32)
            return np.cos(ang).astype(np.float32), np.sin(ang).astype(np.float32)
        cd, sd = tab(32)
        cg, sg = tab(64)
        out[:, 0:16] = cd
        out[:, 16:32] = sd
        out[:, 32:64] = cg
        out[:, 64:96] = sg
    else:
        out[:, 0:16] = 1.0
        out[:, 32:64] = 1.0
    return np.ascontiguousarray(out.reshape(16, 128, 96).transpose(1, 0, 2).reshape(128, 16 * 96))


def _maskb(is_sample):
    m = np.zeros((128, NKB, 8), np.float32)
    if not is_sample:
        for kb in range(NKB):
            for qh in range(8):
                if kb >= 16 or (kb // 2) != qh:
                    m[:, kb, qh] = NEG
    return m.reshape(128, 160)


def make_in_maps(inputs):
    f = lambda a: np.ascontiguousarray(np.asarray(a, dtype=np.float32))
    w_shared = {
        "w_ada": f(inputs["w_ada"]), "b_ada": f(inputs["b_ada"]), "w_in": f(inputs["w_in"]), "w_out": f(inputs["w_out"]),
        "diff_lambda": f(inputs["diff_lambda"]).reshape(DEPTH, 128), "diff_subln_g": f(inputs["diff_subln_g"]),
        "qk_norm_g": f(inputs["qk_norm_g"]).reshape(DEPTH, 128),
        "convw": np.ascontiguousarray(f(inputs["ssd_conv_w"]).reshape(DEPTH, 5, 6, 128).transpose(0, 3, 2, 1).reshape(DEPTH, 128, 30)),
        "convb": np.ascontiguousarray(f(inputs["ssd_conv_b"]).reshape(DEPTH, 6, 128).transpose(0, 2, 1)),
        "ssd_A_log": f(inputs["ssd_A_log"]).reshape(DEPTH, 16), "ssd_dt_bias": f(inputs["ssd_dt_bias"]).reshape(DEPTH, 16),
        "ssd_D": f(inputs["ssd_D"]), "ssd_norm_g": f(inputs["ssd_norm_g"]),
        "ln_g": f(inputs["ln_g"]), "ln_b": f(inputs["ln_b"]),
        "w_ffn_in": f(inputs["w_ffn_in"]), "w_ffn_out": f(inputs["w_ffn_out"]),
        "c128": _consts(),
    }
    xp = f(inputs["x_prompt"])
    xs = f(inputs["x_sample"])
    cc = f(inputs["c"])
    cctx = f(inputs["c_ctx"])
    cdk = f(inputs["cache_diff_k"]).reshape(4, DEPTH, PAST, 256)
    cdv = f(inputs["cache_diff_v"]).reshape(4, DEPTH, PAST, 256)
    cgk = f(inputs["cache_gqa_k"]).reshape(4, DEPTH, PAST, 128)
    cgv = f(inputs["cache_gqa_v"]).reshape(4, DEPTH, PAST, 128)
    sf = f(inputs["state_ssd_fwd"])
    sbw = f(inputs["state_ssd_bwd"])
    maps = []
    rope_p, rope_s = _rope_tables(False), _rope_tables(True)
    mb_p, mb_s = _maskb(False), _maskb(True)
    for core in range(8):
        m = dict(w_shared)
        if core < 4:
            m["x"] = np.ascontiguousarray(xp[8 * core:8 * core + 8].reshape(NT, D))
            cv = cctx
            m["cache_dk"] = np.zeros((DEPTH, PAST, 256), np.float32)
            m["cache_dv"] = np.zeros((DEPTH, PAST, 256), np.float32)
            m["cache_gk"] = np.zeros((DEPTH, PAST, 128), np.float32)
            m["cache_gv"] = np.zeros((DEPTH, PAST, 128), np.float32)
            m["st_f"] = np.zeros((DEPTH, 8, 64, 64), np.float32)
            m["st_b"] = np.zeros((DEPTH, 8, 64, 64), np.float32)
            m["rope"] = rope_p
            m["maskb"] = mb_p
            m["keep"] = np.zeros((128, 8), np.float32)
        else:
            j = core - 4
            m["x"] = np.ascontiguousarray(xs[j])
            cv = cc[j]
            m["cache_dk"] = np.ascontiguousarray(cdk[j])
            m["cache_dv"] = np.ascontiguousarray(cdv[j])
            m["cache_gk"] = np.ascontiguousarray(cgk[j])
            m["cache_gv"] = np.ascontiguousarray(cgv[j])
            m["st_f"] = np.ascontiguousarray(sf[j])
            m["st_b"] = np.ascontiguousarray(sbw[j])
            m["rope"] = rope_s
            m["maskb"] = mb_s
            m["keep"] = np.ones((128, 8), np.float32)
        m["cvecT"] = np.ascontiguousarray(cv.reshape(8, 128).T)
        maps.append(m)
    return maps


def kernel(**inputs):
    nc = bass.Bass("TRN2", target_bir_lowering=False)
    es = build_program(nc)
    maps = make_in_maps(inputs)
    res = run_bass_kernel_spmd(nc, maps, core_ids=list(range(8)))
    es.close()
    r = res.results
    B, SEQ = 32, 256
    y_prompt = np.zeros((B, SEQ, D), np.float32)
    y_sample = np.zeros((4, NT, D), np.float32)
    ndk = np.zeros((B, DEPTH, SEQ, 4, 64), np.float32)
    ndv = np.zeros((B, DEPTH, SEQ, 4, 64), np.float32)
    ngk = np.zeros((B, DEPTH, SEQ, 2, 64), np.float32)
    ngv = np.zeros((B, DEPTH, SEQ, 2, 64), np.float32)
    nsf = np.zeros((B, DEPTH, 8, 64, 64), np.float32)
    nsb = np.zeros((B, DEPTH, 8, 64, 64), np.float32)
    for core in range(4):
        o = r[core]
        sl = slice(8 * core, 8 * core + 8)
        y_prompt[sl] = o["y"].reshape(8, SEQ, D)
        ndk[sl] = o["ndk"].reshape(DEPTH, 8, SEQ, 4, 64).transpose(1, 0, 2, 3, 4)
        ndv[sl] = o["ndv"].reshape(DEPTH, 8, SEQ, 4, 64).transpose(1, 0, 2, 3, 4)
        ngk[sl] = o["ngk"].reshape(DEPTH, 8, SEQ, 2, 64).transpose(1, 0, 2, 3, 4)
        ngv[sl] = o["ngv"].reshape(DEPTH, 8, SEQ, 2, 64).transpose(1, 0, 2, 3, 4)
        nsf[sl] = o["nsf"].reshape(DEPTH, 8, 8, 64, 64).transpose(1, 0, 2, 3, 4)
        nsb[sl] = o["nsb"].reshape(DEPTH, 8, 8, 64, 64).transpose(1, 0, 2, 3, 4)
    for j in range(4):
        y_sample[j] = r[4 + j]["y"]
    return (y_prompt, y_sample, ndk, ndv, ngk, ngv, nsf, nsb)
```

```python
import numpy as np
import concourse.bass as bass
import concourse.mybir as mybir

F32 = mybir.dt.float32
F32R = mybir.dt.float32r
AF = mybir.ActivationFunctionType
ALU = mybir.AluOpType
AX = mybir.AxisListType

N_DMA_SEMS = 32
DEBUG_NAMES = None
BIN = 256
BIN_DRAM = 1 << 16
UNTRACKED = set()
SB_NAMES = set(['sb'] + ['ps%d' % i for i in range(8)])


def _prod(xs):
    r = 1
    for x in xs:
        r *= int(x)
    return r


def region(ap):
    t = ap.tensor
    name = t.name
    apl = [(int(s), int(c)) for s, c in ap.ap]
    off = int(ap.offset)
    sp = str(ap.space)
    if 'DRAM' in sp or 'HBM' in sp:
        ext = 1 + sum(abs(s) * (c - 1) for s, c in apl)
        return (name, 0, 1, off, off + ext)
    rowlen = _prod(list(t.shape)[1:])
    p0 = off // rowlen
    f0 = off % rowlen
    np_ = apl[0][1]
    ext = 1 + sum(abs(s) * (c - 1) for s, c in apl[1:])
    if 'PSUM' in sp:
        return (name, 0, 128, 0, rowlen)
    return (name, p0, p0 + np_, f0, f0 + ext)


class Op:
    __slots__ = ('eng', 'fn', 'waits', 'is_dma', 'chan', 'val', 'marked', 'clock', 'inc', 'desc')

    def __init__(self, eng, fn, is_dma):
        self.eng = eng
        self.fn = fn
        self.is_dma = is_dma
        self.waits = []
        self.marked = False
        self.chan = None
        self.val = 0
        self.clock = None


class Rec:
    __slots__ = ('p0', 'p1', 'f0', 'f1', 'w', 'op', 'dead')

    def __init__(self, r, w, op):
        self.p0, self.p1, self.f0, self.f1 = r[1], r[2], r[3], r[4]
        self.w = w
        self.op = op
        self.dead = False


class Sched:
    ENGS = ('pe', 'act', 'dve', 'pool', 'sp')

    def __init__(self, nc):
        self.nc = nc
        self.streams = {e: [] for e in self.ENGS}
        self.pos = {e: 0 for e in self.ENGS}
        self.clock = {e: {} for e in self.ENGS}
        self.bins = {}
        self.dma_rr = 0
        self.dma_rr2 = [0, 0]
        self.dma_last = [None] * N_DMA_SEMS
        self.dma_cnt = [0] * N_DMA_SEMS
        self.nops = 0

    def _overlaps(self, reg, want_reads):
        name, p0, p1, f0, f1 = reg
        out = []
        seen = set()
        bn = BIN if name in SB_NAMES else BIN_DRAM
        for b in range(f0 // bn, (f1 - 1) // bn + 1):
            lst = self.bins.get((name, b))
            if not lst:
                continue
            alive = []
            for rec in lst:
                if rec.dead:
                    continue
                alive.append(rec)
                if id(rec) in seen:
                    continue
                if rec.f0 < f1 and f0 < rec.f1 and rec.p0 < p1 and p0 < rec.p1:
                    if rec.w or want_reads:
                        seen.add(id(rec))
                        out.append(rec)
            if len(alive) != len(lst):
                self.bins[(name, b)] = alive
        return out

    def _add_rec(self, reg, w, op):
        name, p0, p1, f0, f1 = reg
        rec = Rec(reg, w, op)
        bn = BIN if name in SB_NAMES else BIN_DRAM
        for b in range(f0 // bn, (f1 - 1) // bn + 1):
            self.bins.setdefault((name, b), []).append(rec)

    def _need(self, op, dep):
        if dep is op:
            return
        e = op.eng
        ck = self.clock[e]
        if ck.get(dep.chan, 0) >= dep.val:
            return
        op.waits.append(dep)
        dep.marked = True
        for k, v in dep.clock.items():
            if ck.get(k, 0) < v:
                ck[k] = v
        if ck.get(dep.chan, 0) < dep.val:
            ck[dep.chan] = dep.val

    def add(self, eng, fn, reads=(), writes=(), is_dma=False, after=None):
        op = Op(eng, fn, is_dma)
        self.nops += 1
        if DEBUG_NAMES is not None:
            op.desc = (eng, [region(a) for a in reads], [region(a) for a in writes])
        rregs = [r_ for r_ in (region(a) for a in reads) if r_[0] not in UNTRACKED]
        wregs = [region(a) for a in writes]
        deps = []
        for r in rregs:
            for rec in self._overlaps(r, False):
                deps.append(rec.op)
        for w in wregs:
            for rec in self._overlaps(w, True):
                deps.append(rec.op)
        if is_dma:
            half = N_DMA_SEMS // 2
            qi = 0 if eng == 'sp' else 1
            k = qi * half + self.dma_rr2[qi]
            self.dma_rr2[qi] = (self.dma_rr2[qi] + 1) % half
            if self.dma_last[k] is not None:
                deps.append(self.dma_last[k])
            self.dma_cnt[k] += 1
            op.chan = 'dma%d' % k
            op.val = self.dma_cnt[k]
            self.dma_last[k] = op
            op.inc = k
        else:
            self.pos[eng] += 1
            op.chan = eng
            op.val = self.pos[eng]
            op.inc = None
        own = [d for d in deps if (not d.is_dma) and d.eng == eng]
        oth = [d for d in deps if d.is_dma or d.eng != eng]
        oth.sort(key=lambda d: -d.val)
        own.sort(key=lambda d: -d.val)
        for d in oth + own:
            if eng == 'pe' and d.eng == 'pe' and not d.is_dma and not is_dma:
                continue
            self._need(op, d)
        if after is not None:
            self._need(op, after)
        op.clock = dict(self.clock[eng])
        for w in wregs:
            name, p0, p1, f0, f1 = w
            for rec in self._overlaps(w, True):
                if rec.p0 >= p0 and rec.p1 <= p1 and rec.f0 >= f0 and rec.f1 <= f1:
                    rec.dead = True
            self._add_rec(w, True, op)
        for r in rregs:
            name, p0, p1, f0, f1 = r
            for rec in self._overlaps(r, True):
                if (not rec.w) and rec.op.eng == eng and rec.op.is_dma == is_dma and \
                        rec.p0 == p0 and rec.p1 == p1 and rec.f0 == f0 and rec.f1 == f1 and not is_dma:
                    rec.dead = True
            self._add_rec(r, False, op)
        self.streams[eng].append(op)
        return op

    def drain(self):
        op = Op('sp', None, False)
        op.chan = 'sp'
        op.val = 0
        for k in range(N_DMA_SEMS):
            if self.dma_last[k] is not None:
                op.waits.append(self.dma_last[k])
        op.clock = {}
        self.streams['sp'].append(op)

    def barrier(self):
        for e in self.ENGS:
            self.streams[e].append(None)

    def dma(self, eng, out, in_, extra_reads=(), extra_writes=(), **kw):
        def fn(e, out=out, in_=in_, kw=kw):
            return e.dma_start(out=out, in_=in_, **kw)
        return self.add(eng, fn, reads=[in_] + list(extra_reads), writes=[out] + list(extra_writes), is_dma=True)

    def emit(self):
        nc = self.nc
        fin = Op('sp', None, False)
        for k in range(N_DMA_SEMS):
            if self.dma_last[k] is not None:
                fin.waits.append(self.dma_last[k])
        for e in ('pe', 'act', 'dve', 'pool'):
            if self.streams[e]:
                last = None
                for o in reversed(self.streams[e]):
                    if o is not None and not o.is_dma:
                        last = o
                        break
                if last is not None:
                    last.marked = True
                    fin.waits.append(last)
        self.streams['sp'].append(fin)
        cnt_of = {}
        for e in self.ENGS:
            c = 0
            for o in self.streams[e]:
                if o is None or o.is_dma or o.fn is None:
                    continue
                if o.marked:
                    c += 1
                    cnt_of[id(o)] = c
        segs = {e: [[]] for e in self.ENGS}
        for e in self.ENGS:
            for o in self.streams[e]:
                if o is None:
                    segs[e].append([])
                else:
                    segs[e][-1].append(o)
        nseg = max(len(segs[e]) for e in self.ENGS)
        import contextlib
        with contextlib.ExitStack() as st:
            esem = {e: st.enter_context(nc.semaphore('s_' + e)) for e in self.ENGS}
            dsem = [st.enter_context(nc.semaphore('s_dma%d' % k)) for k in range(N_DMA_SEMS)]

            def make(ename, ops):
                def body(e):
                    for o in ops:
                        ws = {}
                        for d in o.waits:
                            if d.is_dma:
                                key = ('d', d.inc)
                                v = d.val * 16
                            else:
                                key = ('e', d.eng)
                                v = cnt_of[id(d)]
                            if ws.get(key, 0) < v:
                                ws[key] = v
                        for key, v in ws.items():
                            sem = dsem[key[1]] if key[0] == 'd' else esem[key[1]]
                            e.wait_ge(sem, v)
                        if o.fn is None:
                            continue
                        ins = o.fn(e)
                        if DEBUG_NAMES is not None:
                            try:
                                DEBUG_NAMES[str(ins.ins.name)] = getattr(o, 'desc', None)
                            except Exception:
                                pass
                        if o.is_dma:
                            ins.then_inc(dsem[o.inc], 16)
                        elif o.marked:
                            ins.then_inc(esem[ename], 1)
                return body
            for si in range(nseg):
                with (nc.Block(no_gpsimd_drain=True) if si < nseg - 1 else nc.Block()) as blk:
                    engobj = {'pe': blk.tensor, 'act': blk.scalar, 'dve': blk.vector, 'pool': blk.gpsimd, 'sp': blk.sync}
                    for ename in self.ENGS:
                        ops = segs[ename][si] if si < len(segs[ename]) else []
                        engobj[ename](make(ename, ops))


import math
import contextlib
from concourse.bass_utils import run_bass_kernel_spmd

D = 1024
NT = 2048
NTB = 16
DEPTH = 2
PAST = 512
NKB = 20
DFF = 2816
NFF = 22
INW = 2576
EPS = 1e-5
ALPHA = (2 * DEPTH) ** 0.25
NEG = -30000.0
SB_COLS = 52000


def build_program(nc, dbg=False, stop_after=None, nlayers=DEPTH):
    S = Sched(nc)
    es = contextlib.ExitStack()

    def din(name, shape):
        UNTRACKED.add(name)
        return nc.dram_tensor(name, list(shape), F32, kind="ExternalInput").ap()

    def dout(name, shape):
        return nc.dram_tensor(name, list(shape), F32, kind="ExternalOutput").ap()

    def dscr(name, shape):
        return nc.dram_tensor(name, list(shape), F32, kind=("ExternalOutput" if dbg else "Internal")).ap()

    x_d = din("x", [NT, D])
    cvt_d = din("cvecT", [128, 8])
    cdk_d = din("cache_dk", [DEPTH, PAST, 256])
    cdv_d = din("cache_dv", [DEPTH, PAST, 256])
    cgk_d = din("cache_gk", [DEPTH, PAST, 128])
    cgv_d = din("cache_gv", [DEPTH, PAST, 128])
    stf_d = din("st_f", [DEPTH, 8, 64, 64])
    stb_d = din("st_b", [DEPTH, 8, 64, 64])
    wada_d = din("w_ada", [DEPTH, D, 6 * D])
    bada_d = din("b_ada", [DEPTH, 6 * D])
    win_d = din("w_in", [DEPTH, D, INW])
    wout_d = din("w_out", [DEPTH, D, D])
    lam_d = din("diff_lambda", [DEPTH, 128])
    subg_d = din("diff_subln_g", [DEPTH, 64])
    qkg_d = din("qk_norm_g", [DEPTH, 128])
    convw_d = din("convw", [DEPTH, 128, 30])
    convb_d = din("convb", [DEPTH, 128, 6])
    alog_d = din("ssd_A_log", [DEPTH, 16])
    dtb_d = din("ssd_dt_bias", [DEPTH, 16])
    dd_d = din("ssd_D", [DEPTH, 8])
    ssdg_d = din("ssd_norm_g", [DEPTH, 512])
    lng_d = din("ln_g", [DEPTH, 2, D])
    lnb_d = din("ln_b", [DEPTH, 2, D])
    wfi_d = din("w_ffn_in", [DEPTH, D, 2 * DFF])
    wfo_d = din("w_ffn_out", [DEPTH, DFF, D])
    c128_d = din("c128", [128, 6 * 128])
    rope_d = din("rope", [128, 16 * 96])
    maskb_d = din("maskb", [128, 160])
    keep_d = din("keep", [128, 8])

    y_d = dout("y", [NT, D])
    ndk_d = dout("ndk", [DEPTH, NT, 256])
    ndv_d = dout("ndv", [DEPTH, NT, 256])
    ngk_d = dout("ngk", [DEPTH, NT, 128])
    ngv_d = dout("ngv", [DEPTH, NT, 128])
    nsf_d = dout("nsf", [DEPTH, 8, 8, 64, 64])
    nsb_d = dout("nsb", [DEPTH, 8, 8, 64, 64])

    qts_d = dscr("qts", [512, NT])
    mixt_d = dscr("mixt", [1024, NT])
    xbcs_d = dscr("xbcs", [768, NT])
    zs_d = dscr("zs", [NT, 512])
    if dbg:
        dbg_modf = dout("dbg_modf", [128, 96])
        dbg_dt = dout("dbg_dt", [128, 256])
        dbg_x1 = dout("dbg_x1", [NT, D])

    PERS = 18720
    sb = es.enter_context(nc.sbuf_tensor("sb", [128, PERS], F32))
    ps = [es.enter_context(nc.psum_tensor("ps%d" % i, [128, 512], F32)) for i in range(8)]

    class Alloc:
        def __init__(self, t=None, limit=0):
            self.bind(t, limit)

        def bind(self, t, limit):
            self.t = t
            self.cur = 0
            self.limit = limit

        def get(self, n):
            o = self.cur
            self.cur += n
            assert self.cur <= self.limit, ("SBUF overflow", self.cur, self.limit)
            return self.t[:, o:o + n]

        def reset(self):
            self.cur = 0

    pa = Alloc(sb, PERS)
    X = pa.get(NTB * D).rearrange("p (a b) -> p a b", a=NTB)
    C128 = pa.get(768)
    IDENT = C128[:, 0:128]
    TL = C128[:, 128:256]
    TU = C128[:, 256:384]
    NMF = C128[:, 384:512]
    NMB = C128[:, 512:640]
    ONES = C128[:, 640:768]
    MASKB = pa.get(160)
    KEEP = pa.get(8)
    MODF = pa.get(96).rearrange("p (l c) -> p l c", l=2)
    S8 = pa.get(8)
    DT = pa.get(256).rearrange("p (a b) -> p a b", a=NTB)
    CONVW = pa.get(30).rearrange("p (j k) -> p j k", j=6)
    CONVB = pa.get(6)
    pa.get(4)
    ABC = pa.get(16)
    DTB = pa.get(16)
    DBC = pa.get(8)
    LAMS = pa.get(16)
    SSDG = pa.get(512)
    QKG = pa.get(128)
    SUBG = pa.get(64)
    LAMB = pa.get(128)
    tmpl = pa.get(64)
    ph = Alloc()
    pr = Alloc()
    phase_ctx = [None]
    FREE = SB_COLS - PERS

    def close_phase():
        if phase_ctx[0] is not None:
            S.drain()
            S.barrier()
            phase_ctx[0].close()
            phase_ctx[0] = None

    def open_phase(tag, n_plain, n_r):
        close_phase()
        assert n_plain + n_r <= FREE, (tag, n_plain, n_r, FREE)
        ctx = contextlib.ExitStack()
        tp_ = ctx.enter_context(nc.sbuf_tensor("pp_" + tag, [128, n_plain], F32))
        SB_NAMES.add("pp_" + tag)
        ph.bind(tp_, n_plain)
        if n_r:
            tr_ = ctx.enter_context(nc.sbuf_tensor("pr_" + tag, [128, n_r], F32))
            SB_NAMES.add("pr_" + tag)
            pr.bind(tr_, n_r)
        phase_ctx[0] = ctx

    def aps(*xs):
        return [a for a in xs if hasattr(a, 'tensor')]

    import os as _os0
    USE_R = _os0.environ.get('KDBG_R', '1') == '1'

    def R(ap):
        return ap.bitcast(F32R) if USE_R else ap

    def mm(out, lhsT, rhs, start=True, stop=True, tp=None, after=None, r=False):
        if r:
            lhsT = R(lhsT)
            rhs = R(rhs)

        def fn(e):
            if tp is not None:
                return e.matmul(out, lhsT=lhsT, rhs=rhs, start=start, stop=stop, tile_position=tp, skip_group_check=True)
            return e.matmul(out, lhsT=lhsT, rhs=rhs, start=start, stop=stop, skip_group_check=True)
        return S.add('pe', fn, reads=[lhsT, rhs], writes=[out], after=after)

    def tr(out, in_, ident, after=None):
        return S.add('pe', lambda e: e.transpose(out=out, in_=in_, identity=ident), reads=[in_, ident], writes=[out], after=after)

    def act(out, in_, func, bias=None, scale=None, eng='act'):
        kw = {}
        if bias is not None:
            kw['bias'] = bias
        if scale is not None:
            kw['scale'] = scale
        return S.add('act', lambda e: e.activation(out=out, in_=in_, func=func, **kw),
                     reads=aps(in_, bias, scale), writes=[out])

    def ts(eng, out, in0, s1, s2, op0, op1=None):
        def fn(e):
            if op1 is None:
                return e.tensor_scalar(out=out, in0=in0, scalar1=s1, scalar2=None, op0=op0)
            return e.tensor_scalar(out=out, in0=in0, scalar1=s1, scalar2=s2, op0=op0, op1=op1)
        return S.add(eng, fn, reads=aps(in0, s1, s2), writes=[out])

    def tt(eng, out, in0, in1, op):
        return S.add(eng, lambda e: e.tensor_tensor(out=out, in0=in0, in1=in1, op=op), reads=[in0, in1], writes=[out])

    def stt(out, in0, scalar, in1, op0, op1):
        return S.add('dve', lambda e: e.scalar_tensor_tensor(out=out, in0=in0, scalar=scalar, in1=in1, op0=op0, op1=op1),
                     reads=aps(in0, scalar, in1), writes=[out])

    def cp(eng, out, in_):
        if eng == 'act':
            return S.add('act', lambda e: e.copy(out=out, in_=in_), reads=[in_], writes=[out])
        return S.add(eng, lambda e: e.tensor_copy(out=out, in_=in_), reads=[in_], writes=[out])

    def red(out, in_, op=ALU.add):
        return S.add('dve', lambda e: e.tensor_reduce(out=out, in_=in_, axis=AX.X, op=op), reads=[in_], writes=[out])

    def recip(out, in_):
        return S.add('dve', lambda e: e.reciprocal(out=out, in_=in_), reads=[in_], writes=[out])

    def mset(eng, ap, val):
        return S.add(eng, lambda e: e.memset(ap, val), reads=[], writes=[ap])

    def dma(out, in_, eng=None, cast=False):
        is_store = 'DRAM' in str(out.space)
        if cast and USE_R:
            return S.dma('pool', R(out), in_)
        return S.dma('pool' if is_store else 'sp', out, in_)

    def bc(ap, shape):
        return ap.to_broadcast(list(shape))

    class Pref:
        def __init__(self, keys, ahead, issue):
            self.keys = keys
            self.ahead = ahead
            self.issue = issue
            self.n = 0
            self.buf = {}

        def get(self, idx):
            while self.n < len(self.keys) and self.n <= idx + self.ahead:
                self.buf[self.n] = self.issue(self.keys[self.n])
                self.n += 1
            return self.buf.pop(idx)

    rr = {}

    def rot(key, n):
        v = rr.get(key, 0)
        rr[key] = (v + 1) % n
        return v

    def rstd_from(out, ssum, n, tmp):
        act(tmp, ssum, AF.Sqrt, bias=EPS, scale=1.0 / n)
        recip(out, tmp)

    dma(C128, c128_d)
    dma(MASKB, maskb_d)
    dma(KEEP, keep_d)
    for tb in range(NTB):
        dma(X[:, tb, :], x_d[tb * 128:(tb + 1) * 128, :])
    mset('pool', MODF.rearrange("p l c -> p (l c)"), 0.0)
    mset('pool', DT.rearrange("p a b -> p (a b)"), 0.0)
    dma(S8, cvt_d)
    act(S8, S8, AF.Silu)

    done = [False]

    def finish():
        if phase_ctx[0] is not None:
            phase_ctx[0].close()
            phase_ctx[0] = None
        es.enter_context(nc.sbuf_tensor("filler", [128, FREE], F32))
        S.emit()
        done[0] = True

    for l in range(nlayers):
        lam_init = 0.8 - 0.6 * math.exp(-0.3 * l)
        dma(CONVW.rearrange("p j k -> p (j k)"), convw_d[l])
        dma(CONVB, convb_d[l])
        dma(ABC, alog_d[l:l + 1, :].partition_broadcast(128))
        dma(DTB, dtb_d[l:l + 1, :].partition_broadcast(128))
        dma(DBC, dd_d[l:l + 1, :].partition_broadcast(128))
        dma(SSDG, ssdg_d[l:l + 1, :].partition_broadcast(128))
        dma(QKG, qkg_d[l:l + 1, :].partition_broadcast(128))
        dma(SUBG, subg_d[l:l + 1, :].partition_broadcast(128))
        dma(LAMB, lam_d[l:l + 1, :].partition_broadcast(128))
        act(ABC, ABC, AF.Exp)
        ts('dve', ABC, ABC, -1.0, None, ALU.mult)
        ts('dve', SUBG, SUBG, 1.0 - lam_init, None, ALU.mult)
        tt('dve', tmpl[:, 0:32], LAMB[:, 0:32], LAMB[:, 32:64], ALU.mult)
        tt('dve', tmpl[:, 32:64], LAMB[:, 64:96], LAMB[:, 96:128], ALU.mult)
        red(LAMS[:, 0:2], tmpl.rearrange("p (a b) -> p a b", a=2))
        act(LAMS[:, 2:4], LAMS[:, 0:2], AF.Exp)
        tt('dve', LAMS[:, 4:5], LAMS[:, 3:4], LAMS[:, 2:3], ALU.subtract)
        ts('dve', LAMS[:, 4:5], LAMS[:, 4:5], -lam_init, None, ALU.add)
        NEGLAM = LAMS[:, 4:5]

        open_phase('M%d' % l, 10304, 0)
        WA = [ph.get(4096).rearrange("p (a b) -> p a b", a=8) for _ in range(2)]
        ACC = [ph.get(512) for _ in range(2)]
        BST = [ph.get(512) for _ in range(2)]
        for s in range(12):
            b = s % 2
            wsrc = wada_d[l][:, s * 512:(s + 1) * 512].rearrange("(kc p) n -> p kc n", p=128)
            dma(WA[b], wsrc)
            dma(BST[b][0:1, :], bada_d[l:l + 1, s * 512:(s + 1) * 512])
            ts('dve', ACC[b], WA[b][:, 0, :], S8[:, 0:1], None, ALU.mult)
            for kc in range(1, 8):
                stt(ACC[b], WA[b][:, kc, :], S8[:, kc:kc + 1], ACC[b], ALU.mult, ALU.add)
            tt('pool', ACC[b][0:1, :], ACC[b][0:1, :], BST[b][0:1, :], ALU.add)
            bk = ps[rot('m', 2)]
            for j in range(4):
                mm(bk[:, 2 * j:2 * j + 2], ACC[b][:, j * 128:(j + 1) * 128], ONES[:, 0:2])
            m = s // 2
            hf = s % 2
            cp('dve', MODF[:, l, m * 8 + hf * 4:m * 8 + hf * 4 + 4],
               bk[:, 0:8].rearrange("p (j t) -> p j t", t=2)[:, :, 0])
        ts('dve', MODF[:, l, 8:16], MODF[:, l, 8:16], 1.0, None, ALU.add)
        ts('dve', MODF[:, l, 32:40], MODF[:, l, 32:40], 1.0, None, ALU.add)
        if dbg and l == nlayers - 1:
            dma(dbg_modf, MODF.rearrange("p l c -> p (l c)"))
        if stop_after == 'M' and l == nlayers - 1:
            finish()
            break

        open_phase('PA%d' % l, 6944, 26200)
        KTD = pr.get(2 * 2560).rearrange("p (a b) -> p a b", a=2)
        KTG = pr.get(2560)
        VW = 68 if USE_R else 65
        VD = pr.get(NKB * 4 * VW).rearrange("p (k h e) -> p k h e", k=NKB, h=4)
        VG = pr.get(NKB * 2 * VW).rearrange("p (k h e) -> p k h e", k=NKB, h=2)
        r_mark = pr.cur
        if stop_after == 'M2' and l == nlayers - 1:
            mset('dve', KTG, 1.0)
            mset('pool', VG[:, :, :, 64:65], 1.0)
            _tq = ph.get(512)
            cp('act', _tq, KTG[:, 0:512])
            dma(zs_d[0:128, :], _tq)
            import os as _os3
            if _os3.environ.get('KDBG_M3') == '1':
                _rp = ph.get(384).rearrange("p (a b) -> p a b", a=4)
                dma(_rp, rope_d[:, 0:384].rearrange("p (a b) -> p a b", a=4))
                _t2 = ph.get(384)
                cp('dve', _t2, _rp.rearrange("p a b -> p (a b)"))
                dma(zs_d[128:256, 0:384], _t2)
            if _os3.environ.get('KDBG_M4') == '1':
                _h = pr.get(512)
                _w = pr.get(512)
                cp('dve', R(_h), X[:, 0, 0:512])
                cp('dve', R(_w), X[:, 1, 0:512])
                tr(ps[6][:, 0:128], _h[:, 0:128], IDENT)
                mm(ps[5][:, 0:512], _h[:, 0:128], _w, r=True)
                _t3 = ph.get(512)
                cp('dve', _t3, ps[5][:, 0:512])
                dma(zs_d[256:384, :], _t3)
                cp('dve', _t3[:, 0:128], ps[6][:, 0:128])
                dma(zs_d[384:512, 0:128], _t3[:, 0:128])
            finish()
            break
        HTg = pr.get(4096).rearrange("p (a b) -> p a b", a=8)
        WS = [pr.get(2048).rearrange("p (a b) -> p a b", a=8) for _ in range(3)]
        ROPE = ph.get(4 * 96).rearrange("p (a b) -> p a b", a=4)
        TOK = [ph.get(256) for _ in range(2)]
        NRM = [ph.get(256) for _ in range(2)]
        ROT = [ph.get(256) for _ in range(2)]
        RT = [ph.get(128) for _ in range(4)]
        QST = [ph.get(1024).rearrange("p (a b) -> p a b", a=2) for _ in range(2)]
        ZST = [ph.get(256) for _ in range(2)]
        XST = [ph.get(512) for _ in range(2)]
        CK = QST[0].rearrange("p a (b c) -> p (a b) c", b=2)
        CKG = QST[1][:, 0, :].rearrange("p (a b) -> p a b", a=4)
        SM = ph.get(64)
        if not USE_R:
            for vv in (VD, VG):
                mset('pool', vv[:, :, :, 64:65], 1.0)
        else:
            for kb in range(NKB):
                for (vv, nh_) in ((VD, 4), (VG, 2)):
                    ts('dve', R(vv[:, kb, :, 64]), ONES[:, 0:nh_], 1.0, None, ALU.mult)
                    ts('dve', R(vv[:, kb, :, 65:VW]), ONES[:, 0:nh_ * (VW - 65)].rearrange("p (h e) -> p h e", h=nh_), 0.0, None, ALU.mult)
        for kb in range(4):
            dma(VD[:, 16 + kb, :, 0:64], cdv_d[l, kb * 128:(kb + 1) * 128, :].rearrange("p (h e) -> p h e", h=4), cast=True)
            dma(VG[:, 16 + kb, :, 0:64], cgv_d[l, kb * 128:(kb + 1) * 128, :].rearrange("p (h e) -> p h e", h=2), cast=True)
        dma(CK, cdk_d[l].rearrange("(kb p) c -> p kb c", p=128))
        dma(CKG, cgk_d[l].rearrange("(kb p) c -> p kb c", p=128))
        for kb in range(4):
            bk = ps[6 + rot('p1t', 2)]
            for t2 in range(2):
                tr(bk[:, t2 * 128:(t2 + 1) * 128], CK[:, kb, t2 * 128:(t2 + 1) * 128], IDENT)
            tr(bk[:, 256:384], CKG[:, kb, :], IDENT)
            cp('dve', R(KTD[:, :, 2048 + kb * 128:2048 + (kb + 1) * 128]), bk[:, 0:256].rearrange("p (a b) -> p a b", a=2))
            cp('dve', R(KTG[:, 2048 + kb * 128:2048 + (kb + 1) * 128]), bk[:, 256:384])

        if stop_after == 'P0' and l == nlayers - 1:
            finish()
            break
        SQT = ph.get(256)

        def rope_apply(dv, sv, cos, sin, nh, npair):
            cb = bc(cos.unsqueeze(1), [128, nh, npair])
            sbb = bc(sin.unsqueeze(1), [128, nh, npair])
            n = nh * npair
            ta = RT[0][:, 0:n].rearrange("p (h i) -> p h i", h=nh)
            tb_ = RT[1][:, 0:n].rearrange("p (h i) -> p h i", h=nh)
            tc = RT[2][:, 0:n].rearrange("p (h i) -> p h i", h=nh)
            td = RT[3][:, 0:n].rearrange("p (h i) -> p h i", h=nh)
            tt('dve', ta, sv[:, :, :, 0], cb, ALU.mult)
            tt('pool', tb_, sv[:, :, :, 1], sbb, ALU.mult)
            tt('dve', dv[:, :, :, 0], ta, tb_, ALU.subtract)
            tt('pool', tc, sv[:, :, :, 0], sbb, ALU.mult)
            tt('dve', td, sv[:, :, :, 1], cb, ALU.mult)
            tt('pool', dv[:, :, :, 1], tc, td, ALU.add)

        def v4(ap, nh):
            return ap.rearrange("p (h i t) -> p h i t", h=nh, t=2)

        def rmsn(dst, src, nh, gain):
            sv = src.rearrange("p (h e) -> p h e", h=nh)
            dv = dst.rearrange("p (h e) -> p h e", h=nh)
            sqv = SQT[:, 0:nh * 64].rearrange("p (h e) -> p h e", h=nh)
            tt('dve', sqv, sv, sv, ALU.mult)
            red(SM[:, 0:nh], sqv)
            rstd_from(SM[:, 8:8 + nh], SM[:, 0:nh], 64, SM[:, 16:16 + nh])
            tt('dve', dv, sv, bc(SM[:, 8:8 + nh].unsqueeze(2), [128, nh, 64]), ALU.mult)
            tt('dve', dv, dv, bc(gain.unsqueeze(1), [128, nh, 64]), ALU.mult)

        slabs = [(c0, min(256, INW - c0)) for c0 in range(0, INW, 256)]

        def issue_ws(key):
            c0_, ncol_ = slabs[key[1]]
            wb_ = WS[rot('ws', 3)]
            dma(wb_[:, :, 0:ncol_], win_d[l][:, c0_:c0_ + ncol_].rearrange("(kc p) n -> p kc n", p=128), cast=True)
            return wb_
        ws_pref = Pref([(g_, si_) for g_ in range(4) for si_ in range(len(slabs))], 2, issue_ws)
        for g in range(4):
            dma(ROPE, rope_d[:, g * 384:(g + 1) * 384].rearrange("p (a b) -> p a b", a=4))
            for dc in range(8):
                bk = ps[rot('p1h', 2)]
                for tb in range(4):
                    tr(bk[:, tb * 128:(tb + 1) * 128], X[:, 4 * g + tb, dc * 128:(dc + 1) * 128], IDENT)
                act(R(HTg[:, dc, :]), bk[:, :], AF.Identity, bias=MODF[:, l, dc:dc + 1], scale=MODF[:, l, 8 + dc:9 + dc])
            for si, (c0, ncol) in enumerate(slabs):
                wb = ws_pref.get(g * len(slabs) + si)
                if 7 <= si <= 9:
                    for t2 in range(2):
                        bk = ps[2 + rot('p1p', 4)]
                        for dc in range(8):
                            mm(bk[:, :], wb[:, dc, t2 * 128:(t2 + 1) * 128], HTg[:, dc, :], start=(dc == 0), stop=(dc == 7), r=True)
                        xs = XST[rot('xst', 2)]
                        cp('act', xs, bk[:, :])
                        j = (si - 7) * 2 + t2
                        dma(xbcs_d[j * 128:(j + 1) * 128, g * 512:(g + 1) * 512], xs)
                    continue
                qs = None
                for tb in range(4):
                    tbg = 4 * g + tb
                    bk = ps[2 + rot('p1p', 4)]
                    for dc in range(8):
                        mm(bk[:, 0:ncol], HTg[:, dc, tb * 128:(tb + 1) * 128], wb[:, dc, 0:ncol], start=(dc == 0), stop=(dc == 7), r=True)
                    cosd = ROPE[:, tb, 0:16]
                    sind = ROPE[:, tb, 16:32]
                    cosg = ROPE[:, tb, 32:64]
                    sing = ROPE[:, tb, 64:96]
                    if si == 0 or si == 1:
                        tk = TOK[rot('tok', 2)]
                        ro = ROT[rot('rot', 2)]
                        cp('act', tk, bk[:, 0:256])
                        if si == 1:
                            dma(ndk_d[l, tbg * 128:(tbg + 1) * 128, :], tk, eng='pool')
                        rope_apply(v4(ro, 8), v4(tk, 8), cosd, sind, 8, 16)
                        b2 = ps[6 + rot('p1t', 2)]
                        for t2 in range(2):
                            tr(b2[:, t2 * 128:(t2 + 1) * 128], ro[:, t2 * 128:(t2 + 1) * 128], IDENT)
                        if si == 0:
                            if tb == 0:
                                qs = QST[rot('qst', 2)]
                            cp('dve', R(qs[:, :, tb * 128:(tb + 1) * 128]), b2[:, 0:256].rearrange("p (a b) -> p a b", a=2))
                            if tb == 3:
                                dma(qts_d[0:256, g * 512:(g + 1) * 512].rearrange("(a p) t -> p a t", p=128), qs)
                        else:
                            cp('dve', R(KTD[:, :, tbg * 128:(tbg + 1) * 128]), b2[:, 0:256].rearrange("p (a b) -> p a b", a=2))
                    elif si == 2:
                        tk = TOK[rot('tok', 2)]
                        cp('act', tk, bk[:, 0:256])
                        cp('dve', R(VD[:, tbg, :, 0:64]), tk.rearrange("p (h e) -> p h e", h=4))
                        dma(ndv_d[l, tbg * 128:(tbg + 1) * 128, :], tk)
                    elif si == 3:
                        tk = TOK[rot('tok', 2)]
                        nr = NRM[rot('nrm', 2)]
                        ro = ROT[rot('rot', 2)]
                        cp('act', tk, bk[:, 0:256])
                        rmsn(nr, tk, 4, QKG[:, 0:64])
                        for a_ in range(2):
                            dvw = ro.rearrange("p (b a x) -> p a b x", a=2, b=2)[:, a_].rearrange("p b (i t) -> p b i t", t=2)
                            svw = nr[:, a_ * 128:(a_ + 1) * 128].rearrange("p (b i t) -> p b i t", b=2, t=2)
                            rope_apply(dvw, svw, cosg, sing, 2, 32)
                        b2 = ps[6 + rot('p1t', 2)]
                        for t2 in range(2):
                            tr(b2[:, t2 * 128:(t2 + 1) * 128], ro[:, t2 * 128:(t2 + 1) * 128], IDENT)
                        if tb == 0:
                            qs = QST[rot('qst', 2)]
                        cp('dve', R(qs[:, :, tb * 128:(tb + 1) * 128]), b2[:, 0:256].rearrange("p (a b) -> p a b", a=2))
                        if tb == 3:
                            dma(qts_d[256:512, g * 512:(g + 1) * 512].rearrange("(a p) t -> p a t", p=128), qs)
                    elif si == 4:
                        tk = TOK[rot('tok', 2)]
                        nr = NRM[rot('nrm', 2)]
                        ro = ROT[rot('rot', 2)]
                        cp('act', tk, bk[:, 0:256])
                        cp('dve', R(VG[:, tbg, :, 0:64]), tk[:, 128:256].rearrange("p (h e) -> p h e", h=2))
                        dma(ngv_d[l, tbg * 128:(tbg + 1) * 128, :], tk[:, 128:256])
                        rmsn(nr[:, 0:128], tk[:, 0:128], 2, QKG[:, 64:128])
                        dma(ngk_d[l, tbg * 128:(tbg + 1) * 128, :], nr[:, 0:128], eng='pool')
                        rope_apply(v4(ro[:, 0:128], 2), v4(nr[:, 0:128], 2), cosg, sing, 2, 32)
                        b2 = ps[6 + rot('p1t', 2)]
                        tr(b2[:, 0:128], ro[:, 0:128], IDENT)
                        cp('dve', R(KTG[:, tbg * 128:(tbg + 1) * 128]), b2[:, 0:128])
                    elif si == 5 or si == 6:
                        zt = ZST[rot('zst', 2)]
                        act(zt, bk[:, 0:256], AF.Silu)
                        dma(zs_d[tbg * 128:(tbg + 1) * 128, (si - 5) * 256:(si - 4) * 256], zt)
                    elif si == 10:
                        tt('dve', DT[:, tbg, :], bk[:, 0:16], DTB, ALU.add)
                        act(DT[:, tbg, :], DT[:, tbg, :], AF.Exp)
                        act(DT[:, tbg, :], DT[:, tbg, :], AF.Ln, bias=1.0)
        if dbg and l == nlayers - 1:
            dma(dbg_dt, DT.rearrange("p a b -> p (a b)"))
        if stop_after == 'P' and l == nlayers - 1:
            finish()
            break

        pr.cur = r_mark
        ph.reset()
        QG = [pr.get(2048).rearrange("p (a b) -> p a b", a=4) for _ in range(2)]
        PT = [pr.get(512) for _ in range(6)]
        OTOK = ph.get(2048).rearrange("p (a b) -> p a b", a=4)
        T0 = ph.get(256).rearrange("p (a b) -> p a b", a=4)
        T1 = ph.get(256).rearrange("p (a b) -> p a b", a=4)
        T2 = ph.get(256).rearrange("p (a b) -> p a b", a=4)
        RR = ph.get(32)
        MAT = [ph.get(2048).rearrange("p (a b) -> p a b", a=4) for _ in range(2)]
        sc_d = 32 ** -0.5
        sc_g = 64 ** -0.5
        def issue_qg(qg_):
            dma(QG[qg_ % 2], qts_d[:, qg_ * 512:(qg_ + 1) * 512].rearrange("(a p) t -> p a t", p=128), cast=True)
            return QG[qg_ % 2]
        qg_pref = Pref(list(range(4)), 1, issue_qg)
        for qg in range(4):
            qb_ = qg_pref.get(qg)

            def head_pass(kT, kbase, kn, qT, vfn, scale, obank):
                for kb in range(NKB):
                    sbk = ps[rot('as', 3)]
                    tp = (kbase, 0) if kn == 32 else None
                    mm(sbk[:, :], kT(kb), qT, tp=tp, r=True)
                    pt = PT[rot('pt', 6)]
                    for hf in range(2):
                        act(R(pt[:, hf * 256:(hf + 1) * 256]), sbk[:, hf * 256:(hf + 1) * 256], AF.Exp,
                            bias=MASKB[:, kb * 8 + qg * 2 + hf:kb * 8 + qg * 2 + hf + 1], scale=scale)
                    for qb in range(4):
                        mm(obank[:, qb * VW:(qb + 1) * VW], pt[:, qb * 128:(qb + 1) * 128], vfn(kb),
                           start=(kb == 0 and qb == 0), stop=(kb == NKB - 1 and qb == 3), r=True)

            for h in range(4):
                ob = [ps[3 + 2 * (h % 2)], ps[4 + 2 * (h % 2)]]
                for m_ in range(2):
                    hm = 2 * h + m_
                    tile = hm // 4
                    pb = 32 * (hm % 4)
                    head_pass(lambda kb, tile=tile, pb=pb: KTD[pb:pb + 32, tile, kb * 128:(kb + 1) * 128], pb, 32,
                              qb_[pb:pb + 32, tile, :], lambda kb, h=h: VD[:, kb, h, :], sc_d, ob[m_])
                o0 = ob[0][:, 0:4 * VW].rearrange("p (q e) -> p q e", q=4)
                o1 = ob[1][:, 0:4 * VW].rearrange("p (q e) -> p q e", q=4)
                recip(RR[:, 0:4], o0[:, :, 64])
                recip(RR[:, 4:8], o1[:, :, 64])
                ts('dve', RR[:, 4:8], RR[:, 4:8], NEGLAM, None, ALU.mult)
                tt('dve', T0, o0[:, :, 0:64], bc(RR[:, 0:4].unsqueeze(2), [128, 4, 64]), ALU.mult)
                tt('dve', T1, o1[:, :, 0:64], bc(RR[:, 4:8].unsqueeze(2), [128, 4, 64]), ALU.mult)
                tt('pool', T0, T0, T1, ALU.add)
                tt('pool', T2, T0, T0, ALU.mult)
                red(RR[:, 8:12], T2)
                rstd_from(RR[:, 12:16], RR[:, 8:12], 64, RR[:, 16:20])
                tt('dve', T0, T0, bc(RR[:, 12:16].unsqueeze(2), [128, 4, 64]), ALU.mult)
                tt('dve', OTOK[:, :, h * 64:(h + 1) * 64], T0, bc(SUBG.unsqueeze(1), [128, 4, 64]), ALU.mult)
            for qh in range(4):
                a_ = qh // 2
                b_ = qh % 2
                ob = ps[3 + (qh % 2) * 2]
                pb = 64 * a_
                head_pass(lambda kb, pb=pb: KTG[pb:pb + 64, kb * 128:(kb + 1) * 128], pb, 64,
                          qb_[pb:pb + 64, 2 + b_, :], lambda kb, a_=a_: VG[:, kb, a_, :], sc_g, ob)
                o0 = ob[:, 0:4 * VW].rearrange("p (q e) -> p q e", q=4)
                recip(RR[:, 20:24], o0[:, :, 64])
                tt('dve', OTOK[:, :, 256 + qh * 64:256 + (qh + 1) * 64], o0[:, :, 0:64],
                   bc(RR[:, 20:24].unsqueeze(2), [128, 4, 64]), ALU.mult)
            mt = MAT[qg % 2]
            for qb in range(4):
                bk = ps[7]
                for c in range(4):
                    tr(bk[:, c * 128:(c + 1) * 128], OTOK[:, qb, c * 128:(c + 1) * 128], IDENT)
                cp('dve', mt[:, :, qb * 128:(qb + 1) * 128], bk[:, :].rearrange("p (a b) -> p a b", a=4))
            dma(mixt_d[0:512, qg * 512:(qg + 1) * 512].rearrange("(a p) t -> p a t", p=128), mt)
        if stop_after == 'A' and l == nlayers - 1:
            finish()
            break

        open_phase('S%d' % l, 30800, 0)
        XH = ph.get(8192).rearrange("p (c e) -> p c e", c=16)
        BT = ph.get(2048)
        CT = ph.get(2048)
        s_mark = ph.cur
        PB = [ph.get(8 * 260).rearrange("p (s i) -> p s i", s=8) for _ in range(2)]
        CA = [ph.get(2048) for _ in range(2)]
        for b in range(2):
            mset('pool', PB[b][:, 0, 0:2], 0.0)
            mset('pool', PB[b][:, 7, 258:260], 0.0)
        for j in range(6):
            pbuf = PB[j % 2]
            ca = CA[j % 2]
            dma(pbuf[:, :, 2:258], xbcs_d[j * 128:(j + 1) * 128, :].rearrange("p (s i) -> p s i", s=8))
            ts('dve', pbuf[:, 1:8, 0:2], pbuf[:, 0:7, 256:258], KEEP[:, 0:1], None, ALU.mult)
            ts('dve', pbuf[:, 0:7, 258:260], pbuf[:, 1:8, 2:4], KEEP[:, 0:1], None, ALU.mult)
            cav = ca.rearrange("p (s i) -> p s i", s=8)
            ts('dve', cav, pbuf[:, :, 0:256], CONVW[:, j, 0:1], CONVB[:, j:j + 1], ALU.mult, ALU.add)
            for k in range(1, 5):
                stt(cav, pbuf[:, :, k:k + 256], CONVW[:, j, k:k + 1], cav, ALU.mult, ALU.add)
            if j < 4:
                act(ca, ca, AF.Silu)
                for c4 in range(4):
                    bk = ps[6 + rot('s0', 2)]
                    for cc in range(4):
                        c = c4 * 4 + cc
                        tr(bk[:, cc * 128:(cc + 1) * 128], ca[:, c * 128:(c + 1) * 128], IDENT)
                    cp('dve', XH[:, c4 * 4:(c4 + 1) * 4, j * 128:(j + 1) * 128], bk[:, :].rearrange("p (a b) -> p a b", a=4))
            elif j == 4:
                act(BT, ca, AF.Silu)
            else:
                act(CT, ca, AF.Silu)
        if stop_after == 'S0' and l == nlayers - 1:
            finish()
            break
        S.barrier()
        ph.cur = s_mark
        YB = ph.get(8192).rearrange("p (c e) -> p c e", c=16)
        ph.get(64 + 256)
        STATE = [ph.get(256), ph.get(256)]
        L2 = ph.get(512).rearrange("p (a b) -> p a b", a=4)
        DTA = ph.get(8)
        ACS = ph.get(8)
        NACS = ph.get(8)
        EACS = ph.get(8)
        TEA = ph.get(8)
        TOEND = ph.get(8)
        CDB = ph.get(8)
        SS = ph.get(8)
        DTA_A = [ph.get(128).rearrange("p (c h) -> p c h", c=16) for _ in range(2)]
        ACS_A = [ph.get(128).rearrange("p (c h) -> p c h", c=16) for _ in range(2)]
        NACS_A = [ph.get(128).rearrange("p (c h) -> p c h", c=16) for _ in range(2)]
        EACS_A = [ph.get(128).rearrange("p (c h) -> p c h", c=16) for _ in range(2)]
        AEND_A = [ph.get(128).rearrange("p (c h) -> p c h", c=16) for _ in range(2)]
        CDB_A = [ph.get(128).rearrange("p (c h) -> p c h", c=16) for _ in range(2)]
        TOE_A = [ph.get(128).rearrange("p (c h) -> p c h", c=16) for _ in range(2)]
        EMf = ph.get(1024)
        EM = EMf.rearrange("p (h s) -> p h s", h=8)
        MT = ph.get(1024).rearrange("p (h s) -> p h s", h=8)
        GT = ph.get(256).rearrange("p (g s) -> p g s", g=2)
        BTOK = ph.get(128)
        XD = ph.get(512)
        XDTE = ph.get(512)
        ZC = [ph.get(512) for _ in range(2)]
        F1 = ph.get(512)
        F2 = ph.get(512)
        MST = [ph.get(512).rearrange("p (a b) -> p a b", a=4) for _ in range(2)]
        SOUT = [ph.get(256).rearrange("p (a b) -> p a b", a=4) for _ in range(2)]

        def h8(ap):
            return ap.rearrange("p (h e) -> p h e", h=8)

        for d_, st_d in enumerate([stf_d, stb_d]):
            mset('pool', L2, 0.0)
            src = st_d[l].rearrange("(i hh) p n -> (hh p) i n", hh=2)
            dma(L2[:, 0:2, 0:64], src[:, 0:2, :])
            dma(L2[:, 2:4, 64:128], src[:, 2:4, :])
            bk = ps[7]
            for i in range(4):
                tr(bk[:, i * 128:(i + 1) * 128], L2[:, i, :], IDENT)
            cp('dve', STATE[d_][0:64, :], bk[0:64, 0:256])
            cp('dve', STATE[d_][64:128, :], bk[64:128, 256:512])

        if stop_after == 'S1' and l == nlayers - 1:
            finish()
            break

        S.barrier()
        import os as _os
        _cut = int(_os.environ.get('KDBG_CUT', '99'))

        def fl(a):
            return a.rearrange("p c h -> p (c h)")
        for d_ in range(2):
            Tm_ = TL if d_ == 0 else TU
            tt('dve', DTA_A[d_], DT[:, :, d_ * 8:(d_ + 1) * 8], bc(ABC[:, d_ * 8:(d_ + 1) * 8].unsqueeze(1), [128, 16, 8]), ALU.mult)
            mm(ps[0][:, 0:128], Tm_, fl(DTA_A[d_]))
            cp('dve', fl(ACS_A[d_]), ps[0][:, 0:128])
            ts('dve', fl(NACS_A[d_]), fl(ACS_A[d_]), -1.0, None, ALU.mult)
            act(fl(EACS_A[d_]), fl(ACS_A[d_]), AF.Exp)
            mm(ps[1][:, 0:128], ONES, fl(DTA_A[d_]))
            cp('dve', fl(AEND_A[d_]), ps[1][:, 0:128])
            act(fl(CDB_A[d_]), fl(AEND_A[d_]), AF.Exp)
            tt('dve', fl(TOE_A[d_]), fl(AEND_A[d_]), fl(ACS_A[d_]), ALU.subtract)
            act(fl(TOE_A[d_]), fl(TOE_A[d_]), AF.Exp)

        def ssd_chunk(d_, c, final):
            cs = slice(c * 128, (c + 1) * 128)
            Tm = TL if d_ == 0 else TU
            NM = NMF if d_ == 0 else NMB
            eidx = 127 if d_ == 0 else 0
            dtc = DT[:, c, d_ * 8:(d_ + 1) * 8]
            DTA = DTA_A[d_][:, c, :]
            ACS = ACS_A[d_][:, c, :]
            NACS = NACS_A[d_][:, c, :]
            EACS = EACS_A[d_][:, c, :]
            CDB = CDB_A[d_][:, c, :]
            TOEND = TOE_A[d_][:, c, :]
            for h in range(8):
                bk = ps[1 + h // 4]
                mm(bk[:, (h % 4) * 128:(h % 4 + 1) * 128], bc(DTA[:, h:h + 1], [128, 128]), Tm)
            for hh in range(2):
                v = ps[1 + hh][:, :].rearrange("p (h s) -> p h s", h=4)
                for h4 in range(4):
                    h = hh * 4 + h4
                    stt(EM[:, h, :], v[:, h4, :], NACS[:, h:h + 1], NM, ALU.add, ALU.add)
            act(EMf, EMf, AF.Exp)
            if _cut <= 2:
                return
            gbank = [ps[3][:, 0:128], ps[0][:, 128:256]]
            prev = None
            for g in range(2):
                prev = mm(gbank[g], BT[64 * g:64 * g + 64, cs], CT[64 * g:64 * g + 64, cs], after=prev)
            for g in range(2):
                cp('act', GT[:, g, :], gbank[g])
            for g in range(2):
                tt('dve', MT[:, 4 * g:4 * g + 4, :], EM[:, 4 * g:4 * g + 4, :],
                   bc(GT[:, g, :].unsqueeze(1), [128, 4, 128]), ALU.mult)
            if _cut <= 3:
                return
            xh = h8(XH[:, c, :])
            tt('dve', h8(XD), xh, bc(dtc.unsqueeze(2), [128, 8, 64]), ALU.mult)
            tt('dve', h8(XDTE), h8(XD), bc(TOEND.unsqueeze(2), [128, 8, 64]), ALU.mult)
            if _cut <= 4:
                return
            for h in range(8):
                mm(ps[4][:, h * 64:(h + 1) * 64], MT[:, h, :], XD[:, h * 64:(h + 1) * 64])
            ybank = [ps[5][:, 0:256], ps[3][:, 256:512]]
            prev = None
            for g in range(2):
                prev = mm(ybank[g], CT[64 * g:64 * g + 64, cs], STATE[d_][64 * g:64 * g + 64, :], after=prev)
            tr(ps[7][:, 0:128], BT[:, cs], IDENT)
            cp('act', BTOK, ps[7][:, 0:128])
            mm(ps[6][:, :], BTOK, XDTE)
            if _cut <= 5:
                return
            for g in range(2):
                tt('dve', F2[:, 256 * g:256 * g + 256].rearrange("p (h e) -> p h e", h=4),
                   ybank[g].rearrange("p (h e) -> p h e", h=4),
                   bc(EACS[:, 4 * g:4 * g + 4].unsqueeze(2), [128, 4, 64]), ALU.mult)
            if not final:
                tt('dve', YB[:, c, :], ps[4][:, :], F2, ALU.add)
            else:
                zc = ZC[rot('zc', 2)]
                dma(zc, zs_d[c * 128:(c + 1) * 128, :])
                tt('dve', F1, ps[4][:, :], F2, ALU.add)
                tt('dve', F1, F1, YB[:, c, :], ALU.add)
                tt('dve', h8(F2), xh, bc(DBC.unsqueeze(2), [128, 8, 64]), ALU.mult)
                tt('dve', F1, F1, F2, ALU.add)
                tt('dve', F1, F1, zc, ALU.mult)
                tt('dve', F2, F1, F1, ALU.mult)
                red(SS[:, 0:1], F2)
                rstd_from(SS[:, 1:2], SS[:, 0:1], 512, SS[:, 2:3])
                ts('dve', F1, F1, SS[:, 1:2], None, ALU.mult)
                tt('dve', F1, F1, SSDG, ALU.mult)
                for j in range(4):
                    tr(ps[7][:, j * 128:(j + 1) * 128], F1[:, j * 128:(j + 1) * 128], IDENT)
                mst = MST[rot('mst', 2)]
                cp('act', mst, ps[7][:, :].rearrange("p (a b) -> p a b", a=4))
                dma(mixt_d[512:1024, cs].rearrange("(j p) t -> p j t", p=128), mst)
            if _cut <= 6:
                return
            for g in range(2):
                sl = STATE[d_][64 * g:64 * g + 64, :]
                sv = sl.rearrange("p (h e) -> p h e", h=4)
                tt('dve', sv, sv, bc(CDB[64 * g:64 * g + 64, 4 * g:4 * g + 4].unsqueeze(2), [64, 4, 64]), ALU.mult)
                tt('dve', sl, sl, ps[6][64 * g:64 * g + 64, 256 * g:256 * g + 256], ALU.add)
            boundary = (c % 2 == 1) if d_ == 0 else (c % 2 == 0)
            if boundary:
                seq = c // 2
                sbank = [ps[7], ps[6]]
                prev = None
                for i in range(4):
                    g = i // 2
                    prev = tr(sbank[g][:, i * 64:(i + 1) * 64], STATE[d_][64 * g:64 * g + 64, (i % 2) * 128:(i % 2 + 1) * 128],
                              IDENT[64 * g:64 * g + 64, 64 * g:64 * g + 64], after=(prev if i == 2 else None))
                so = SOUT[rot('so', 2)]
                for g in range(2):
                    cp('act', so[:, 2 * g:2 * g + 2, :], sbank[g][:, 128 * g:128 * g + 128].rearrange("p (a b) -> p a b", a=2))
                od = nsf_d if d_ == 0 else nsb_d
                dma(od[l, seq].rearrange("(i hh) p n -> (hh p) i n", hh=2), so)
                last = (c == 15) if d_ == 0 else (c == 0)
                if not last:
                    ts('dve', STATE[d_], STATE[d_], KEEP[:, 0:1], None, ALU.mult)

        for c in range(int(_os.environ.get('KDBG_NCH', '16'))):
            ssd_chunk(0, c, False)
        if stop_after == 'S2' and l == nlayers - 1:
            finish()
            break
        for c in reversed(range(16)):
            ssd_chunk(1, c, True)
        if stop_after == 'S' and l == nlayers - 1:
            finish()
            break

        def build_gate(GBC, DG, col0):
            for c in range(8):
                ts('dve', DG[:, c, :], IDENT, MODF[:, l, col0 + c:col0 + c + 1], None, ALU.mult)
            for hf in range(2):
                bk = ps[rot('o', 8)]
                mm(bk[:, :], ONES, DG[:, 4 * hf:4 * hf + 4, :].rearrange("p a b -> p (a b)"))
                cp('act', GBC[:, hf * 512:(hf + 1) * 512], bk[:, :])

        def resid_evac(bk, tbg, hf, GBC, TMP):
            tmp = TMP[rot('tmp', 2)]
            tt('dve', tmp, bk[:, :], GBC[:, hf * 512:(hf + 1) * 512], ALU.mult)
            xs = X[:, tbg, hf * 512:(hf + 1) * 512]
            stt(xs, xs, ALPHA, tmp, ALU.mult, ALU.add)

        def ln_block(tbg, LNG, LNB, ST6, MV):
            xs = X[:, tbg, :]
            for hf in range(2):
                src_ = X[:, tbg, hf * 512:(hf + 1) * 512]
                dst_ = ST6[:, hf * 6:(hf + 1) * 6]
                S.add('dve', lambda e, dst_=dst_, src_=src_: e.bn_stats(out=dst_, in_=src_), reads=[src_], writes=[dst_])
            S.add('dve', lambda e: e.bn_aggr(out=MV[:, 0:2], in_=ST6[:, 0:12]), reads=[ST6[:, 0:12]], writes=[MV[:, 0:2]])
            act(MV[:, 2:3], MV[:, 1:2], AF.Sqrt, bias=EPS, scale=1.0)
            recip(MV[:, 3:4], MV[:, 2:3])
            ts('dve', xs, xs, MV[:, 0:1], MV[:, 3:4], ALU.subtract, ALU.mult)
            tt('pool', xs, xs, LNG, ALU.mult)
            tt('pool', xs, xs, LNB, ALU.add)

        open_phase('O%d' % l, 5400, 16384)
        WO = pr.get(8192).rearrange("p (c n) -> p c n", c=8)
        MX = [pr.get(4096).rearrange("p (c t) -> p c t", c=8) for _ in range(2)]
        GBC = ph.get(1024)
        LNG = ph.get(1024)
        LNB = ph.get(1024)
        DG = ph.get(1024).rearrange("p (c q) -> p c q", c=8)
        TMP = [ph.get(512) for _ in range(2)]
        ST6 = ph.get(16)
        MV = ph.get(8)
        dma(WO, wout_d[l].rearrange("(c p) n -> p c n", p=128), cast=True)
        dma(LNG, lng_d[l, 0:1, :].partition_broadcast(128))
        dma(LNB, lnb_d[l, 0:1, :].partition_broadcast(128))
        build_gate(GBC, DG, 16)
        def issue_mx(g_):
            dma(MX[g_ % 2], mixt_d[:, g_ * 512:(g_ + 1) * 512].rearrange("(c p) t -> p c t", p=128), cast=True)
            return MX[g_ % 2]
        mx_pref = Pref(list(range(4)), 1, issue_mx)
        for g in range(4):
            mx = mx_pref.get(g)
            for tb in range(4):
                tbg = 4 * g + tb
                for hf in range(2):
                    bk = ps[rot('o', 8)]
                    for ec in range(8):
                        mm(bk[:, :], mx[:, ec, tb * 128:(tb + 1) * 128], WO[:, ec, hf * 512:(hf + 1) * 512],
                           start=(ec == 0), stop=(ec == 7), r=True)
                    resid_evac(bk, tbg, hf, GBC, TMP)
                ln_block(tbg, LNG, LNB, ST6, MV)
        if dbg and l == nlayers - 1:
            for tb in range(NTB):
                dma(dbg_x1[tb * 128:(tb + 1) * 128, :], X[:, tb, :])
        if stop_after == 'O' and l == nlayers - 1:
            finish()
            break

        open_phase('F%d' % l, 6400, 23552)
        HTf = pr.get(4096).rearrange("p (a b) -> p a b", a=8)
        ACTT = pr.get(NFF * 512).rearrange("p (i t) -> p i t", i=NFF)
        WGU = [pr.get(2048).rearrange("p (u c n) -> p u c n", u=2, c=8) for _ in range(3)]
        WD = [pr.get(512) for _ in range(4)]
        GBC = ph.get(1024)
        LNG = ph.get(1024)
        LNB = ph.get(1024)
        DG = ph.get(1024).rearrange("p (c q) -> p c q", c=8)
        TMP = [ph.get(512) for _ in range(2)]
        SIL = [ph.get(512) for _ in range(2)]
        ST6 = ph.get(16)
        MV = ph.get(8)
        dma(LNG, lng_d[l, 1:2, :].partition_broadcast(128))
        dma(LNB, lnb_d[l, 1:2, :].partition_broadcast(128))
        build_gate(GBC, DG, 40)

        def issue_wgu(key):
            i_ = key[1]
            w_ = WGU[rot('wgu', 3)]
            dma(w_[:, 0], wfi_d[l][:, i_ * 128:(i_ + 1) * 128].rearrange("(c p) n -> p c n", p=128), cast=True)
            dma(w_[:, 1], wfi_d[l][:, DFF + i_ * 128:DFF + (i_ + 1) * 128].rearrange("(c p) n -> p c n", p=128), cast=True)
            return w_

        def issue_wd(key):
            _, hf_, i_ = key
            wd_ = WD[rot('wd', 4)]
            dma(wd_, wfo_d[l][i_ * 128:(i_ + 1) * 128, hf_ * 512:(hf_ + 1) * 512], cast=True)
            return wd_
        wgu_pref = Pref([(g_, i_) for g_ in range(4) for i_ in range(NFF)], 2, issue_wgu)
        wd_pref = Pref([(g_, hf_, i_) for g_ in range(4) for hf_ in range(2) for i_ in range(NFF)], 3, issue_wd)
        for g in range(4):
            for dc in range(8):
                bk = ps[rot('fh', 2)]
                for tb in range(4):
                    tr(bk[:, tb * 128:(tb + 1) * 128], X[:, 4 * g + tb, dc * 128:(dc + 1) * 128], IDENT)
                act(R(HTf[:, dc, :]), bk[:, :], AF.Identity, bias=MODF[:, l, 24 + dc:25 + dc], scale=MODF[:, l, 32 + dc:33 + dc])
            for i in range(NFF):
                w = wgu_pref.get(g * NFF + i)
                bg = ps[rot('fg', 2)]
                bu = ps[2 + rot('fu', 2)]
                for dc in range(8):
                    mm(bg[:, :], w[:, 0, dc, :], HTf[:, dc, :], start=(dc == 0), stop=(dc == 7), r=True)
                for dc in range(8):
                    mm(bu[:, :], w[:, 1, dc, :], HTf[:, dc, :], start=(dc == 0), stop=(dc == 7), r=True)
                sil = SIL[rot('sil', 2)]
                act(sil, bg[:, :], AF.Silu)
                tt('dve', R(ACTT[:, i, :]), sil, bu[:, :], ALU.mult)
            for hf in range(2):
                for i in range(NFF):
                    wd = wd_pref.get((g * 2 + hf) * NFF + i)
                    for tb in range(4):
                        mm(ps[4 + tb][:, :], ACTT[:, i, tb * 128:(tb + 1) * 128], wd, start=(i == 0), stop=(i == NFF - 1), r=True)
                for tb in range(4):
                    resid_evac(ps[4 + tb], 4 * g + tb, hf, GBC, TMP)
            for tb in range(4):
                ln_block(4 * g + tb, LNG, LNB, ST6, MV)
    if not done[0]:
        for tb in range(NTB):
            dma(y_d[tb * 128:(tb + 1) * 128, :], X[:, tb, :])
        finish()
    if phase_ctx[0] is not None:
        phase_ctx[0].close()
    return es


def _consts():
    i = np.arange(128)
    ident = np.eye(128, dtype=np.float32)
    tl = (i[:, None] <= i[None, :]).astype(np.float32)
    tu = (i[:, None] >= i[None, :]).astype(np.float32)
    nmf = np.where(i[None, :] >= i[:, None], 0.0, NEG).astype(np.float32)
    nmb = np.where(i[None, :] <= i[:, None], 0.0, NEG).astype(np.float32)
    ones = np.ones((128, 128), np.float32)
    return np.concatenate([ident, tl, tu, nmf, nmb, ones], axis=1)


def _rope_tables(is_sample):
    t = np.arange(NT)
    out = np.zeros((NT, 96), np.float32)
    if is_sample:
        row = (t // 64).astype(np.float32)
        col = (t % 64).astype(np.float32)

        def tab(dim):
            nf = dim // 4
            inv = (np.float32(10000.0) ** (-np.arange(nf, dtype=np.float32) / np.float32(nf))).astype(np.float32)
            ang = np.concatenate([row[:, None] * inv, col[:, None] * inv], -1).astype(np.float32)
            return np.cos(ang).astype(np.float32), np.sin(ang).astype(np.float32)
        cd, sd = tab(32)
        cg, sg = tab(64)
        out[:, 0:16] = cd
        out[:, 16:32] = sd
        out[:, 32:64] = cg
        out[:, 64:96] = sg
    else:
        out[:, 0:16] = 1.0
        out[:, 32:64] = 1.0
    return np.ascontiguousarray(out.reshape(16, 128, 96).transpose(1, 0, 2).reshape(128, 16 * 96))


def _maskb(is_sample):
    m = np.zeros((128, NKB, 8), np.float32)
    if not is_sample:
        for kb in range(NKB):
            for qh in range(8):
                if kb >= 16 or (kb // 2) != qh:
                    m[:, kb, qh] = NEG
    return m.reshape(128, 160)


def make_in_maps(inputs):
    f = lambda a: np.ascontiguousarray(np.asarray(a, dtype=np.float32))
    w_shared = {
        "w_ada": f(inputs["w_ada"]), "b_ada": f(inputs["b_ada"]), "w_in": f(inputs["w_in"]), "w_out": f(inputs["w_out"]),
        "diff_lambda": f(inputs["diff_lambda"]).reshape(DEPTH, 128), "diff_subln_g": f(inputs["diff_subln_g"]),
        "qk_norm_g": f(inputs["qk_norm_g"]).reshape(DEPTH, 128),
        "convw": np.ascontiguousarray(f(inputs["ssd_conv_w"]).reshape(DEPTH, 5, 6, 128).transpose(0, 3, 2, 1).reshape(DEPTH, 128, 30)),
        "convb": np.ascontiguousarray(f(inputs["ssd_conv_b"]).reshape(DEPTH, 6, 128).transpose(0, 2, 1)),
        "ssd_A_log": f(inputs["ssd_A_log"]).reshape(DEPTH, 16), "ssd_dt_bias": f(inputs["ssd_dt_bias"]).reshape(DEPTH, 16),
        "ssd_D": f(inputs["ssd_D"]), "ssd_norm_g": f(inputs["ssd_norm_g"]),
        "ln_g": f(inputs["ln_g"]), "ln_b": f(inputs["ln_b"]),
        "w_ffn_in": f(inputs["w_ffn_in"]), "w_ffn_out": f(inputs["w_ffn_out"]),
        "c128": _consts(),
    }
    xp = f(inputs["x_prompt"])
    xs = f(inputs["x_sample"])
    cc = f(inputs["c"])
    cctx = f(inputs["c_ctx"])
    cdk = f(inputs["cache_diff_k"]).reshape(4, DEPTH, PAST, 256)
    cdv = f(inputs["cache_diff_v"]).reshape(4, DEPTH, PAST, 256)
    cgk = f(inputs["cache_gqa_k"]).reshape(4, DEPTH, PAST, 128)
    cgv = f(inputs["cache_gqa_v"]).reshape(4, DEPTH, PAST, 128)
    sf = f(inputs["state_ssd_fwd"])
    sbw = f(inputs["state_ssd_bwd"])
    maps = []
    rope_p, rope_s = _rope_tables(False), _rope_tables(True)
    mb_p, mb_s = _maskb(False), _maskb(True)
    for core in range(8):
        m = dict(w_shared)
        if core < 4:
            m["x"] = np.ascontiguousarray(xp[8 * core:8 * core + 8].reshape(NT, D))
            cv = cctx
            m["cache_dk"] = np.zeros((DEPTH, PAST, 256), np.float32)
            m["cache_dv"] = np.zeros((DEPTH, PAST, 256), np.float32)
            m["cache_gk"] = np.zeros((DEPTH, PAST, 128), np.float32)
            m["cache_gv"] = np.zeros((DEPTH, PAST, 128), np.float32)
            m["st_f"] = np.zeros((DEPTH, 8, 64, 64), np.float32)
            m["st_b"] = np.zeros((DEPTH, 8, 64, 64), np.float32)
            m["rope"] = rope_p
            m["maskb"] = mb_p
            m["keep"] = np.zeros((128, 8), np.float32)
        else:
            j = core - 4
            m["x"] = np.ascontiguousarray(xs[j])
            cv = cc[j]
            m["cache_dk"] = np.ascontiguousarray(cdk[j])
            m["cache_dv"] = np.ascontiguousarray(cdv[j])
            m["cache_gk"] = np.ascontiguousarray(cgk[j])
            m["cache_gv"] = np.ascontiguousarray(cgv[j])
            m["st_f"] = np.ascontiguousarray(sf[j])
            m["st_b"] = np.ascontiguousarray(sbw[j])
            m["rope"] = rope_s
            m["maskb"] = mb_s
            m["keep"] = np.ones((128, 8), np.float32)
        m["cvecT"] = np.ascontiguousarray(cv.reshape(8, 128).T)
        maps.append(m)
    return maps


def kernel(**inputs):
    nc = bass.Bass("TRN2", target_bir_lowering=False)
    es = build_program(nc)
    maps = make_in_maps(inputs)
    res = run_bass_kernel_spmd(nc, maps, core_ids=list(range(8)))
    es.close()
    r = res.results
    B, SEQ = 32, 256
    y_prompt = np.zeros((B, SEQ, D), np.float32)
    y_sample = np.zeros((4, NT, D), np.float32)
    ndk = np.zeros((B, DEPTH, SEQ, 4, 64), np.float32)
    ndv = np.zeros((B, DEPTH, SEQ, 4, 64), np.float32)
    ngk = np.zeros((B, DEPTH, SEQ, 2, 64), np.float32)
    ngv = np.zeros((B, DEPTH, SEQ, 2, 64), np.float32)
    nsf = np.zeros((B, DEPTH, 8, 64, 64), np.float32)
    nsb = np.zeros((B, DEPTH, 8, 64, 64), np.float32)
    for core in range(4):
        o = r[core]
        sl = slice(8 * core, 8 * core + 8)
        y_prompt[sl] = o["y"].reshape(8, SEQ, D)
        ndk[sl] = o["ndk"].reshape(DEPTH, 8, SEQ, 4, 64).transpose(1, 0, 2, 3, 4)
        ndv[sl] = o["ndv"].reshape(DEPTH, 8, SEQ, 4, 64).transpose(1, 0, 2, 3, 4)
        ngk[sl] = o["ngk"].reshape(DEPTH, 8, SEQ, 2, 64).transpose(1, 0, 2, 3, 4)
        ngv[sl] = o["ngv"].reshape(DEPTH, 8, SEQ, 2, 64).transpose(1, 0, 2, 3, 4)
        nsf[sl] = o["nsf"].reshape(DEPTH, 8, 8, 64, 64).transpose(1, 0, 2, 3, 4)
        nsb[sl] = o["nsb"].reshape(DEPTH, 8, 8, 64, 64).transpose(1, 0, 2, 3, 4)
    for j in range(4):
        y_sample[j] = r[4 + j]["y"]
    return (y_prompt, y_sample, ndk, ndv, ngk, ngv, nsf, nsb)
```

```python
import numpy as np
import concourse.bass as bass
import concourse.mybir as mybir

F32 = mybir.dt.float32
F32R = mybir.dt.float32r
AF = mybir.ActivationFunctionType
ALU = mybir.AluOpType
AX = mybir.AxisListType

N_DMA_SEMS = 32
DEBUG_NAMES = None
BIN = 256
BIN_DRAM = 1 << 16
UNTRACKED = set()
SB_NAMES = set(['sb'] + ['ps%d' % i for i in range(8)])


def _prod(xs):
    r = 1
    for x in xs:
        r *= int(x)
    return r


def region(ap):
    t = ap.tensor
    name = t.name
    apl = [(int(s), int(c)) for s, c in ap.ap]
    off = int(ap.offset)
    sp = str(ap.space)
    if 'DRAM' in sp or 'HBM' in sp:
        ext = 1 + sum(abs(s) * (c - 1) for s, c in apl)
        return (name, 0, 1, off, off + ext)
    rowlen = _prod(list(t.shape)[1:])
    p0 = off // rowlen
    f0 = off % rowlen
    np_ = apl[0][1]
    ext = 1 + sum(abs(s) * (c - 1) for s, c in apl[1:])
    if 'PSUM' in sp:
        return (name, 0, 128, 0, rowlen)
    return (name, p0, p0 + np_, f0, f0 + ext)


class Op:
    __slots__ = ('eng', 'fn', 'waits', 'is_dma', 'chan', 'val', 'marked', 'clock', 'inc', 'desc')

    def __init__(self, eng, fn, is_dma):
        self.eng = eng
        self.fn = fn
        self.is_dma = is_dma
        self.waits = []
        self.marked = False
        self.chan = None
        self.val = 0
        self.clock = None


class Rec:
    __slots__ = ('p0', 'p1', 'f0', 'f1', 'w', 'op', 'dead')

    def __init__(self, r, w, op):
        self.p0, self.p1, self.f0, self.f1 = r[1], r[2], r[3], r[4]
        self.w = w
        self.op = op
        self.dead = False


class Sched:
    ENGS = ('pe', 'act', 'dve', 'pool', 'sp')

    def __init__(self, nc):
        self.nc = nc
        self.streams = {e: [] for e in self.ENGS}
        self.pos = {e: 0 for e in self.ENGS}
        self.clock = {e: {} for e in self.ENGS}
        self.bins = {}
        self.dma_rr = 0
        self.dma_rr2 = [0, 0]
        self.dma_last = [None] * N_DMA_SEMS
        self.dma_cnt = [0] * N_DMA_SEMS
        self.nops = 0

    def _overlaps(self, reg, want_reads):
        name, p0, p1, f0, f1 = reg
        out = []
        seen = set()
        bn = BIN if name in SB_NAMES else BIN_DRAM
        for b in range(f0 // bn, (f1 - 1) // bn + 1):
            lst = self.bins.get((name, b))
            if not lst:
                continue
            alive = []
            for rec in lst:
                if rec.dead:
                    continue
                alive.append(rec)
                if id(rec) in seen:
                    continue
                if rec.f0 < f1 and f0 < rec.f1 and rec.p0 < p1 and p0 < rec.p1:
                    if rec.w or want_reads:
                        seen.add(id(rec))
                        out.append(rec)
            if len(alive) != len(lst):
                self.bins[(name, b)] = alive
        return out

    def _add_rec(self, reg, w, op):
        name, p0, p1, f0, f1 = reg
        rec = Rec(reg, w, op)
        bn = BIN if name in SB_NAMES else BIN_DRAM
        for b in range(f0 // bn, (f1 - 1) // bn + 1):
            self.bins.setdefault((name, b), []).append(rec)

    def _need(self, op, dep):
        if dep is op:
            return
        e = op.eng
        ck = self.clock[e]
        if ck.get(dep.chan, 0) >= dep.val:
            return
        op.waits.append(dep)
        dep.marked = True
        for k, v in dep.clock.items():
            if ck.get(k, 0) < v:
                ck[k] = v
        if ck.get(dep.chan, 0) < dep.val:
            ck[dep.chan] = dep.val

    def add(self, eng, fn, reads=(), writes=(), is_dma=False, after=None):
        op = Op(eng, fn, is_dma)
        self.nops += 1
        if DEBUG_NAMES is not None:
            op.desc = (eng, [region(a) for a in reads], [region(a) for a in writes])
        rregs = [r_ for r_ in (region(a) for a in reads) if r_[0] not in UNTRACKED]
        wregs = [region(a) for a in writes]
        deps = []
        for r in rregs:
            for rec in self._overlaps(r, False):
                deps.append(rec.op)
        for w in wregs:
            for rec in self._overlaps(w, True):
                deps.append(rec.op)
        if is_dma:
            half = N_DMA_SEMS // 2
            qi = 0 if eng == 'sp' else 1
            k = qi * half + self.dma_rr2[qi]
            self.dma_rr2[qi] = (self.dma_rr2[qi] + 1) % half
            if self.dma_last[k] is not None:
                deps.append(self.dma_last[k])
            self.dma_cnt[k] += 1
            op.chan = 'dma%d' % k
            op.val = self.dma_cnt[k]
            self.dma_last[k] = op
            op.inc = k
        else:
            self.pos[eng] += 1
            op.chan = eng
            op.val = self.pos[eng]
            op.inc = None
        own = [d for d in deps if (not d.is_dma) and d.eng == eng]
        oth = [d for d in deps if d.is_dma or d.eng != eng]
        oth.sort(key=lambda d: -d.val)
        own.sort(key=lambda d: -d.val)
        for d in oth + own:
            if eng == 'pe' and d.eng == 'pe' and not d.is_dma and not is_dma:
                continue
            self._need(op, d)
        if after is not None:
            self._need(op, after)
        op.clock = dict(self.clock[eng])
        for w in wregs:
            name, p0, p1, f0, f1 = w
            for rec in self._overlaps(w, True):
                if rec.p0 >= p0 and rec.p1 <= p1 and rec.f0 >= f0 and rec.f1 <= f1:
                    rec.dead = True
            self._add_rec(w, True, op)
        for r in rregs:
            name, p0, p1, f0, f1 = r
            for rec in self._overlaps(r, True):
                if (not rec.w) and rec.op.eng == eng and rec.op.is_dma == is_dma and \
                        rec.p0 == p0 and rec.p1 == p1 and rec.f0 == f0 and rec.f1 == f1 and not is_dma:
                    rec.dead = True
            self._add_rec(r, False, op)
        self.streams[eng].append(op)
        return op

    def drain(self):
        op = Op('sp', None, False)
        op.chan = 'sp'
        op.val = 0
        for k in range(N_DMA_SEMS):
            if self.dma_last[k] is not None:
                op.waits.append(self.dma_last[k])
        op.clock = {}
        self.streams['sp'].append(op)

    def barrier(self):
        for e in self.ENGS:
            self.streams[e].append(None)

    def dma(self, eng, out, in_, extra_reads=(), extra_writes=(), **kw):
        def fn(e, out=out, in_=in_, kw=kw):
            return e.dma_start(out=out, in_=in_, **kw)
        return self.add(eng, fn, reads=[in_] + list(extra_reads), writes=[out] + list(extra_writes), is_dma=True)

    def emit(self):
        nc = self.nc
        fin = Op('sp', None, False)
        for k in range(N_DMA_SEMS):
            if self.dma_last[k] is not None:
                fin.waits.append(self.dma_last[k])
        for e in ('pe', 'act', 'dve', 'pool'):
            if self.streams[e]:
                last = None
                for o in reversed(self.streams[e]):
                    if o is not None and not o.is_dma:
                        last = o
                        break
                if last is not None:
                    last.marked = True
                    fin.waits.append(last)
        self.streams['sp'].append(fin)
        cnt_of = {}
        for e in self.ENGS:
            c = 0
            for o in self.streams[e]:
                if o is None or o.is_dma or o.fn is None:
                    continue
                if o.marked:
                    c += 1
                    cnt_of[id(o)] = c
        segs = {e: [[]] for e in self.ENGS}
        for e in self.ENGS:
            for o in self.streams[e]:
                if o is None:
                    segs[e].append([])
                else:
                    segs[e][-1].append(o)
        nseg = max(len(segs[e]) for e in self.ENGS)
        import contextlib
        with contextlib.ExitStack() as st:
            esem = {e: st.enter_context(nc.semaphore('s_' + e)) for e in self.ENGS}
            dsem = [st.enter_context(nc.semaphore('s_dma%d' % k)) for k in range(N_DMA_SEMS)]

            def make(ename, ops):
                def body(e):
                    for o in ops:
                        ws = {}
                        for d in o.waits:
                            if d.is_dma:
                                key = ('d', d.inc)
                                v = d.val * 16
                            else:
                                key = ('e', d.eng)
                                v = cnt_of[id(d)]
                            if ws.get(key, 0) < v:
                                ws[key] = v
                        for key, v in ws.items():
                            sem = dsem[key[1]] if key[0] == 'd' else esem[key[1]]
                            e.wait_ge(sem, v)
                        if o.fn is None:
                            continue
                        ins = o.fn(e)
                        if DEBUG_NAMES is not None:
                            try:
                                DEBUG_NAMES[str(ins.ins.name)] = getattr(o, 'desc', None)
                            except Exception:
                                pass
                        if o.is_dma:
                            ins.then_inc(dsem[o.inc], 16)
                        elif o.marked:
                            ins.then_inc(esem[ename], 1)
                return body
            for si in range(nseg):
                with (nc.Block(no_gpsimd_drain=True) if si < nseg - 1 else nc.Block()) as blk:
                    engobj = {'pe': blk.tensor, 'act': blk.scalar, 'dve': blk.vector, 'pool': blk.gpsimd, 'sp': blk.sync}
                    for ename in self.ENGS:
                        ops = segs[ename][si] if si < len(segs[ename]) else []
                        engobj[ename](make(ename, ops))


import math
import contextlib
from concourse.bass_utils import run_bass_kernel_spmd

D = 1024
NT = 2048
NTB = 16
DEPTH = 2
PAST = 512
NKB = 20
DFF = 2816
NFF = 22
INW = 2576
EPS = 1e-5
ALPHA = (2 * DEPTH) ** 0.25
NEG = -30000.0
SB_COLS = 52000


def build_program(nc, dbg=False, stop_after=None, nlayers=DEPTH):
    S = Sched(nc)
    es = contextlib.ExitStack()

    def din(name, shape):
        UNTRACKED.add(name)
        return nc.dram_tensor(name, list(shape), F32, kind="ExternalInput").ap()

    def dout(name, shape):
        return nc.dram_tensor(name, list(shape), F32, kind="ExternalOutput").ap()

    def dscr(name, shape):
        return nc.dram_tensor(name, list(shape), F32, kind=("ExternalOutput" if dbg else "Internal")).ap()

    x_d = din("x", [NT, D])
    cvt_d = din("cvecT", [128, 8])
    cdk_d = din("cache_dk", [DEPTH, PAST, 256])
    cdv_d = din("cache_dv", [DEPTH, PAST, 256])
    cgk_d = din("cache_gk", [DEPTH, PAST, 128])
    cgv_d = din("cache_gv", [DEPTH, PAST, 128])
    stf_d = din("st_f", [DEPTH, 8, 64, 64])
    stb_d = din("st_b", [DEPTH, 8, 64, 64])
    wada_d = din("w_ada", [DEPTH, D, 6 * D])
    bada_d = din("b_ada", [DEPTH, 6 * D])
    win_d = din("w_in", [DEPTH, D, INW])
    wout_d = din("w_out", [DEPTH, D, D])
    lam_d = din("diff_lambda", [DEPTH, 128])
    subg_d = din("diff_subln_g", [DEPTH, 64])
    qkg_d = din("qk_norm_g", [DEPTH, 128])
    convw_d = din("convw", [DEPTH, 128, 30])
    convb_d = din("convb", [DEPTH, 128, 6])
    alog_d = din("ssd_A_log", [DEPTH, 16])
    dtb_d = din("ssd_dt_bias", [DEPTH, 16])
    dd_d = din("ssd_D", [DEPTH, 8])
    ssdg_d = din("ssd_norm_g", [DEPTH, 512])
    lng_d = din("ln_g", [DEPTH, 2, D])
    lnb_d = din("ln_b", [DEPTH, 2, D])
    wfi_d = din("w_ffn_in", [DEPTH, D, 2 * DFF])
    wfo_d = din("w_ffn_out", [DEPTH, DFF, D])
    c128_d = din("c128", [128, 6 * 128])
    rope_d = din("rope", [128, 16 * 96])
    maskb_d = din("maskb", [128, 160])
    keep_d = din("keep", [128, 8])

    y_d = dout("y", [NT, D])
    ndk_d = dout("ndk", [DEPTH, NT, 256])
    ndv_d = dout("ndv", [DEPTH, NT, 256])
    ngk_d = dout("ngk", [DEPTH, NT, 128])
    ngv_d = dout("ngv", [DEPTH, NT, 128])
    nsf_d = dout("nsf", [DEPTH, 8, 8, 64, 64])
    nsb_d = dout("nsb", [DEPTH, 8, 8, 64, 64])

    qts_d = dscr("qts", [512, NT])
    mixt_d = dscr("mixt", [1024, NT])
    xbcs_d = dscr("xbcs", [768, NT])
    zs_d = dscr("zs", [NT, 512])
    if dbg:
        dbg_modf = dout("dbg_modf", [128, 96])
        dbg_dt = dout("dbg_dt", [128, 256])
        dbg_x1 = dout("dbg_x1", [NT, D])

    PERS = 18720
    sb = es.enter_context(nc.sbuf_tensor("sb", [128, PERS], F32))
    ps = [es.enter_context(nc.psum_tensor("ps%d" % i, [128, 512], F32)) for i in range(8)]

    class Alloc:
        def __init__(self, t=None, limit=0):
            self.bind(t, limit)

        def bind(self, t, limit):
            self.t = t
            self.cur = 0
            self.limit = limit

        def get(self, n):
            o = self.cur
            self.cur += n
            assert self.cur <= self.limit, ("SBUF overflow", self.cur, self.limit)
            return self.t[:, o:o + n]

        def reset(self):
            self.cur = 0

    pa = Alloc(sb, PERS)
    X = pa.get(NTB * D).rearrange("p (a b) -> p a b", a=NTB)
    C128 = pa.get(768)
    IDENT = C128[:, 0:128]
    TL = C128[:, 128:256]
    TU = C128[:, 256:384]
    NMF = C128[:, 384:512]
    NMB = C128[:, 512:640]
    ONES = C128[:, 640:768]
    MASKB = pa.get(160)
    KEEP = pa.get(8)
    MODF = pa.get(96).rearrange("p (l c) -> p l c", l=2)
    S8 = pa.get(8)
    DT = pa.get(256).rearrange("p (a b) -> p a b", a=NTB)
    CONVW = pa.get(30).rearrange("p (j k) -> p j k", j=6)
    CONVB = pa.get(6)
    pa.get(4)
    ABC = pa.get(16)
    DTB = pa.get(16)
    DBC = pa.get(8)
    LAMS = pa.get(16)
    SSDG = pa.get(512)
    QKG = pa.get(128)
    SUBG = pa.get(64)
    LAMB = pa.get(128)
    tmpl = pa.get(64)
    ph = Alloc()
    pr = Alloc()
    phase_ctx = [None]
    FREE = SB_COLS - PERS

    def close_phase():
        if phase_ctx[0] is not None:
            S.drain()
            S.barrier()
            phase_ctx[0].close()
            phase_ctx[0] = None

    def open_phase(tag, n_plain, n_r):
        close_phase()
        assert n_plain + n_r <= FREE, (tag, n_plain, n_r, FREE)
        ctx = contextlib.ExitStack()
        tp_ = ctx.enter_context(nc.sbuf_tensor("pp_" + tag, [128, n_plain], F32))
        SB_NAMES.add("pp_" + tag)
        ph.bind(tp_, n_plain)
        if n_r:
            tr_ = ctx.enter_context(nc.sbuf_tensor("pr_" + tag, [128, n_r], F32))
            SB_NAMES.add("pr_" + tag)
            pr.bind(tr_, n_r)
        phase_ctx[0] = ctx

    def aps(*xs):
        return [a for a in xs if hasattr(a, 'tensor')]

    import os as _os0
    USE_R = _os0.environ.get('KDBG_R', '1') == '1'

    def R(ap):
        return ap.bitcast(F32R) if USE_R else ap

    def mm(out, lhsT, rhs, start=True, stop=True, tp=None, after=None, r=False):
        if r:
            lhsT = R(lhsT)
            rhs = R(rhs)

        def fn(e):
            if tp is not None:
                return e.matmul(out, lhsT=lhsT, rhs=rhs, start=start, stop=stop, tile_position=tp, skip_group_check=True)
            return e.matmul(out, lhsT=lhsT, rhs=rhs, start=start, stop=stop, skip_group_check=True)
        return S.add('pe', fn, reads=[lhsT, rhs], writes=[out], after=after)

    def tr(out, in_, ident, after=None):
        return S.add('pe', lambda e: e.transpose(out=out, in_=in_, identity=ident), reads=[in_, ident], writes=[out], after=after)

    def act(out, in_, func, bias=None, scale=None, eng='act'):
        kw = {}
        if bias is not None:
            kw['bias'] = bias
        if scale is not None:
            kw['scale'] = scale
        return S.add('act', lambda e: e.activation(out=out, in_=in_, func=func, **kw),
                     reads=aps(in_, bias, scale), writes=[out])

    def ts(eng, out, in0, s1, s2, op0, op1=None):
        def fn(e):
            if op1 is None:
                return e.tensor_scalar(out=out, in0=in0, scalar1=s1, scalar2=None, op0=op0)
            return e.tensor_scalar(out=out, in0=in0, scalar1=s1, scalar2=s2, op0=op0, op1=op1)
        return S.add(eng, fn, reads=aps(in0, s1, s2), writes=[out])

    def tt(eng, out, in0, in1, op):
        return S.add(eng, lambda e: e.tensor_tensor(out=out, in0=in0, in1=in1, op=op), reads=[in0, in1], writes=[out])

    def stt(out, in0, scalar, in1, op0, op1):
        return S.add('dve', lambda e: e.scalar_tensor_tensor(out=out, in0=in0, scalar=scalar, in1=in1, op0=op0, op1=op1),
                     reads=aps(in0, scalar, in1), writes=[out])

    def cp(eng, out, in_):
        if eng == 'act':
            return S.add('act', lambda e: e.copy(out=out, in_=in_), reads=[in_], writes=[out])
        return S.add(eng, lambda e: e.tensor_copy(out=out, in_=in_), reads=[in_], writes=[out])

    def red(out, in_, op=ALU.add):
        return S.add('dve', lambda e: e.tensor_reduce(out=out, in_=in_, axis=AX.X, op=op), reads=[in_], writes=[out])

    def recip(out, in_):
        return S.add('dve', lambda e: e.reciprocal(out=out, in_=in_), reads=[in_], writes=[out])

    def mset(eng, ap, val):
        return S.add(eng, lambda e: e.memset(ap, val), reads=[], writes=[ap])

    def dma(out, in_, eng=None, cast=False):
        is_store = 'DRAM' in str(out.space)
        if cast and USE_R:
            return S.dma('pool', R(out), in_)
        return S.dma('pool' if is_store else 'sp', out, in_)

    def bc(ap, shape):
        return ap.to_broadcast(list(shape))

    class Pref:
        def __init__(self, keys, ahead, issue):
            self.keys = keys
            self.ahead = ahead
            self.issue = issue
            self.n = 0
            self.buf = {}

        def get(self, idx):
            while self.n < len(self.keys) and self.n <= idx + self.ahead:
                self.buf[self.n] = self.issue(self.keys[self.n])
                self.n += 1
            return self.buf.pop(idx)

    rr = {}

    def rot(key, n):
        v = rr.get(key, 0)
        rr[key] = (v + 1) % n
        return v

    def rstd_from(out, ssum, n, tmp):
        act(tmp, ssum, AF.Sqrt, bias=EPS, scale=1.0 / n)
        recip(out, tmp)

    dma(C128, c128_d)
    dma(MASKB, maskb_d)
    dma(KEEP, keep_d)
    for tb in range(NTB):
        dma(X[:, tb, :], x_d[tb * 128:(tb + 1) * 128, :])
    mset('pool', MODF.rearrange("p l c -> p (l c)"), 0.0)
    mset('pool', DT.rearrange("p a b -> p (a b)"), 0.0)
    dma(S8, cvt_d)
    act(S8, S8, AF.Silu)

    done = [False]

    def finish():
        if phase_ctx[0] is not None:
            phase_ctx[0].close()
            phase_ctx[0] = None
        es.enter_context(nc.sbuf_tensor("filler", [128, FREE], F32))
        S.emit()
        done[0] = True

    for l in range(nlayers):
        lam_init = 0.8 - 0.6 * math.exp(-0.3 * l)
        dma(CONVW.rearrange("p j k -> p (j k)"), convw_d[l])
        dma(CONVB, convb_d[l])
        dma(ABC, alog_d[l:l + 1, :].partition_broadcast(128))
        dma(DTB, dtb_d[l:l + 1, :].partition_broadcast(128))
        dma(DBC, dd_d[l:l + 1, :].partition_broadcast(128))
        dma(SSDG, ssdg_d[l:l + 1, :].partition_broadcast(128))
        dma(QKG, qkg_d[l:l + 1, :].partition_broadcast(128))
        dma(SUBG, subg_d[l:l + 1, :].partition_broadcast(128))
        dma(LAMB, lam_d[l:l + 1, :].partition_broadcast(128))
        act(ABC, ABC, AF.Exp)
        ts('dve', ABC, ABC, -1.0, None, ALU.mult)
        ts('dve', SUBG, SUBG, 1.0 - lam_init, None, ALU.mult)
        tt('dve', tmpl[:, 0:32], LAMB[:, 0:32], LAMB[:, 32:64], ALU.mult)
        tt('dve', tmpl[:, 32:64], LAMB[:, 64:96], LAMB[:, 96:128], ALU.mult)
        red(LAMS[:, 0:2], tmpl.rearrange("p (a b) -> p a b", a=2))
        act(LAMS[:, 2:4], LAMS[:, 0:2], AF.Exp)
        tt('dve', LAMS[:, 4:5], LAMS[:, 3:4], LAMS[:, 2:3], ALU.subtract)
        ts('dve', LAMS[:, 4:5], LAMS[:, 4:5], -lam_init, None, ALU.add)
        NEGLAM = LAMS[:, 4:5]

        open_phase('M%d' % l, 10304, 0)
        WA = [ph.get(4096).rearrange("p (a b) -> p a b", a=8) for _ in range(2)]
        ACC = [ph.get(512) for _ in range(2)]
        BST = [ph.get(512) for _ in range(2)]
        for s in range(12):
            b = s % 2
            wsrc = wada_d[l][:, s * 512:(s + 1) * 512].rearrange("(kc p) n -> p kc n", p=128)
            dma(WA[b], wsrc)
            dma(BST[b][0:1, :], bada_d[l:l + 1, s * 512:(s + 1) * 512])
            ts('dve', ACC[b], WA[b][:, 0, :], S8[:, 0:1], None, ALU.mult)
            for kc in range(1, 8):
                stt(ACC[b], WA[b][:, kc, :], S8[:, kc:kc + 1], ACC[b], ALU.mult, ALU.add)
            tt('pool', ACC[b][0:1, :], ACC[b][0:1, :], BST[b][0:1, :], ALU.add)
            bk = ps[rot('m', 2)]
            for j in range(4):
                mm(bk[:, 2 * j:2 * j + 2], ACC[b][:, j * 128:(j + 1) * 128], ONES[:, 0:2])
            m = s // 2
            hf = s % 2
            cp('dve', MODF[:, l, m * 8 + hf * 4:m * 8 + hf * 4 + 4],
               bk[:, 0:8].rearrange("p (j t) -> p j t", t=2)[:, :, 0])
        ts('dve', MODF[:, l, 8:16], MODF[:, l, 8:16], 1.0, None, ALU.add)
        ts('dve', MODF[:, l, 32:40], MODF[:, l, 32:40], 1.0, None, ALU.add)
        if dbg and l == nlayers - 1:
            dma(dbg_modf, MODF.rearrange("p l c -> p (l c)"))
        if stop_after == 'M' and l == nlayers - 1:
            finish()
            break

        open_phase('PA%d' % l, 6944, 26200)
        KTD = pr.get(2 * 2560).rearrange("p (a b) -> p a b", a=2)
        KTG = pr.get(2560)
        VW = 68 if USE_R else 65
        VD = pr.get(NKB * 4 * VW).rearrange("p (k h e) -> p k h e", k=NKB, h=4)
        VG = pr.get(NKB * 2 * VW).rearrange("p (k h e) -> p k h e", k=NKB, h=2)
        r_mark = pr.cur
        if stop_after == 'M2' and l == nlayers - 1:
            mset('dve', KTG, 1.0)
            mset('pool', VG[:, :, :, 64:65], 1.0)
            _tq = ph.get(512)
            cp('act', _tq, KTG[:, 0:512])
            dma(zs_d[0:128, :], _tq)
            import os as _os3
            if _os3.environ.get('KDBG_M3') == '1':
                _rp = ph.get(384).rearrange("p (a b) -> p a b", a=4)
                dma(_rp, rope_d[:, 0:384].rearrange("p (a b) -> p a b", a=4))
                _t2 = ph.get(384)
                cp('dve', _t2, _rp.rearrange("p a b -> p (a b)"))
                dma(zs_d[128:256, 0:384], _t2)
            if _os3.environ.get('KDBG_M4') == '1':
                _h = pr.get(512)
                _w = pr.get(512)
                cp('dve', R(_h), X[:, 0, 0:512])
                cp('dve', R(_w), X[:, 1, 0:512])
                tr(ps[6][:, 0:128], _h[:, 0:128], IDENT)
                mm(ps[5][:, 0:512], _h[:, 0:128], _w, r=True)
                _t3 = ph.get(512)
                cp('dve', _t3, ps[5][:, 0:512])
                dma(zs_d[256:384, :], _t3)
                cp('dve', _t3[:, 0:128], ps[6][:, 0:128])
                dma(zs_d[384:512, 0:128], _t3[:, 0:128])
            finish()
            break
        HTg = pr.get(4096).rearrange("p (a b) -> p a b", a=8)
        WS = [pr.get(2048).rearrange("p (a b) -> p a b", a=8) for _ in range(3)]
        ROPE = ph.get(4 * 96).rearrange("p (a b) -> p a b", a=4)
        TOK = [ph.get(256) for _ in range(2)]
        NRM = [ph.get(256) for _ in range(2)]
        ROT = [ph.get(256) for _ in range(2)]
        RT = [ph.get(128) for _ in range(4)]
        QST = [ph.get(1024).rearrange("p (a b) -> p a b", a=2) for _ in range(2)]
        ZST = [ph.get(256) for _ in range(2)]
        XST = [ph.get(512) for _ in range(2)]
        CK = QST[0].rearrange("p a (b c) -> p (a b) c", b=2)
        CKG = QST[1][:, 0, :].rearrange("p (a b) -> p a b", a=4)
        SM = ph.get(64)
        if not USE_R:
            for vv in (VD, VG):
                mset('pool', vv[:, :, :, 64:65], 1.0)
        else:
            for kb in range(NKB):
                for (vv, nh_) in ((VD, 4), (VG, 2)):
                    ts('dve', R(vv[:, kb, :, 64]), ONES[:, 0:nh_], 1.0, None, ALU.mult)
                    ts('dve', R(vv[:, kb, :, 65:VW]), ONES[:, 0:nh_ * (VW - 65)].rearrange("p (h e) -> p h e", h=nh_), 0.0, None, ALU.mult)
        for kb in range(4):
            dma(VD[:, 16 + kb, :, 0:64], cdv_d[l, kb * 128:(kb + 1) * 128, :].rearrange("p (h e) -> p h e", h=4), cast=True)
            dma(VG[:, 16 + kb, :, 0:64], cgv_d[l, kb * 128:(kb + 1) * 128, :].rearrange("p (h e) -> p h e", h=2), cast=True)
        dma(CK, cdk_d[l].rearrange("(kb p) c -> p kb c", p=128))
        dma(CKG, cgk_d[l].rearrange("(kb p) c -> p kb c", p=128))
        for kb in range(4):
            bk = ps[6 + rot('p1t', 2)]
            for t2 in range(2):
                tr(bk[:, t2 * 128:(t2 + 1) * 128], CK[:, kb, t2 * 128:(t2 + 1) * 128], IDENT)
            tr(bk[:, 256:384], CKG[:, kb, :], IDENT)
            cp('dve', R(KTD[:, :, 2048 + kb * 128:2048 + (kb + 1) * 128]), bk[:, 0:256].rearrange("p (a b) -> p a b", a=2))
            cp('dve', R(KTG[:, 2048 + kb * 128:2048 + (kb + 1) * 128]), bk[:, 256:384])

        if stop_after == 'P0' and l == nlayers - 1:
            finish()
            break
        SQT = ph.get(256)

        def rope_apply(dv, sv, cos, sin, nh, npair):
            cb = bc(cos.unsqueeze(1), [128, nh, npair])
            sbb = bc(sin.unsqueeze(1), [128, nh, npair])
            n = nh * npair
            ta = RT[0][:, 0:n].rearrange("p (h i) -> p h i", h=nh)
            tb_ = RT[1][:, 0:n].rearrange("p (h i) -> p h i", h=nh)
            tc = RT[2][:, 0:n].rearrange("p (h i) -> p h i", h=nh)
            td = RT[3][:, 0:n].rearrange("p (h i) -> p h i", h=nh)
            tt('dve', ta, sv[:, :, :, 0], cb, ALU.mult)
            tt('pool', tb_, sv[:, :, :, 1], sbb, ALU.mult)
            tt('dve', dv[:, :, :, 0], ta, tb_, ALU.subtract)
            tt('pool', tc, sv[:, :, :, 0], sbb, ALU.mult)
            tt('dve', td, sv[:, :, :, 1], cb, ALU.mult)
            tt('pool', dv[:, :, :, 1], tc, td, ALU.add)

        def v4(ap, nh):
            return ap.rearrange("p (h i t) -> p h i t", h=nh, t=2)

        def rmsn(dst, src, nh, gain):
            sv = src.rearrange("p (h e) -> p h e", h=nh)
            dv = dst.rearrange("p (h e) -> p h e", h=nh)
            sqv = SQT[:, 0:nh * 64].rearrange("p (h e) -> p h e", h=nh)
            tt('dve', sqv, sv, sv, ALU.mult)
            red(SM[:, 0:nh], sqv)
            rstd_from(SM[:, 8:8 + nh], SM[:, 0:nh], 64, SM[:, 16:16 + nh])
            tt('dve', dv, sv, bc(SM[:, 8:8 + nh].unsqueeze(2), [128, nh, 64]), ALU.mult)
            tt('dve', dv, dv, bc(gain.unsqueeze(1), [128, nh, 64]), ALU.mult)

        slabs = [(c0, min(256, INW - c0)) for c0 in range(0, INW, 256)]

        def issue_ws(key):
            c0_, ncol_ = slabs[key[1]]
            wb_ = WS[rot('ws', 3)]
            dma(wb_[:, :, 0:ncol_], win_d[l][:, c0_:c0_ + ncol_].rearrange("(kc p) n -> p kc n", p=128), cast=True)
            return wb_
        ws_pref = Pref([(g_, si_) for g_ in range(4) for si_ in range(len(slabs))], 2, issue_ws)
        for g in range(4):
            dma(ROPE, rope_d[:, g * 384:(g + 1) * 384].rearrange("p (a b) -> p a b", a=4))
            for dc in range(8):
                bk = ps[rot('p1h', 2)]
                for tb in range(4):
                    tr(bk[:, tb * 128:(tb + 1) * 128], X[:, 4 * g + tb, dc * 128:(dc + 1) * 128], IDENT)
                act(R(HTg[:, dc, :]), bk[:, :], AF.Identity, bias=MODF[:, l, dc:dc + 1], scale=MODF[:, l, 8 + dc:9 + dc])
            for si, (c0, ncol) in enumerate(slabs):
                wb = ws_pref.get(g * len(slabs) + si)
                if 7 <= si <= 9:
                    for t2 in range(2):
                        bk = ps[2 + rot('p1p', 4)]
                        for dc in range(8):
                            mm(bk[:, :], wb[:, dc, t2 * 128:(t2 + 1) * 128], HTg[:, dc, :], start=(dc == 0), stop=(dc == 7), r=True)
                        xs = XST[rot('xst', 2)]
                        cp('act', xs, bk[:, :])
                        j = (si - 7) * 2 + t2
                        dma(xbcs_d[j * 128:(j + 1) * 128, g * 512:(g + 1) * 512], xs)
                    continue
                qs = None
                for tb in range(4):
                    tbg = 4 * g + tb
                    bk = ps[2 + rot('p1p', 4)]
                    for dc in range(8):
                        mm(bk[:, 0:ncol], HTg[:, dc, tb * 128:(tb + 1) * 128], wb[:, dc, 0:ncol], start=(dc == 0), stop=(dc == 7), r=True)
                    cosd = ROPE[:, tb, 0:16]
                    sind = ROPE[:, tb, 16:32]
                    cosg = ROPE[:, tb, 32:64]
                    sing = ROPE[:, tb, 64:96]
                    if si == 0 or si == 1:
                        tk = TOK[rot('tok', 2)]
                        ro = ROT[rot('rot', 2)]
                        cp('act', tk, bk[:, 0:256])
                        if si == 1:
                            dma(ndk_d[l, tbg * 128:(tbg + 1) * 128, :], tk, eng='pool')
                        rope_apply(v4(ro, 8), v4(tk, 8), cosd, sind, 8, 16)
                        b2 = ps[6 + rot('p1t', 2)]
                        for t2 in range(2):
                            tr(b2[:, t2 * 128:(t2 + 1) * 128], ro[:, t2 * 128:(t2 + 1) * 128], IDENT)
                        if si == 0:
                            if tb == 0:
                                qs = QST[rot('qst', 2)]
                            cp('dve', R(qs[:, :, tb * 128:(tb + 1) * 128]), b2[:, 0:256].rearrange("p (a b) -> p a b", a=2))
                            if tb == 3:
                                dma(qts_d[0:256, g * 512:(g + 1) * 512].rearrange("(a p) t -> p a t", p=128), qs)
                        else:
                            cp('dve', R(KTD[:, :, tbg * 128:(tbg + 1) * 128]), b2[:, 0:256].rearrange("p (a b) -> p a b", a=2))
                    elif si == 2:
                        tk = TOK[rot('tok', 2)]
                        cp('act', tk, bk[:, 0:256])
                        cp('dve', R(VD[:, tbg, :, 0:64]), tk.rearrange("p (h e) -> p h e", h=4))
                        dma(ndv_d[l, tbg * 128:(tbg + 1) * 128, :], tk)
                    elif si == 3:
                        tk = TOK[rot('tok', 2)]
                        nr = NRM[rot('nrm', 2)]
                        ro = ROT[rot('rot', 2)]
                        cp('act', tk, bk[:, 0:256])
                        rmsn(nr, tk, 4, QKG[:, 0:64])
                        for a_ in range(2):
                            dvw = ro.rearrange("p (b a x) -> p a b x", a=2, b=2)[:, a_].rearrange("p b (i t) -> p b i t", t=2)
                            svw = nr[:, a_ * 128:(a_ + 1) * 128].rearrange("p (b i t) -> p b i t", b=2, t=2)
                            rope_apply(dvw, svw, cosg, sing, 2, 32)
                        b2 = ps[6 + rot('p1t', 2)]
                        for t2 in range(2):
                            tr(b2[:, t2 * 128:(t2 + 1) * 128], ro[:, t2 * 128:(t2 + 1) * 128], IDENT)
                        if tb == 0:
                            qs = QST[rot('qst', 2)]
                        cp('dve', R(qs[:, :, tb * 128:(tb + 1) * 128]), b2[:, 0:256].rearrange("p (a b) -> p a b", a=2))
                        if tb == 3:
                            dma(qts_d[256:512, g * 512:(g + 1) * 512].rearrange("(a p) t -> p a t", p=128), qs)
                    elif si == 4:
                        tk = TOK[rot('tok', 2)]
                        nr = NRM[rot('nrm', 2)]
                        ro = ROT[rot('rot', 2)]
                        cp('act', tk, bk[:, 0:256])
                        cp('dve', R(VG[:, tbg, :, 0:64]), tk[:, 128:256].rearrange("p (h e) -> p h e", h=2))
                        dma(ngv_d[l, tbg * 128:(tbg + 1) * 128, :], tk[:, 128:256])
                        rmsn(nr[:, 0:128], tk[:, 0:128], 2, QKG[:, 64:128])
                        dma(ngk_d[l, tbg * 128:(tbg + 1) * 128, :], nr[:, 0:128], eng='pool')
                        rope_apply(v4(ro[:, 0:128], 2), v4(nr[:, 0:128], 2), cosg, sing, 2, 32)
                        b2 = ps[6 + rot('p1t', 2)]
                        tr(b2[:, 0:128], ro[:, 0:128], IDENT)
                        cp('dve', R(KTG[:, tbg * 128:(tbg + 1) * 128]), b2[:, 0:128])
                    elif si == 5 or si == 6:
                        zt = ZST[rot('zst', 2)]
                        act(zt, bk[:, 0:256], AF.Silu)
                        dma(zs_d[tbg * 128:(tbg + 1) * 128, (si - 5) * 256:(si - 4) * 256], zt)
                    elif si == 10:
                        tt('dve', DT[:, tbg, :], bk[:, 0:16], DTB, ALU.add)
                        act(DT[:, tbg, :], DT[:, tbg, :], AF.Exp)
                        act(DT[:, tbg, :], DT[:, tbg, :], AF.Ln, bias=1.0)
        if dbg and l == nlayers - 1:
            dma(dbg_dt, DT.rearrange("p a b -> p (a b)"))
        if stop_after == 'P' and l == nlayers - 1:
            finish()
            break

        pr.cur = r_mark
        ph.reset()
        QG = [pr.get(2048).rearrange("p (a b) -> p a b", a=4) for _ in range(2)]
        PT = [pr.get(512) for _ in range(6)]
        OTOK = ph.get(2048).rearrange("p (a b) -> p a b", a=4)
        T0 = ph.get(256).rearrange("p (a b) -> p a b", a=4)
        T1 = ph.get(256).rearrange("p (a b) -> p a b", a=4)
        T2 = ph.get(256).rearrange("p (a b) -> p a b", a=4)
        RR = ph.get(32)
        MAT = [ph.get(2048).rearrange("p (a b) -> p a b", a=4) for _ in range(2)]
        sc_d = 32 ** -0.5
        sc_g = 64 ** -0.5
        def issue_qg(qg_):
            dma(QG[qg_ % 2], qts_d[:, qg_ * 512:(qg_ + 1) * 512].rearrange("(a p) t -> p a t", p=128), cast=True)
            return QG[qg_ % 2]
        qg_pref = Pref(list(range(4)), 1, issue_qg)
        for qg in range(4):
            qb_ = qg_pref.get(qg)

            def head_pass(kT, kbase, kn, qT, vfn, scale, obank):
                for kb in range(NKB):
                    sbk = ps[rot('as', 3)]
                    tp = (kbase, 0) if kn == 32 else None
                    mm(sbk[:, :], kT(kb), qT, tp=tp, r=True)
                    pt = PT[rot('pt', 6)]
                    for hf in range(2):
                        act(R(pt[:, hf * 256:(hf + 1) * 256]), sbk[:, hf * 256:(hf + 1) * 256], AF.Exp,
                            bias=MASKB[:, kb * 8 + qg * 2 + hf:kb * 8 + qg * 2 + hf + 1], scale=scale)
                    for qb in range(4):
                        mm(obank[:, qb * VW:(qb + 1) * VW], pt[:, qb * 128:(qb + 1) * 128], vfn(kb),
                           start=(kb == 0 and qb == 0), stop=(kb == NKB - 1 and qb == 3), r=True)

            for h in range(4):
                ob = [ps[3 + 2 * (h % 2)], ps[4 + 2 * (h % 2)]]
                for m_ in range(2):
                    hm = 2 * h + m_
                    tile = hm // 4
                    pb = 32 * (hm % 4)
                    head_pass(lambda kb, tile=tile, pb=pb: KTD[pb:pb + 32, tile, kb * 128:(kb + 1) * 128], pb, 32,
                              qb_[pb:pb + 32, tile, :], lambda kb, h=h: VD[:, kb, h, :], sc_d, ob[m_])
                o0 = ob[0][:, 0:4 * VW].rearrange("p (q e) -> p q e", q=4)
                o1 = ob[1][:, 0:4 * VW].rearrange("p (q e) -> p q e", q=4)
                recip(RR[:, 0:4], o0[:, :, 64])
                recip(RR[:, 4:8], o1[:, :, 64])
                ts('dve', RR[:, 4:8], RR[:, 4:8], NEGLAM, None, ALU.mult)
                tt('dve', T0, o0[:, :, 0:64], bc(RR[:, 0:4].unsqueeze(2), [128, 4, 64]), ALU.mult)
                tt('dve', T1, o1[:, :, 0:64], bc(RR[:, 4:8].unsqueeze(2), [128, 4, 64]), ALU.mult)
                tt('pool', T0, T0, T1, ALU.add)
                tt('pool', T2, T0, T0, ALU.mult)
                red(RR[:, 8:12], T2)
                rstd_from(RR[:, 12:16], RR[:, 8:12], 64, RR[:, 16:20])
                tt('dve', T0, T0, bc(RR[:, 12:16].unsqueeze(2), [128, 4, 64]), ALU.mult)
                tt('dve', OTOK[:, :, h * 64:(h + 1) * 64], T0, bc(SUBG.unsqueeze(1), [128, 4, 64]), ALU.mult)
            for qh in range(4):
                a_ = qh // 2
                b_ = qh % 2
                ob = ps[3 + (qh % 2) * 2]
                pb = 64 * a_
                head_pass(lambda kb, pb=pb: KTG[pb:pb + 64, kb * 128:(kb + 1) * 128], pb, 64,
                          qb_[pb:pb + 64, 2 + b_, :], lambda kb, a_=a_: VG[:, kb, a_, :], sc_g, ob)
                o0 = ob[:, 0:4 * VW].rearrange("p (q e) -> p q e", q=4)
                recip(RR[:, 20:24], o0[:, :, 64])
                tt('dve', OTOK[:, :, 256 + qh * 64:256 + (qh + 1) * 64], o0[:, :, 0:64],
                   bc(RR[:, 20:24].unsqueeze(2), [128, 4, 64]), ALU.mult)
            mt = MAT[qg % 2]
            for qb in range(4):
                bk = ps[7]
                for c in range(4):
                    tr(bk[:, c * 128:(c + 1) * 128], OTOK[:, qb, c * 128:(c + 1) * 128], IDENT)
                cp('dve', mt[:, :, qb * 128:(qb + 1) * 128], bk[:, :].rearrange("p (a b) -> p a b", a=4))
            dma(mixt_d[0:512, qg * 512:(qg + 1) * 512].rearrange("(a p) t -> p a t", p=128), mt)
        if stop_after == 'A' and l == nlayers - 1:
            finish()
            break

        open_phase('S%d' % l, 30800, 0)
        XH = ph.get(8192).rearrange("p (c e) -> p c e", c=16)
        BT = ph.get(2048)
        CT = ph.get(2048)
        s_mark = ph.cur
        PB = [ph.get(8 * 260).rearrange("p (s i) -> p s i", s=8) for _ in range(2)]
        CA = [ph.get(2048) for _ in range(2)]
        for b in range(2):
            mset('pool', PB[b][:, 0, 0:2], 0.0)
            mset('pool', PB[b][:, 7, 258:260], 0.0)
        for j in range(6):
            pbuf = PB[j % 2]
            ca = CA[j % 2]
            dma(pbuf[:, :, 2:258], xbcs_d[j * 128:(j + 1) * 128, :].rearrange("p (s i) -> p s i", s=8))
            ts('dve', pbuf[:, 1:8, 0:2], pbuf[:, 0:7, 256:258], KEEP[:, 0:1], None, ALU.mult)
            ts('dve', pbuf[:, 0:7, 258:260], pbuf[:, 1:8, 2:4], KEEP[:, 0:1], None, ALU.mult)
            cav = ca.rearrange("p (s i) -> p s i", s=8)
            ts('dve', cav, pbuf[:, :, 0:256], CONVW[:, j, 0:1], CONVB[:, j:j + 1], ALU.mult, ALU.add)
            for k in range(1, 5):
                stt(cav, pbuf[:, :, k:k + 256], CONVW[:, j, k:k + 1], cav, ALU.mult, ALU.add)
            if j < 4:
                act(ca, ca, AF.Silu)
                for c4 in range(4):
                    bk = ps[6 + rot('s0', 2)]
                    for cc in range(4):
                        c = c4 * 4 + cc
                        tr(bk[:, cc * 128:(cc + 1) * 128], ca[:, c * 128:(c + 1) * 128], IDENT)
                    cp('dve', XH[:, c4 * 4:(c4 + 1) * 4, j * 128:(j + 1) * 128], bk[:, :].rearrange("p (a b) -> p a b", a=4))
            elif j == 4:
                act(BT, ca, AF.Silu)
            else:
                act(CT, ca, AF.Silu)
        if stop_after == 'S0' and l == nlayers - 1:
            finish()
            break
        S.barrier()
        ph.cur = s_mark
        YB = ph.get(8192).rearrange("p (c e) -> p c e", c=16)
        ph.get(64 + 256)
        STATE = [ph.get(256), ph.get(256)]
        L2 = ph.get(512).rearrange("p (a b) -> p a b", a=4)
        DTA = ph.get(8)
        ACS = ph.get(8)
        NACS = ph.get(8)
        EACS = ph.get(8)
        TEA = ph.get(8)
        TOEND = ph.get(8)
        CDB = ph.get(8)
        SS = ph.get(8)
        DTA_A = [ph.get(128).rearrange("p (c h) -> p c h", c=16) for _ in range(2)]
        ACS_A = [ph.get(128).rearrange("p (c h) -> p c h", c=16) for _ in range(2)]
        NACS_A = [ph.get(128).rearrange("p (c h) -> p c h", c=16) for _ in range(2)]
        EACS_A = [ph.get(128).rearrange("p (c h) -> p c h", c=16) for _ in range(2)]
        AEND_A = [ph.get(128).rearrange("p (c h) -> p c h", c=16) for _ in range(2)]
        CDB_A = [ph.get(128).rearrange("p (c h) -> p c h", c=16) for _ in range(2)]
        TOE_A = [ph.get(128).rearrange("p (c h) -> p c h", c=16) for _ in range(2)]
        EMf = ph.get(1024)
        EM = EMf.rearrange("p (h s) -> p h s", h=8)
        MT = ph.get(1024).rearrange("p (h s) -> p h s", h=8)
        GT = ph.get(256).rearrange("p (g s) -> p g s", g=2)
        BTOK = ph.get(128)
        XD = ph.get(512)
        XDTE = ph.get(512)
        ZC = [ph.get(512) for _ in range(2)]
        F1 = ph.get(512)
        F2 = ph.get(512)
        MST = [ph.get(512).rearrange("p (a b) -> p a b", a=4) for _ in range(2)]
        SOUT = [ph.get(256).rearrange("p (a b) -> p a b", a=4) for _ in range(2)]

        def h8(ap):
            return ap.rearrange("p (h e) -> p h e", h=8)

        for d_, st_d in enumerate([stf_d, stb_d]):
            mset('pool', L2, 0.0)
            src = st_d[l].rearrange("(i hh) p n -> (hh p) i n", hh=2)
            dma(L2[:, 0:2, 0:64], src[:, 0:2, :])
            dma(L2[:, 2:4, 64:128], src[:, 2:4, :])
            bk = ps[7]
            for i in range(4):
                tr(bk[:, i * 128:(i + 1) * 128], L2[:, i, :], IDENT)
            cp('dve', STATE[d_][0:64, :], bk[0:64, 0:256])
            cp('dve', STATE[d_][64:128, :], bk[64:128, 256:512])

        if stop_after == 'S1' and l == nlayers - 1:
            finish()
            break

        S.barrier()
        import os as _os
        _cut = int(_os.environ.get('KDBG_CUT', '99'))

        def fl(a):
            return a.rearrange("p c h -> p (c h)")
        for d_ in range(2):
            Tm_ = TL if d_ == 0 else TU
            tt('dve', DTA_A[d_], DT[:, :, d_ * 8:(d_ + 1) * 8], bc(ABC[:, d_ * 8:(d_ + 1) * 8].unsqueeze(1), [128, 16, 8]), ALU.mult)
            mm(ps[0][:, 0:128], Tm_, fl(DTA_A[d_]))
            cp('dve', fl(ACS_A[d_]), ps[0][:, 0:128])
            ts('dve', fl(NACS_A[d_]), fl(ACS_A[d_]), -1.0, None, ALU.mult)
            act(fl(EACS_A[d_]), fl(ACS_A[d_]), AF.Exp)
            mm(ps[1][:, 0:128], ONES, fl(DTA_A[d_]))
            cp('dve', fl(AEND_A[d_]), ps[1][:, 0:128])
            act(fl(CDB_A[d_]), fl(AEND_A[d_]), AF.Exp)
            tt('dve', fl(TOE_A[d_]), fl(AEND_A[d_]), fl(ACS_A[d_]), ALU.subtract)
            act(fl(TOE_A[d_]), fl(TOE_A[d_]), AF.Exp)

        def ssd_chunk(d_, c, final):
            cs = slice(c * 128, (c + 1) * 128)
            Tm = TL if d_ == 0 else TU
            NM = NMF if d_ == 0 else NMB
            eidx = 127 if d_ == 0 else 0
            dtc = DT[:, c, d_ * 8:(d_ + 1) * 8]
            DTA = DTA_A[d_][:, c, :]
            ACS = ACS_A[d_][:, c, :]
            NACS = NACS_A[d_][:, c, :]
            EACS = EACS_A[d_][:, c, :]
            CDB = CDB_A[d_][:, c, :]
            TOEND = TOE_A[d_][:, c, :]
            xh = h8(XH[:, c, :])
            tt('dve', h8(XD), xh, bc(dtc.unsqueeze(2), [128, 8, 64]), ALU.mult)
            tt('dve', h8(XDTE), h8(XD), bc(TOEND.unsqueeze(2), [128, 8, 64]), ALU.mult)
            for h in range(8):
                bk = ps[1 + h // 4]
                mm(bk[:, (h % 4) * 128:(h % 4 + 1) * 128], bc(DTA[:, h:h + 1], [128, 128]), Tm)
            for hh in range(2):
                v = ps[1 + hh][:, :].rearrange("p (h s) -> p h s", h=4)
                for h4 in range(4):
                    h = hh * 4 + h4
                    stt(EM[:, h, :], v[:, h4, :], NACS[:, h:h + 1], NM, ALU.add, ALU.add)
            act(EMf, EMf, AF.Exp)
            if _cut <= 2:
                return
            gbank = [ps[3][:, 0:128], ps[0][:, 128:256]]
            prev = None
            for g in range(2):
                prev = mm(gbank[g], BT[64 * g:64 * g + 64, cs], CT[64 * g:64 * g + 64, cs], after=prev)
            for g in range(2):
                cp('act', GT[:, g, :], gbank[g])
            for g in range(2):
                tt('dve', MT[:, 4 * g:4 * g + 4, :], EM[:, 4 * g:4 * g + 4, :],
                   bc(GT[:, g, :].unsqueeze(1), [128, 4, 128]), ALU.mult)
            if _cut <= 3:
                return
            if _cut <= 4:
                return
            for h in range(8):
                mm(ps[4][:, h * 64:(h + 1) * 64], MT[:, h, :], XD[:, h * 64:(h + 1) * 64])
            ybank = [ps[5][:, 0:256], ps[3][:, 256:512]]
            prev = None
            for g in range(2):
                prev = mm(ybank[g], CT[64 * g:64 * g + 64, cs], STATE[d_][64 * g:64 * g + 64, :], after=prev)
            tr(ps[7][:, 0:128], BT[:, cs], IDENT)
            cp('act', BTOK, ps[7][:, 0:128])
            mm(ps[6][:, :], BTOK, XDTE)
            if _cut <= 5:
                return
            for g in range(2):
                tt('dve', F2[:, 256 * g:256 * g + 256].rearrange("p (h e) -> p h e", h=4),
                   ybank[g].rearrange("p (h e) -> p h e", h=4),
                   bc(EACS[:, 4 * g:4 * g + 4].unsqueeze(2), [128, 4, 64]), ALU.mult)
            if not final:
                tt('dve', YB[:, c, :], ps[4][:, :], F2, ALU.add)
            else:
                zc = ZC[rot('zc', 2)]
                dma(zc, zs_d[c * 128:(c + 1) * 128, :])
                tt('dve', F1, ps[4][:, :], F2, ALU.add)
                tt('dve', F1, F1, YB[:, c, :], ALU.add)
                tt('dve', h8(F2), xh, bc(DBC.unsqueeze(2), [128, 8, 64]), ALU.mult)
                tt('dve', F1, F1, F2, ALU.add)
                tt('dve', F1, F1, zc, ALU.mult)
                tt('dve', F2, F1, F1, ALU.mult)
                red(SS[:, 0:1], F2)
                rstd_from(SS[:, 1:2], SS[:, 0:1], 512, SS[:, 2:3])
                ts('dve', F1, F1, SS[:, 1:2], None, ALU.mult)
                tt('dve', F1, F1, SSDG, ALU.mult)
                for j in range(4):
                    tr(ps[7][:, j * 128:(j + 1) * 128], F1[:, j * 128:(j + 1) * 128], IDENT)
                mst = MST[rot('mst', 2)]
                cp('act', mst, ps[7][:, :].rearrange("p (a b) -> p a b", a=4))
                dma(mixt_d[512:1024, cs].rearrange("(j p) t -> p j t", p=128), mst)
            if _cut <= 6:
                return
            for g in range(2):
                sl = STATE[d_][64 * g:64 * g + 64, :]
                sv = sl.rearrange("p (h e) -> p h e", h=4)
                tt('dve', sv, sv, bc(CDB[64 * g:64 * g + 64, 4 * g:4 * g + 4].unsqueeze(2), [64, 4, 64]), ALU.mult)
                tt('dve', sl, sl, ps[6][64 * g:64 * g + 64, 256 * g:256 * g + 256], ALU.add)
            boundary = (c % 2 == 1) if d_ == 0 else (c % 2 == 0)
            if boundary:
                seq = c // 2
                sbank = [ps[7], ps[6]]
                prev = None
                for i in range(4):
                    g = i // 2
                    prev = tr(sbank[g][:, i * 64:(i + 1) * 64], STATE[d_][64 * g:64 * g + 64, (i % 2) * 128:(i % 2 + 1) * 128],
                              IDENT[64 * g:64 * g + 64, 64 * g:64 * g + 64], after=(prev if i == 2 else None))
                so = SOUT[rot('so', 2)]
                for g in range(2):
                    cp('act', so[:, 2 * g:2 * g + 2, :], sbank[g][:, 128 * g:128 * g + 128].rearrange("p (a b) -> p a b", a=2))
                od = nsf_d if d_ == 0 else nsb_d
                dma(od[l, seq].rearrange("(i hh) p n -> (hh p) i n", hh=2), so)
                last = (c == 15) if d_ == 0 else (c == 0)
                if not last:
                    ts('dve', STATE[d_], STATE[d_], KEEP[:, 0:1], None, ALU.mult)

        for c in range(int(_os.environ.get('KDBG_NCH', '16'))):
            ssd_chunk(0, c, False)
        if stop_after == 'S2' and l == nlayers - 1:
            finish()
            break
        for c in reversed(range(16)):
            ssd_chunk(1, c, True)
        if stop_after == 'S' and l == nlayers - 1:
            finish()
            break

        def build_gate(GBC, DG, col0):
            for c in range(8):
                ts('dve', DG[:, c, :], IDENT, MODF[:, l, col0 + c:col0 + c + 1], None, ALU.mult)
            for hf in range(2):
                bk = ps[rot('o', 8)]
                mm(bk[:, :], ONES, DG[:, 4 * hf:4 * hf + 4, :].rearrange("p a b -> p (a b)"))
                cp('act', GBC[:, hf * 512:(hf + 1) * 512], bk[:, :])

        def resid_evac(bk, tbg, hf, GBC, TMP):
            tmp = TMP[rot('tmp', 2)]
            tt('dve', tmp, bk[:, :], GBC[:, hf * 512:(hf + 1) * 512], ALU.mult)
            xs = X[:, tbg, hf * 512:(hf + 1) * 512]
            stt(xs, xs, ALPHA, tmp, ALU.mult, ALU.add)

        def ln_block(tbg, LNG, LNB, ST6, MV):
            xs = X[:, tbg, :]
            for hf in range(2):
                src_ = X[:, tbg, hf * 512:(hf + 1) * 512]
                dst_ = ST6[:, hf * 6:(hf + 1) * 6]
                S.add('dve', lambda e, dst_=dst_, src_=src_: e.bn_stats(out=dst_, in_=src_), reads=[src_], writes=[dst_])
            S.add('dve', lambda e: e.bn_aggr(out=MV[:, 0:2], in_=ST6[:, 0:12]), reads=[ST6[:, 0:12]], writes=[MV[:, 0:2]])
            act(MV[:, 2:3], MV[:, 1:2], AF.Sqrt, bias=EPS, scale=1.0)
            recip(MV[:, 3:4], MV[:, 2:3])
            ts('dve', xs, xs, MV[:, 0:1], MV[:, 3:4], ALU.subtract, ALU.mult)
            tt('pool', xs, xs, LNG, ALU.mult)
            tt('pool', xs, xs, LNB, ALU.add)

        open_phase('O%d' % l, 5400, 16384)
        WO = pr.get(8192).rearrange("p (c n) -> p c n", c=8)
        MX = [pr.get(4096).rearrange("p (c t) -> p c t", c=8) for _ in range(2)]
        GBC = ph.get(1024)
        LNG = ph.get(1024)
        LNB = ph.get(1024)
        DG = ph.get(1024).rearrange("p (c q) -> p c q", c=8)
        TMP = [ph.get(512) for _ in range(2)]
        ST6 = ph.get(16)
        MV = ph.get(8)
        dma(WO, wout_d[l].rearrange("(c p) n -> p c n", p=128), cast=True)
        dma(LNG, lng_d[l, 0:1, :].partition_broadcast(128))
        dma(LNB, lnb_d[l, 0:1, :].partition_broadcast(128))
        build_gate(GBC, DG, 16)
        def issue_mx(g_):
            dma(MX[g_ % 2], mixt_d[:, g_ * 512:(g_ + 1) * 512].rearrange("(c p) t -> p c t", p=128), cast=True)
            return MX[g_ % 2]
        mx_pref = Pref(list(range(4)), 1, issue_mx)
        for g in range(4):
            mx = mx_pref.get(g)
            for tb in range(4):
                tbg = 4 * g + tb
                for hf in range(2):
                    bk = ps[rot('o', 8)]
                    for ec in range(8):
                        mm(bk[:, :], mx[:, ec, tb * 128:(tb + 1) * 128], WO[:, ec, hf * 512:(hf + 1) * 512],
                           start=(ec == 0), stop=(ec == 7), r=True)
                    resid_evac(bk, tbg, hf, GBC, TMP)
                ln_block(tbg, LNG, LNB, ST6, MV)
        if dbg and l == nlayers - 1:
            for tb in range(NTB):
                dma(dbg_x1[tb * 128:(tb + 1) * 128, :], X[:, tb, :])
        if stop_after == 'O' and l == nlayers - 1:
            finish()
            break

        open_phase('F%d' % l, 6400, 23552)
        HTf = pr.get(4096).rearrange("p (a b) -> p a b", a=8)
        ACTT = pr.get(NFF * 512).rearrange("p (i t) -> p i t", i=NFF)
        WGU = [pr.get(2048).rearrange("p (u c n) -> p u c n", u=2, c=8) for _ in range(3)]
        WD = [pr.get(512) for _ in range(4)]
        GBC = ph.get(1024)
        LNG = ph.get(1024)
        LNB = ph.get(1024)
        DG = ph.get(1024).rearrange("p (c q) -> p c q", c=8)
        TMP = [ph.get(512) for _ in range(2)]
        SIL = [ph.get(512) for _ in range(2)]
        ST6 = ph.get(16)
        MV = ph.get(8)
        dma(LNG, lng_d[l, 1:2, :].partition_broadcast(128))
        dma(LNB, lnb_d[l, 1:2, :].partition_broadcast(128))
        build_gate(GBC, DG, 40)

        def issue_wgu(key):
            i_ = key[1]
            w_ = WGU[rot('wgu', 3)]
            dma(w_[:, 0], wfi_d[l][:, i_ * 128:(i_ + 1) * 128].rearrange("(c p) n -> p c n", p=128), cast=True)
            dma(w_[:, 1], wfi_d[l][:, DFF + i_ * 128:DFF + (i_ + 1) * 128].rearrange("(c p) n -> p c n", p=128), cast=True)
            return w_

        def issue_wd(key):
            _, hf_, i_ = key
            wd_ = WD[rot('wd', 4)]
            dma(wd_, wfo_d[l][i_ * 128:(i_ + 1) * 128, hf_ * 512:(hf_ + 1) * 512], cast=True)
            return wd_
        wgu_pref = Pref([(g_, i_) for g_ in range(4) for i_ in range(NFF)], 2, issue_wgu)
        wd_pref = Pref([(g_, hf_, i_) for g_ in range(4) for hf_ in range(2) for i_ in range(NFF)], 3, issue_wd)
        for g in range(4):
            for dc in range(8):
                bk = ps[rot('fh', 2)]
                for tb in range(4):
                    tr(bk[:, tb * 128:(tb + 1) * 128], X[:, 4 * g + tb, dc * 128:(dc + 1) * 128], IDENT)
                act(R(HTf[:, dc, :]), bk[:, :], AF.Identity, bias=MODF[:, l, 24 + dc:25 + dc], scale=MODF[:, l, 32 + dc:33 + dc])
            for i in range(NFF):
                w = wgu_pref.get(g * NFF + i)
                bg = ps[rot('fg', 2)]
                bu = ps[2 + rot('fu', 2)]
                for dc in range(8):
                    mm(bg[:, :], w[:, 0, dc, :], HTf[:, dc, :], start=(dc == 0), stop=(dc == 7), r=True)
                for dc in range(8):
                    mm(bu[:, :], w[:, 1, dc, :], HTf[:, dc, :], start=(dc == 0), stop=(dc == 7), r=True)
                sil = SIL[rot('sil', 2)]
                act(sil, bg[:, :], AF.Silu)
                tt('dve', R(ACTT[:, i, :]), sil, bu[:, :], ALU.mult)
            for hf in range(2):
                for i in range(NFF):
                    wd = wd_pref.get((g * 2 + hf) * NFF + i)
                    for tb in range(4):
                        mm(ps[4 + tb][:, :], ACTT[:, i, tb * 128:(tb + 1) * 128], wd, start=(i == 0), stop=(i == NFF - 1), r=True)
                for tb in range(4):
                    resid_evac(ps[4 + tb], 4 * g + tb, hf, GBC, TMP)
            for tb in range(4):
                ln_block(4 * g + tb, LNG, LNB, ST6, MV)
    if not done[0]:
        for tb in range(NTB):
            dma(y_d[tb * 128:(tb + 1) * 128, :], X[:, tb, :])
        finish()
    if phase_ctx[0] is not None:
        phase_ctx[0].close()
    return es


def _consts():
    i = np.arange(128)
    ident = np.eye(128, dtype=np.float32)
    tl = (i[:, None] <= i[None, :]).astype(np.float32)
    tu = (i[:, None] >= i[None, :]).astype(np.float32)
    nmf = np.where(i[None, :] >= i[:, None], 0.0, NEG).astype(np.float32)
    nmb = np.where(i[None, :] <= i[:, None], 0.0, NEG).astype(np.float32)
    ones = np.ones((128, 128), np.float32)
    return np.concatenate([ident, tl, tu, nmf, nmb, ones], axis=1)


def _rope_tables(is_sample):
    t = np.arange(NT)
    out = np.zeros((NT, 96), np.float32)
    if is_sample:
        row = (t // 64).astype(np.float32)
        col = (t % 64).astype(np.float32)

        def tab(dim):
            nf = dim // 4
            inv = (np.float32(10000.0) ** (-np.arange(nf, dtype=np.float32) / np.float32(nf))).astype(np.float32)
            ang = np.concatenate([row[:, None] * inv, col[:, None] * inv], -1).astype(np.float32)
            return np.cos(ang).astype(np.float32), np.sin(ang).astype(np.float32)
        cd, sd = tab(32)
        cg, sg = tab(64)
        out[:, 0:16] = cd
        out[:, 16:32] = sd
        out[:, 32:64] = cg
        out[:, 64:96] = sg
    else:
        out[:, 0:16] = 1.0
        out[:, 32:64] = 1.0
    return np.ascontiguousarray(out.reshape(16, 128, 96).transpose(1, 0, 2).reshape(128, 16 * 96))


def _maskb(is_sample):
    m = np.zeros((128, NKB, 8), np.float32)
    if not is_sample:
        for kb in range(NKB):
            for qh in range(8):
                if kb >= 16 or (kb // 2) != qh:
                    m[:, kb, qh] = NEG
    return m.reshape(128, 160)


def make_in_maps(inputs):
    f = lambda a: np.ascontiguousarray(np.asarray(a, dtype=np.float32))
    w_shared = {
        "w_ada": f(inputs["w_ada"]), "b_ada": f(inputs["b_ada"]), "w_in": f(inputs["w_in"]), "w_out": f(inputs["w_out"]),
        "diff_lambda": f(inputs["diff_lambda"]).reshape(DEPTH, 128), "diff_subln_g": f(inputs["diff_subln_g"]),
        "qk_norm_g": f(inputs["qk_norm_g"]).reshape(DEPTH, 128),
        "convw": np.ascontiguousarray(f(inputs["ssd_conv_w"]).reshape(DEPTH, 5, 6, 128).transpose(0, 3, 2, 1).reshape(DEPTH, 128, 30)),
        "convb": np.ascontiguousarray(f(inputs["ssd_conv_b"]).reshape(DEPTH, 6, 128).transpose(0, 2, 1)),
        "ssd_A_log": f(inputs["ssd_A_log"]).reshape(DEPTH, 16), "ssd_dt_bias": f(inputs["ssd_dt_bias"]).reshape(DEPTH, 16),
        "ssd_D": f(inputs["ssd_D"]), "ssd_norm_g": f(inputs["ssd_norm_g"]),
        "ln_g": f(inputs["ln_g"]), "ln_b": f(inputs["ln_b"]),
        "w_ffn_in": f(inputs["w_ffn_in"]), "w_ffn_out": f(inputs["w_ffn_out"]),
        "c128": _consts(),
    }
    xp = f(inputs["x_prompt"])
    xs = f(inputs["x_sample"])
    cc = f(inputs["c"])
    cctx = f(inputs["c_ctx"])
    cdk = f(inputs["cache_diff_k"]).reshape(4, DEPTH, PAST, 256)
    cdv = f(inputs["cache_diff_v"]).reshape(4, DEPTH, PAST, 256)
    cgk = f(inputs["cache_gqa_k"]).reshape(4, DEPTH, PAST, 128)
    cgv = f(inputs["cache_gqa_v"]).reshape(4, DEPTH, PAST, 128)
    sf = f(inputs["state_ssd_fwd"])
    sbw = f(inputs["state_ssd_bwd"])
    maps = []
    rope_p, rope_s = _rope_tables(False), _rope_tables(True)
    mb_p, mb_s = _maskb(False), _maskb(True)
    for core in range(8):
        m = dict(w_shared)
        if core < 4:
            m["x"] = np.ascontiguousarray(xp[8 * core:8 * core + 8].reshape(NT, D))
            cv = cctx
            m["cache_dk"] = np.zeros((DEPTH, PAST, 256), np.float32)
            m["cache_dv"] = np.zeros((DEPTH, PAST, 256), np.float32)
            m["cache_gk"] = np.zeros((DEPTH, PAST, 128), np.float32)
            m["cache_gv"] = np.zeros((DEPTH, PAST, 128), np.float32)
            m["st_f"] = np.zeros((DEPTH, 8, 64, 64), np.float32)
            m["st_b"] = np.zeros((DEPTH, 8, 64, 64), np.float32)
            m["rope"] = rope_p
            m["maskb"] = mb_p
            m["keep"] = np.zeros((128, 8), np.float32)
        else:
            j = core - 4
            m["x"] = np.ascontiguousarray(xs[j])
            cv = cc[j]
            m["cache_dk"] = np.ascontiguousarray(cdk[j])
            m["cache_dv"] = np.ascontiguousarray(cdv[j])
            m["cache_gk"] = np.ascontiguousarray(cgk[j])
            m["cache_gv"] = np.ascontiguousarray(cgv[j])
            m["st_f"] = np.ascontiguousarray(sf[j])
            m["st_b"] = np.ascontiguousarray(sbw[j])
            m["rope"] = rope_s
            m["maskb"] = mb_s
            m["keep"] = np.ones((128, 8), np.float32)
        m["cvecT"] = np.ascontiguousarray(cv.reshape(8, 128).T)
        maps.append(m)
    return maps


def kernel(**inputs):
    nc = bass.Bass("TRN2", target_bir_lowering=False)
    es = build_program(nc)
    maps = make_in_maps(inputs)
    res = run_bass_kernel_spmd(nc, maps, core_ids=list(range(8)))
    es.close()
    r = res.results
    B, SEQ = 32, 256
    y_prompt = np.zeros((B, SEQ, D), np.float32)
    y_sample = np.zeros((4, NT, D), np.float32)
    ndk = np.zeros((B, DEPTH, SEQ, 4, 64), np.float32)
    ndv = np.zeros((B, DEPTH, SEQ, 4, 64), np.float32)
    ngk = np.zeros((B, DEPTH, SEQ, 2, 64), np.float32)
    ngv = np.zeros((B, DEPTH, SEQ, 2, 64), np.float32)
    nsf = np.zeros((B, DEPTH, 8, 64, 64), np.float32)
    nsb = np.zeros((B, DEPTH, 8, 64, 64), np.float32)
    for core in range(4):
        o = r[core]
        sl = slice(8 * core, 8 * core + 8)
        y_prompt[sl] = o["y"].reshape(8, SEQ, D)
        ndk[sl] = o["ndk"].reshape(DEPTH, 8, SEQ, 4, 64).transpose(1, 0, 2, 3, 4)
        ndv[sl] = o["ndv"].reshape(DEPTH, 8, SEQ, 4, 64).transpose(1, 0, 2, 3, 4)
        ngk[sl] = o["ngk"].reshape(DEPTH, 8, SEQ, 2, 64).transpose(1, 0, 2, 3, 4)
        ngv[sl] = o["ngv"].reshape(DEPTH, 8, SEQ, 2, 64).transpose(1, 0, 2, 3, 4)
        nsf[sl] = o["nsf"].reshape(DEPTH, 8, 8, 64, 64).transpose(1, 0, 2, 3, 4)
        nsb[sl] = o["nsb"].reshape(DEPTH, 8, 8, 64, 64).transpose(1, 0, 2, 3, 4)
    for j in range(4):
        y_sample[j] = r[4 + j]["y"]
    return (y_prompt, y_sample, ndk, ndv, ngk, ngv, nsf, nsb)
```
